# Optimizing a Trainium2 kernel written in Bass

```python
import math
import jax, jax.numpy as jnp
from jax import lax
import numpy as np

D_MODEL = 2048
BATCH = 4
SEQ = 8192
DEPTH = 1

ATT_HEADS = 8
ATT_QK_DIM = 64
ATT_V_DIM = 2 * ATT_QK_DIM
D_ATT = ATT_HEADS * ATT_V_DIM
D_ATT_QK = ATT_HEADS * 2 * ATT_QK_DIM
Q_BLOCK = 128

D_SSM = D_MODEL // 2
SSM_GROUP = 16
SSM_GROUPS = D_SSM // SSM_GROUP
SSM_STATE = 64
DT_MIN = 1e-3
DT_MAX = 1e-1

IN_SIZES = (D_ATT_QK, D_ATT_QK, D_ATT, D_ATT, D_SSM, D_SSM, D_MODEL, D_MODEL)
N_IN = sum(IN_SIZES)
RMS_EPS = 1e-6

kernel_name = "hybrid_diffattn_s5_gated_block"


def rms_norm(x, gain):
    xf = x.astype(jnp.float32)
    y = xf * lax.rsqrt(jnp.mean(xf * xf, axis=-1, keepdims=True) + RMS_EPS)
    return (y * gain.astype(jnp.float32)).astype(x.dtype)


def lambda_init_fn(layer_idx):
    return 0.8 - 0.6 * math.exp(-0.3 * layer_idx)


def diff_attention(q, k, v, lam):
    bsz, seq = q.shape[0], q.shape[1]
    n_blocks = seq // Q_BLOCK
    scale = ATT_QK_DIM ** -0.5
    q_blocks = q.reshape(bsz, n_blocks, Q_BLOCK, ATT_HEADS, 2, ATT_QK_DIM).swapaxes(0, 1)
    k_pos = jnp.arange(seq)

    def one_block(args):
        q_blk, blk = args
        s = jnp.einsum('bqhcd,bkhcd->bhcqk', q_blk, k).astype(jnp.float32) * scale
        q_pos = blk * Q_BLOCK + jnp.arange(Q_BLOCK)
        causal = k_pos[None, :] <= q_pos[:, None]
        s = jnp.where(causal, s, -jnp.inf)
        p = jax.nn.softmax(s, axis=-1)
        w = p[:, :, 0] - lam * p[:, :, 1]
        return jnp.einsum('bhqk,bkhe->bqhe', w.astype(v.dtype), v)

    out = lax.map(one_block, (q_blocks, jnp.arange(n_blocks)))
    return out.swapaxes(0, 1).reshape(bsz, seq, ATT_HEADS, ATT_V_DIM)


def s5_branch(u, lam_re, lam_im, log_dt, b_re, b_im, c_re, c_im, d_skip, w_glu, b_glu):
    f32 = jnp.float32
    bsz, seq, _ = u.shape
    uf = u.astype(f32).reshape(bsz, seq, SSM_GROUPS, SSM_GROUP)
    dt = jnp.exp(log_dt.astype(f32))[:, None]
    lr = lam_re.astype(f32)
    li = lam_im.astype(f32)
    mag = jnp.exp(lr * dt)
    ab_re = mag * jnp.cos(li * dt)
    ab_im = mag * jnp.sin(li * dt)
    nr = ab_re - 1.0
    ni = ab_im
    den = lr * lr + li * li
    coef_re = (nr * lr + ni * li) / den
    coef_im = (ni * lr - nr * li) / den
    br = b_re.astype(f32)
    bi = b_im.astype(f32)
    bb_re = coef_re[..., None] * br - coef_im[..., None] * bi
    bb_im = coef_re[..., None] * bi + coef_im[..., None] * br
    bu_re = jnp.einsum('bsgc,gpc->bsgp', uf, bb_re)
    bu_im = jnp.einsum('bsgc,gpc->bsgp', uf, bb_im)
    a_re = jnp.broadcast_to(ab_re[None, None], (1, seq, SSM_GROUPS, SSM_STATE))
    a_im = jnp.broadcast_to(ab_im[None, None], (1, seq, SSM_GROUPS, SSM_STATE))

    def combine(e1, e2):
        a1r, a1i, b1r, b1i = e1
        a2r, a2i, b2r, b2i = e2
        return (a2r * a1r - a2i * a1i,
                a2r * a1i + a2i * a1r,
                a2r * b1r - a2i * b1i + b2r,
                a2r * b1i + a2i * b1r + b2i)

    _, _, xs_re, xs_im = lax.associative_scan(combine, (a_re, a_im, bu_re, bu_im), axis=1)
    y = (jnp.einsum('bsgp,gcp->bsgc', xs_re, c_re.astype(f32))
         - jnp.einsum('bsgp,gcp->bsgc', xs_im, c_im.astype(f32)))
    y = y.reshape(bsz, seq, D_SSM) + d_skip.astype(f32) * u.astype(f32)
    y = jax.nn.gelu(y.astype(u.dtype))
    return y * jax.nn.sigmoid(y @ w_glu + b_glu)


def setup_inputs(seed: int = 0) -> dict:
    key = jax.random.key(seed)
    ks = jax.random.split(key, 24)
    f32 = jnp.float32
    L = DEPTH

    def nrm(k, shape, std):
        return jax.random.normal(k, shape, f32) * std

    n_idx = jnp.arange(SSM_STATE, dtype=f32)
    lam_re = -0.5 + nrm(ks[10], (L, SSM_GROUPS, SSM_STATE), 0.01)
    lam_im = math.pi * n_idx[None, None, :] + nrm(ks[11], (L, SSM_GROUPS, SSM_STATE), 0.01)
    log_dt = jax.random.uniform(ks[12], (L, SSM_GROUPS), f32,
                                minval=math.log(DT_MIN), maxval=math.log(DT_MAX))
    return {
        "x": jax.random.normal(ks[0], (BATCH, SEQ, D_MODEL), f32),
        "ln_gain": 1.0 + nrm(ks[1], (L, D_MODEL), 0.02),
        "w_in": nrm(ks[2], (L, D_MODEL, N_IN), D_MODEL ** -0.5),
        "q_norm_gain": 1.0 + nrm(ks[3], (L, ATT_QK_DIM), 0.02),
        "k_norm_gain": 1.0 + nrm(ks[4], (L, ATT_QK_DIM), 0.02),
        "lambda_q1": nrm(ks[5], (L, ATT_QK_DIM), 0.1),
        "lambda_k1": nrm(ks[6], (L, ATT_QK_DIM), 0.1),
        "lambda_q2": nrm(ks[7], (L, ATT_QK_DIM), 0.1),
        "lambda_k2": nrm(ks[8], (L, ATT_QK_DIM), 0.1),
        "subln_gain": 1.0 + nrm(ks[9], (L, ATT_V_DIM), 0.02),
        "ssm_lambda_re": lam_re,
        "ssm_lambda_im": lam_im,
        "ssm_log_dt": log_dt,
        "ssm_b_re": nrm(ks[13], (L, SSM_GROUPS, SSM_STATE, SSM_GROUP), (2.0 * SSM_GROUP) ** -0.5),
        "ssm_b_im": nrm(ks[14], (L, SSM_GROUPS, SSM_STATE, SSM_GROUP), (2.0 * SSM_GROUP) ** -0.5),
        "ssm_c_re": nrm(ks[15], (L, SSM_GROUPS, SSM_GROUP, SSM_STATE), (2.0 * SSM_STATE) ** -0.5),
        "ssm_c_im": nrm(ks[16], (L, SSM_GROUPS, SSM_GROUP, SSM_STATE), (2.0 * SSM_STATE) ** -0.5),
        "ssm_d": 1.0 + nrm(ks[17], (L, D_SSM), 0.1),
        "w_glu": nrm(ks[18], (L, D_SSM, D_SSM), D_SSM ** -0.5),
        "b_glu": nrm(ks[19], (L, D_SSM), 0.01),
        "w_proj_att": nrm(ks[20], (L, D_ATT, D_MODEL), D_ATT ** -0.5),
        "w_proj_ssm": nrm(ks[21], (L, D_SSM, D_MODEL), D_SSM ** -0.5),
        "w_out": nrm(ks[22], (L, D_MODEL, D_MODEL), D_MODEL ** -0.5),
    }


def reference(x, ln_gain, w_in, q_norm_gain, k_norm_gain, lambda_q1, lambda_k1, lambda_q2,
              lambda_k2, subln_gain, ssm_lambda_re, ssm_lambda_im, ssm_log_dt, ssm_b_re,
              ssm_b_im, ssm_c_re, ssm_c_im, ssm_d, w_glu, b_glu, w_proj_att, w_proj_ssm, w_out):
    bsz, seq, _ = x.shape
    splits = [int(s) for s in np.cumsum(IN_SIZES)[:-1]]
    for l in range(DEPTH):
        lam_init = lambda_init_fn(l)
        h = rms_norm(x, ln_gain[l])
        proj = h @ w_in[l]
        q, k, v, z_att, u, z_ssm, g_att, g_ssm = jnp.split(proj, splits, axis=-1)

        q = rms_norm(q.reshape(bsz, seq, ATT_HEADS, 2, ATT_QK_DIM), q_norm_gain[l])
        k = rms_norm(k.reshape(bsz, seq, ATT_HEADS, 2, ATT_QK_DIM), k_norm_gain[l])
        v = v.reshape(bsz, seq, ATT_HEADS, ATT_V_DIM)
        lam = (jnp.exp(jnp.sum(lambda_q1[l].astype(jnp.float32) * lambda_k1[l].astype(jnp.float32)))
               - jnp.exp(jnp.sum(lambda_q2[l].astype(jnp.float32) * lambda_k2[l].astype(jnp.float32)))
               + lam_init)
        attn = diff_attention(q, k, v, lam)
        attn = rms_norm(attn, subln_gain[l]) * (1.0 - lam_init)
        y_att = attn.reshape(bsz, seq, D_ATT) * jax.nn.silu(z_att)

        y_ssm = s5_branch(u, ssm_lambda_re[l], ssm_lambda_im[l], ssm_log_dt[l], ssm_b_re[l],
                          ssm_b_im[l], ssm_c_re[l], ssm_c_im[l], ssm_d[l], w_glu[l], b_glu[l])
        y_ssm = y_ssm.astype(x.dtype) * jax.nn.silu(z_ssm)

        merged = (jax.nn.sigmoid(g_att) * (y_att @ w_proj_att[l])
                  + jax.nn.sigmoid(g_ssm) * (y_ssm @ w_proj_ssm[l]))
        x = x + merged @ w_out[l]
    return x
```

```python
import numpy as np
import ml_dtypes
import contextlib
import concourse.bass as bass
import concourse.mybir as mybir
from concourse.bass_utils import run_bass_kernel_spmd

F32 = mybir.dt.float32
BF16 = mybir.dt.bfloat16
AF = mybir.ActivationFunctionType
ALU = mybir.AluOpType
AX = mybir.AxisListType

D = 2048
NT = 512
NJ_FULL = 8
TC = 8
NCH = NT // TC
NPAIR = 32
MAGIC = 12582912.0
TWO_PI = 6.283185307179586
NEG = -30000.0
EPS = 1e-6
SAME_ENG_SYNC = True
SPLIT_ROLES = 1
ARENA_WORDS = 53000


class Buf:
    __slots__ = ("name", "ws", "rs", "multi", "excl")

    def __init__(self, name, multi=False):
        self.name = name
        self.ws = []
        self.rs = []
        self.multi = multi
        self.excl = False


class DSem:
    def __init__(self, name, bulk=False):
        self.name = name
        self.count = 0
        self.last = None
        self.bulk = bulk
        self.handle = None


class Op:
    __slots__ = ("eng", "fn", "deps", "dsem", "ev", "signal", "idx")


ENGS = ("pe", "act", "dve", "pool", "sp")


class Sched:
    def __init__(self):
        self.by_eng = {e: [] for e in ENGS}
        self.dsems = []
        self.pending = {e: None for e in ENGS}

    def dsem(self, name, bulk=False):
        s = DSem(name, bulk)
        self.dsems.append(s)
        return s

    def barrier(self):
        lasts = [self.by_eng[e][-1] for e in ENGS if e != "sp" and self.by_eng[e]]
        for s in self.dsems:
            if s.last is not None:
                lasts.append(s.last)
        for e in ENGS:
            self.pending[e] = list(lasts)

    def add(self, eng, fn, reads=(), writes=(), dsem=None):
        op = Op()
        op.eng = eng
        op.fn = fn
        op.dsem = dsem
        op.signal = False
        op.ev = None
        deps = set()
        ex = [b for b in reads if b.excl]
        if ex:
            reads = [b for b in reads if not b.excl]
            writes = list(writes) + [b for b in ex if b not in writes]
        if self.pending[eng] is not None:
            deps.update(self.pending[eng])
            self.pending[eng] = None
        for b in reads:
            deps.update(b.ws)
        for b in writes:
            if not b.multi:
                deps.update(b.ws)
            deps.update(b.rs)
        for b in reads:
            b.rs.append(op)
        for b in writes:
            if b.multi:
                b.ws.append(op)
            else:
                b.ws = [op]
            b.rs = []
        if dsem is not None:
            if (not dsem.bulk) and dsem.last is not None:
                deps.add(dsem.last)
            dsem.last = op
            dsem.count += 16
            op.ev = dsem.count
        deps.discard(op)
        fd = []
        for d in deps:
            if d.eng == eng and d.dsem is None:
                if eng == "pe" or not SAME_ENG_SYNC:
                    continue
            fd.append(d)
            d.signal = True
        op.deps = fd
        self.by_eng[eng].append(op)
        return op

    def emit(self, nc, es):
        esem = {}
        for e in ("pe", "act", "dve", "pool"):
            esem[e] = es.enter_context(nc.semaphore("sem_" + e))
        for s in self.dsems:
            s.handle = es.enter_context(nc.semaphore("d_" + s.name))
        for e in ("pe", "act", "dve", "pool"):
            c = 0
            for op in self.by_eng[e]:
                if op.signal:
                    c += 1
                    op.ev = c
        final_waits = [(s.handle, s.count) for s in self.dsems if s.count > 0]
        block = es.enter_context(nc.Block())
        engs = {"pe": block.tensor, "act": block.scalar, "dve": block.vector,
                "pool": block.gpsimd, "sp": block.sync}

        def make(ename):
            ops = self.by_eng[ename]

            def body(eng):
                waited = {}
                for op in ops:
                    need = {}
                    for d in op.deps:
                        if d.dsem is not None:
                            key = ("d", id(d.dsem))
                            sem = d.dsem.handle
                            val = d.dsem.count if d.dsem.bulk else d.ev
                        else:
                            key = ("e", d.eng)
                            sem = esem[d.eng]
                            val = d.ev
                        if key not in need or need[key][1] < val:
                            need[key] = (sem, val)
                    for key, (sem, val) in need.items():
                        if waited.get(key, 0) >= val:
                            continue
                        waited[key] = val
                        eng.wait_ge(sem, val)
                    ins = op.fn(eng)
                    if op.dsem is not None:
                        ins.then_inc(op.dsem.handle, 16)
                    elif op.signal:
                        ins.then_inc(esem[ename], 1)
                if ename == "sp":
                    for sem, val in final_waits:
                        eng.wait_ge(sem, val)
            return body

        for ename, dec in engs.items():
            dec(make(ename))


class T:
    def __init__(self, ap, name, multi=False, sem=None):
        self.ap = ap
        self.b = Buf(name, multi)
        self.sem = sem

    def __getitem__(self, k):
        return self.ap[k]


class Ring:
    def __init__(self, items):
        self.items = items
        self.i = 0

    def next(self):
        it = self.items[self.i % len(self.items)]
        self.i += 1
        return it


def build(NJ=NJ_FULL, stop=99):
    nc = bass.Bass("TRN2", target_bir_lowering=False)
    S = Sched()
    NTOK = NJ * NT
    NSLOT = NJ * 2 * NT

    def din(name, shape, dt=F32):
        return nc.dram_tensor(name, list(shape), dt, kind="ExternalInput").ap()

    x_own = din("x_own", [NTOK, D])
    x_oth = din("x_oth", [NTOK, D])
    w_in = din("w_in", [D, 10240])
    w_glu = din("w_glu", [1024, 1024])
    w_pa = din("w_pa", [1024, 2048])
    w_ps = din("w_ps", [1024, 2048])
    w_out = din("w_out", [D, D])
    gain_col = din("gain_col", [128, 16])
    qg_in = din("qg", [128, 1])
    kg_in = din("kg", [128, 1])
    lamv_in = din("lamv", [128, 4, 64])
    subln_in = din("subln", [128, 1])
    dcol_in = din("dcol", [128, 8])
    bglu_in = din("bglu", [128, 8])
    pn_in = din("pn", [128, 3, NPAIR])
    bn_in = din("bn", [128, 2, NPAIR, 32])
    cn_in = din("cn", [128, 2, NPAIR, 32])
    pt_in = din("pt", [8, 128, 3, 128])
    bt_in = din("bt", [8, 128, 2, 128])
    ident_in = din("ident", [128, 128], BF16)
    ones_in = din("onesb", [128, 128], BF16)
    obd_in = din("onesbd", [128, 128])
    o128_in = din("ones128", [128, 128])
    kv_in = din("kvals", [128, TC + 1])
    negm_in = din("negmask", [128, 4, NT], BF16)
    pic_in = din("picol", [128, 3])
    out_own = nc.dram_tensor("out_own", [NTOK, D], F32, kind="ExternalOutput").ap()

    def dscr(name, shape, dt=BF16):
        return T(nc.dram_tensor(name, list(shape), dt).ap(), name, multi=True)

    win_s = dscr("win_s", [80, 128, 16, 128])
    wglu_s = dscr("wglu_s", [8, 128, 8, 128])
    wpa_s = dscr("wpa_s", [16, 128, 8, 128])
    wps_s = dscr("wps_s", [16, 128, 8, 128])
    wout_s = dscr("wout_s", [4, 128, 16, 512])
    bd_s = dscr("bd_s", [8, 128, TC, 128])
    ws_s = dscr("ws_s", [8, 128, TC, 2, 128])
    vw_s = dscr("vw_s", [8, 128, 4, TC, 2, 32])
    kt_s = dscr("kt_s", [8, 128, NSLOT])
    v_s = dscr("v_s", [8, 128, NSLOT // 128, 128])

    es = contextlib.ExitStack()
    with es:
        arena_h = es.enter_context(nc.sbuf_tensor("arena", [128, ARENA_WORDS], F32))
        aoff = [0]

        def sb(name, shape, dt=F32, sem=False):
            n = 1
            for s_ in shape[1:]:
                n *= s_
            words = n if dt == F32 else (n + 1) // 2
            words += words & 1
            assert aoff[0] + words <= ARENA_WORDS, (name, aoff[0], words)
            ap = arena_h[:, aoff[0]:aoff[0] + words]
            aoff[0] += words
            if dt == BF16:
                ap = ap.bitcast(BF16)
            ap = ap[:, 0:n]
            if len(shape) > 2:
                names = ["a%d" % i for i in range(len(shape) - 1)]
                kw = {nm: s_ for nm, s_ in zip(names, shape[1:])}
                ap = ap.rearrange("p (%s) -> p %s" % (" ".join(names), " ".join(names)), **kw)
            return T(ap, name, sem=S.dsem(name) if sem else None)

        psum_all = es.enter_context(nc.psum_tensor("psall", [128, 4096], F32))
        banks = [T(psum_all[:, 512 * i:512 * (i + 1)], "pb%d" % i) for i in range(8)]
        for bk_ in banks:
            bk_.b.excl = True
        gen_ring = Ring(banks)
        misc_ring = Ring(banks[6:8])

        def mm(out, lhsT, rhs, start, stop, reads, writes):
            return S.add("pe", lambda e: e.matmul(out, lhsT, rhs, start=start, stop=stop),
                         reads=reads, writes=writes)

        def act(out, in_, func, reads, writes, bias=None, scale=None, accum_out=None):
            kw = {}
            if bias is not None:
                kw["bias"] = bias
            if scale is not None:
                kw["scale"] = scale
            if accum_out is not None:
                kw["accum_out"] = accum_out
            return S.add("act", lambda e: e.activation(out, in_, func, **kw), reads=reads, writes=writes)

        def tt(eng, out, in0, in1, op, reads, writes):
            return S.add(eng, lambda e: e.tensor_tensor(out, in0, in1, op), reads=reads, writes=writes)

        def ts(eng, out, in0, s1, s2, op0, op1, reads, writes):
            if s2 is None:
                return S.add(eng, lambda e: e.tensor_scalar(out, in0, s1, None, op0), reads=reads, writes=writes)
            return S.add(eng, lambda e: e.tensor_scalar(out, in0, s1, s2, op0, op1), reads=reads, writes=writes)

        def stt(eng, out, in0, scalar, in1, op0, op1, reads, writes):
            return S.add(eng, lambda e: e.scalar_tensor_tensor(out, in0, scalar, in1, op0, op1),
                         reads=reads, writes=writes)

        def cp(eng, out, in_, reads, writes):
            if eng == "act":
                return S.add("act", lambda e: e.copy(out, in_), reads=reads, writes=writes)
            return S.add(eng, lambda e: e.tensor_copy(out, in_), reads=reads, writes=writes)

        def dma(out, in_, dsem, reads, writes):
            return S.add("sp", lambda e: e.dma_start(out=out, in_=in_), reads=reads, writes=writes, dsem=dsem)

        MUL, ADD, SUB = ALU.mult, ALU.add, ALU.subtract

        bulk = S.dsem("bulk", bulk=True)

        def cload(name, src, shape, dt=F32):
            t = sb("c_" + name, shape, dt)
            dma(t.ap, src, bulk, [], [t.b])
            return t

        gcol = cload("gain", gain_col, [128, 16])
        qg = cload("qg", qg_in, [128, 1])
        kg = cload("kg", kg_in, [128, 1])
        subln = cload("subln", subln_in, [128, 1])
        dcol = cload("dcol", dcol_in, [128, 8])
        bglu = cload("bglu", bglu_in, [128, 8])
        ident = cload("ident", ident_in, [128, 128], BF16)
        onesb = cload("onesb", ones_in, [128, 128], BF16)
        obd = cload("obd", obd_in, [128, 128])
        o128 = cload("o128", o128_in, [128, 128])
        negm = cload("negm", negm_in, [128, 4, NT], BF16)
        picol = cload("picol", pic_in, [128, 3])
        neglam = sb("neglam", [128, 1])
        qg8 = sb("qg8", [128, 1])
        sub08 = sb("sub08", [128, 1])
        SH = [128, 2, NPAIR]
        CA = sb("CA", SH)
        CB = sb("CB", SH)
        A5 = sb("A5", SH)
        A5A = sb("A5A", SH)
        A5B = sb("A5B", SH)
        Est = sb("Est", SH)
        Fst = sb("Fst", SH)
        Xin = sb("Xin", SH)
        Soth = sb("Soth", SH)
        Zst = sb("Zst", SH)
        pt1 = sb("pt1", SH)
        pt2 = sb("pt2", SH)
        pt3 = sb("pt3", SH)
        persist_end = aoff[0]

        lamv = cload("lamv", lamv_in, [128, 4, 64])
        pn = cload("pn", pn_in, [128, 3, NPAIR])
        bn = cload("bn", bn_in, [128, 2, NPAIR, 32])
        cn = cload("cn", cn_in, [128, 2, NPAIR, 32])
        kvals = cload("kvals", kv_in, [128, TC + 1])
        xt0 = [sb("xt%d" % i, [128, D], sem=True) for i in range(2)]
        xn0 = [sb("xn%d" % i, [128, D], BF16, sem=True) for i in range(2)]
        r0 = Ring([0, 1])

        def cast_weight(src, K, N, dst, ncol, use_gain):
            for kt in range(K // 128):
                for c0 in range(0, N, D):
                    w = min(D, N - c0)
                    i = r0.next()
                    a, an = xt0[i], xn0[i]
                    dma(a[:, 0:w], src[kt * 128:(kt + 1) * 128, c0:c0 + w], a.sem, [], [a.b])
                    if use_gain:
                        ts("dve", an[:, 0:w], a[:, 0:w], gcol[:, kt:kt + 1], None, MUL, None, [a.b, gcol.b], [an.b])
                    else:
                        cp("act", an[:, 0:w], a[:, 0:w], [a.b], [an.b])
                    nb = w // ncol
                    dsl = dst[c0 // ncol:c0 // ncol + nb, :, kt, :].rearrange("c p n -> p c n")
                    dma(dsl, an[:, 0:w].rearrange("p (c n) -> p c n", n=ncol), an.sem, [an.b], [dst.b])

        if stop >= 1:
            cast_weight(w_in, D, 10240, win_s, 128, True)
            cast_weight(w_glu, 1024, 1024, wglu_s, 128, False)
            cast_weight(w_pa, 1024, 2048, wpa_s, 128, False)
            cast_weight(w_ps, 1024, 2048, wps_s, 128, False)
            cast_weight(w_out, D, D, wout_s, 512, False)

        real_add = S.add
        if stop < 2:
            S.add = lambda *a_, **k_: None
        lam_init = 0.8 - 0.6 * 1.0
        ltmp = sb("ltmp", [128, 2, 64])
        lsum = sb("lsum", [128, 2])
        tt("dve", ltmp[:, 0, :], lamv[:, 0, :], lamv[:, 1, :], MUL, [lamv.b], [ltmp.b])
        tt("dve", ltmp[:, 1, :], lamv[:, 2, :], lamv[:, 3, :], MUL, [lamv.b], [ltmp.b])
        S.add("dve", lambda e: e.reduce_sum(lsum[:, :], ltmp[:, :, :], AX.X), reads=[ltmp.b], writes=[lsum.b])
        act(lsum[:, :], lsum[:, :], AF.Exp, [lsum.b], [lsum.b])
        stt("dve", neglam[:, :], lsum[:, 1:2], -lam_init, lsum[:, 0:1], ADD, SUB, [lsum.b], [neglam.b])
        ts("dve", qg8[:, :], qg[:, :], 0.125, None, MUL, None, [qg.b], [qg8.b])
        ts("dve", sub08[:, :], subln[:, :], 1.0 - lam_init, None, MUL, None, [subln.b], [sub08.b])

        def trig_alloc(prefix, n):
            sh3 = [128, n, TC + 1]
            sh2 = [128, n]
            d = {"n": n}
            for nm in ("arg", "t1", "t2", "PR", "PI"):
                d[nm] = sb(prefix + nm, sh3)
            for nm in ("dt", "lrdt", "ang", "nr", "den", "u1", "cr", "ci"):
                d[nm] = sb(prefix + nm, sh2)
            return d

        def trig_tables(d, lre, lim, ldt, rd):
            n = d["n"]
            sh3 = [128, n, TC + 1]
            arg, t1, t2, PR, PI = d["arg"], d["t1"], d["t2"], d["PR"], d["PI"]
            dtt, lrdt, ang, nr, den, u1, cr, ci = (d[k] for k in ("dt", "lrdt", "ang", "nr", "den", "u1", "cr", "ci"))
            act(dtt[:, :], ldt, AF.Exp, rd, [dtt.b])
            tt("dve", lrdt[:, :], lre, dtt[:, :], MUL, rd + [dtt.b], [lrdt.b])
            tt("dve", ang[:, :], lim, dtt[:, :], MUL, rd + [dtt.b], [ang.b])
            kb = kvals[:, :].unsqueeze(1).to_broadcast(sh3)
            tt("dve", arg[:, :, :], lrdt[:, :].unsqueeze(2).to_broadcast(sh3), kb, MUL, [lrdt.b, kvals.b], [arg.b])
            act(PR[:, :, :], arg[:, :, :], AF.Exp, [arg.b], [PR.b])
            tt("dve", arg[:, :, :], ang[:, :].unsqueeze(2).to_broadcast(sh3), kb, MUL, [ang.b, kvals.b], [arg.b])

            def reduce_sin(dst, shift):
                ts("dve", t2[:, :, :], arg[:, :, :], shift, None, ADD, None, [arg.b], [t2.b])
                ts("dve", t1[:, :, :], t2[:, :, :], 1.0 / TWO_PI, MAGIC, MUL, ADD, [t2.b], [t1.b])
                ts("dve", t1[:, :, :], t1[:, :, :], MAGIC, None, SUB, None, [t1.b], [t1.b])
                stt("dve", t1[:, :, :], t1[:, :, :], -TWO_PI, t2[:, :, :], MUL, ADD, [t1.b, t2.b], [t1.b])
                ts("dve", t1[:, :, :], t1[:, :, :], 3.1415925, -3.1415925, ALU.min, ALU.max, [t1.b], [t1.b])
                act(dst[:, :, :], t1[:, :, :], AF.Sin, [t1.b], [dst.b])

            reduce_sin(PI, 0.0)
            tt("dve", PI[:, :, :], PI[:, :, :], PR[:, :, :], MUL, [PI.b, PR.b], [PI.b])
            reduce_sin(t1, 1.5707963267948966)
            tt("dve", PR[:, :, :], PR[:, :, :], t1[:, :, :], MUL, [PR.b, t1.b], [PR.b])
            ts("dve", nr[:, :], PR[:, :, 1], -1.0, None, ADD, None, [PR.b], [nr.b])
            tt("dve", den[:, :], lre, lre, MUL, rd, [den.b])
            tt("dve", u1[:, :], lim, lim, MUL, rd, [u1.b])
            tt("dve", den[:, :], den[:, :], u1[:, :], ADD, [den.b, u1.b], [den.b])
            S.add("dve", lambda e: e.reciprocal(den[:, :], den[:, :]), reads=[den.b], writes=[den.b])
            tt("dve", cr[:, :], nr[:, :], lre, MUL, rd + [nr.b], [cr.b])
            tt("dve", u1[:, :], PI[:, :, 1], lim, MUL, rd + [PI.b], [u1.b])
            tt("dve", cr[:, :], cr[:, :], u1[:, :], ADD, [cr.b, u1.b], [cr.b])
            tt("dve", cr[:, :], cr[:, :], den[:, :], MUL, [cr.b, den.b], [cr.b])
            tt("dve", ci[:, :], PI[:, :, 1], lre, MUL, rd + [PI.b], [ci.b])
            tt("dve", u1[:, :], nr[:, :], lim, MUL, rd + [nr.b], [u1.b])
            tt("dve", ci[:, :], ci[:, :], u1[:, :], SUB, [ci.b, u1.b], [ci.b])
            tt("dve", ci[:, :], ci[:, :], den[:, :], MUL, [ci.b, den.b], [ci.b])

        dn = trig_alloc("n_", NPAIR)
        trig_tables(dn, pn[:, 0, :], pn[:, 1, :], pn[:, 2, :], [pn.b])
        PRn, PIn, crn, cin_ = dn["PR"], dn["PI"], dn["cr"], dn["ci"]

        def pack_coef(A, CAt, CBt):
            cp("dve", CAt[:, 0, :], A[:, 0, :], [A.b], [CAt.b])
            cp("dve", CAt[:, 1, :], A[:, 0, :], [A.b], [CAt.b])
            ts("dve", CBt[:, 0, :], A[:, 1, :], -1.0, None, MUL, None, [A.b], [CBt.b])
            cp("dve", CBt[:, 1, :], A[:, 1, :], [A.b], [CBt.b])

        cp("dve", A5[:, 0, :], PRn[:, :, TC], [PRn.b], [A5.b])
        cp("dve", A5[:, 1, :], PIn[:, :, TC], [PIn.b], [A5.b])
        pack_coef(A5, CA, CB)
        for _ in range(6):
            tt("dve", pt1[:, :, :], A5[:, :, :], A5[:, :, :], MUL, [A5.b], [pt1.b])
            tt("dve", pt2[:, 0, :], A5[:, 0, :], A5[:, 1, :], MUL, [A5.b], [pt2.b])
            tt("dve", A5[:, 0, :], pt1[:, 0, :], pt1[:, 1, :], SUB, [pt1.b], [A5.b])
            ts("dve", A5[:, 1, :], pt2[:, 0, :], 2.0, None, MUL, None, [pt2.b], [A5.b])
        pack_coef(A5, A5A, A5B)
        S.add("pool", lambda e: e.memset(Est[:, :, :], 0.0), writes=[Est.b])
        S.add("pool", lambda e: e.memset(Zst[:, :, :], 0.0), writes=[Zst.b])

        sh4 = [128, NPAIR, 32]
        bbn = sb("bbn", [128, 2, NPAIR, 32])
        w1 = sb("w1", sh4)
        g2 = sb("g2", sh4)
        gre = sb("gre", [128, NPAIR, 64])
        gim = sb("gim", [128, NPAIR, 64])
        S.add("pool", lambda e: e.memset(gre[:, :, :], 0.0), writes=[gre.b])
        S.add("pool", lambda e: e.memset(gim[:, :, :], 0.0), writes=[gim.b])
        ncim = sb("ncim", sh4)
        crb = crn[:, :].unsqueeze(2).to_broadcast(sh4)
        cib = cin_[:, :].unsqueeze(2).to_broadcast(sh4)
        tt("dve", bbn[:, 0, :, :], bn[:, 0, :, :], crb, MUL, [bn.b, crn.b], [bbn.b])
        tt("dve", w1[:, :, :], bn[:, 1, :, :], cib, MUL, [bn.b, cin_.b], [w1.b])
        tt("dve", bbn[:, 0, :, :], bbn[:, 0, :, :], w1[:, :, :], SUB, [bbn.b, w1.b], [bbn.b])
        tt("dve", bbn[:, 1, :, :], bn[:, 1, :, :], crb, MUL, [bn.b, crn.b], [bbn.b])
        tt("dve", w1[:, :, :], bn[:, 0, :, :], cib, MUL, [bn.b, cin_.b], [w1.b])
        tt("dve", bbn[:, 1, :, :], bbn[:, 1, :, :], w1[:, :, :], ADD, [bbn.b, w1.b], [bbn.b])
        ts("dve", ncim[:, :, :], cn[:, 1, :, :], -1.0, None, MUL, None, [cn.b], [ncim.b])

        vw = sb("vw", [128, NPAIR, TC, 2, 32], BF16, sem=True)
        for i in range(TC):
            prb = PRn[:, :, i + 1].unsqueeze(2).to_broadcast(sh4)
            pib = PIn[:, :, i + 1].unsqueeze(2).to_broadcast(sh4)
            tt("dve", w1[:, :, :], cn[:, 0, :, :], prb, MUL, [cn.b, PRn.b], [w1.b])
            tt("dve", g2[:, :, :], ncim[:, :, :], pib, MUL, [ncim.b, PIn.b], [g2.b])
            tt("dve", vw[:, :, i, 0, :], w1[:, :, :], g2[:, :, :], ADD, [w1.b, g2.b], [vw.b])
            tt("dve", w1[:, :, :], cn[:, 0, :, :], pib, MUL, [cn.b, PIn.b], [w1.b])
            tt("dve", g2[:, :, :], ncim[:, :, :], prb, MUL, [ncim.b, PRn.b], [g2.b])
            tt("dve", vw[:, :, i, 1, :], g2[:, :, :], w1[:, :, :], SUB, [w1.b, g2.b], [vw.b])
        for tau in range(8):
            dma(vw_s[tau], vw[:, 4 * tau:4 * tau + 4, :, :, :], vw.sem, [vw.b], [vw_s.b])

        bd = sb("bd", [128, 8, TC, 128], BF16, sem=True)
        S.add("pool", lambda e: e.memset(bd[:, :, :, :], 0.0), writes=[bd.b])
        for l in range(TC):
            prb = PRn[:, :, l].unsqueeze(2).to_broadcast(sh4)
            pib = PIn[:, :, l].unsqueeze(2).to_broadcast(sh4)
            tt("dve", gre[:, :, 32:64], bbn[:, 0, :, :], prb, MUL, [bbn.b, PRn.b], [gre.b])
            tt("dve", g2[:, :, :], bbn[:, 1, :, :], pib, MUL, [bbn.b, PIn.b], [g2.b])
            tt("dve", gre[:, :, 32:64], gre[:, :, 32:64], g2[:, :, :], SUB, [gre.b, g2.b], [gre.b])
            tt("dve", gim[:, :, 32:64], bbn[:, 0, :, :], pib, MUL, [bbn.b, PIn.b], [gim.b])
            tt("dve", g2[:, :, :], bbn[:, 1, :, :], prb, MUL, [bbn.b, PRn.b], [g2.b])
            tt("dve", gim[:, :, 32:64], gim[:, :, 32:64], g2[:, :, :], ADD, [gim.b, g2.b], [gim.b])
            for tau in range(8):
                bk = gen_ring.next()
                for q in range(4):
                    P = 4 * tau + q
                    if q < 3:
                        o = bk[32 * q:32 * q + 32, 32 * q:32 * q + 32]
                        c0_ = 32
                    else:
                        o = bk[64:128, 96:128]
                        c0_ = 0
                    mm(o, gre[:, P, c0_:64], cn[:, 0, P, :], True, False, [gre.b, cn.b], [bk.b])
                    mm(o, gim[:, P, c0_:64], ncim[:, P, :], False, True, [gim.b, ncim.b], [bk.b])
                for q in range(3):
                    cp("dve", bd[32 * q:32 * q + 32, tau, l, 32 * q:32 * q + 32],
                       bk[32 * q:32 * q + 32, 32 * q:32 * q + 32], [bk.b], [bd.b])
                cp("dve", bd[64:128, tau, l, 96:128], bk[64:128, 96:128], [bk.b], [bd.b])
        for tau in range(8):
            dma(bd_s[tau], bd[:, tau, :, :], bd.sem, [bd.b], [bd_s.b])

        ptl = sb("ptl", [128, 3, 128], sem=True)
        btl = sb("btl", [128, 2, 128], sem=True)
        wsw0 = [sb("wsw0_%d" % i, [128, TC, 2, 128], BF16, sem=True) for i in range(2)]
        bbt = sb("bbt", [128, 2, 128])
        w2 = sb("w2", [128, 128])
        w3 = sb("w3", [128, 128])
        dt_ = trig_alloc("t_", 128)
        for tau in range(8):
            dma(ptl[:, :, :], pt_in[tau], ptl.sem, [], [ptl.b])
            dma(btl[:, :, :], bt_in[tau], btl.sem, [], [btl.b])
            trig_tables(dt_, ptl[:, 0, :], ptl[:, 1, :], ptl[:, 2, :], [ptl.b])
            PRt, PIt, crt, cit = dt_["PR"], dt_["PI"], dt_["cr"], dt_["ci"]
            tt("dve", bbt[:, 0, :], btl[:, 0, :], crt[:, :], MUL, [btl.b, crt.b], [bbt.b])
            tt("dve", w2[:, :], btl[:, 1, :], cit[:, :], MUL, [btl.b, cit.b], [w2.b])
            tt("dve", bbt[:, 0, :], bbt[:, 0, :], w2[:, :], SUB, [bbt.b, w2.b], [bbt.b])
            tt("dve", bbt[:, 1, :], btl[:, 1, :], crt[:, :], MUL, [btl.b, crt.b], [bbt.b])
            tt("dve", w2[:, :], btl[:, 0, :], cit[:, :], MUL, [btl.b, cit.b], [w2.b])
            tt("dve", bbt[:, 1, :], bbt[:, 1, :], w2[:, :], ADD, [bbt.b, w2.b], [bbt.b])
            ws = wsw0[tau % 2]
            for j in range(TC):
                k = TC - 1 - j
                tt("dve", w2[:, :], bbt[:, 0, :], PRt[:, :, k], MUL, [bbt.b, PRt.b], [w2.b])
                tt("dve", w3[:, :], bbt[:, 1, :], PIt[:, :, k], MUL, [bbt.b, PIt.b], [w3.b])
                tt("dve", ws[:, j, 0, :], w2[:, :], w3[:, :], SUB, [w2.b, w3.b], [ws.b])
                tt("dve", w2[:, :], bbt[:, 0, :], PIt[:, :, k], MUL, [bbt.b, PIt.b], [w2.b])
                tt("dve", w3[:, :], bbt[:, 1, :], PRt[:, :, k], MUL, [bbt.b, PRt.b], [w3.b])
                tt("dve", ws[:, j, 1, :], w2[:, :], w3[:, :], ADD, [w2.b, w3.b], [ws.b])
            dma(ws_s[tau], ws[:, :, :, :], ws.sem, [ws.b], [ws_s.b])

        S.add = real_add
        S.barrier()
        aoff[0] = persist_end
        hT = {"own": sb("hT_own", [128, 16, NT], BF16), "oth": sb("hT_oth", [128, 16, NT], BF16)}
        mT = hT["oth"]
        uTc = sb("uTc", [128, 8, 2, NT], BF16)
        uT = {"oth": T(uTc[:, :, 0, :], "uT_oth"), "own": T(uTc[:, :, 1, :], "uT_own")}
        uT["oth"].b = uTc.b
        uT["own"].b = uTc.b
        gT = uT["oth"]
        ybuf = sb("ybuf", [128, 4096])
        yssm = T(ybuf[:, 0:2048].bitcast(BF16).rearrange("p (a n) -> p a n", n=NT), "yssm")
        yatt = T(ybuf[:, 2048:4096].bitcast(BF16).rearrange("p (a n) -> p a n", n=NT), "yatt")
        Sown = T(hT["oth"].ap.rearrange("p a n -> p (a n)").bitcast(F32)
                 .rearrange("p (c r q) -> p c r q", r=2, q=NPAIR), "Sown")
        Sown.b = hT["oth"].b
        wt_ring = Ring([sb("wt%d" % i, [128, 16, 128], BF16, sem=True) for i in range(3)])
        wg = sb("wg", [128, 16, 512], BF16, sem=True)
        ktp_ring = Ring([sb("ktp%d" % i, [128, 1024], BF16, sem=True) for i in range(4)])
        vp_ring = Ring([sb("vp%d" % i, [128, 8, 128], BF16, sem=True) for i in range(4)])
        p_ring = Ring([sb("P%d" % i, [128, 2, NT], BF16) for i in range(3)])
        pacc = [sb("pacc%d" % i, [128, 2, NT]) for i in range(2)]
        qT = sb("qT", [128, NT], BF16)
        Sbuf = sb("Sbuf", [128, NCH, 2, NPAIR])
        Xh = sb("Xh", [128, 2, NPAIR, NCH], BF16)
        bdw_ring = Ring([sb("bdw%d" % i, [128, TC, 128], BF16, sem=True) for i in range(1)])
        wsw_ring = Ring([sb("wsw%d" % i, [128, TC, 2, 128], BF16, sem=True) for i in range(1)])
        vww_ring = Ring([sb("vww%d" % i, [128, 4, TC, 2, 32], BF16, sem=True) for i in range(1)])
        wsw3 = sb("wsw3", [128, TC, 2, 128], BF16, sem=True)
        vw3 = sb("vw3", [128, TC, 2, 64], BF16, sem=True)
        S.add("pool", lambda e: e.memset(wsw3[64:96, :, :, :], 0.0), writes=[wsw3.b])
        S.add("pool", lambda e: e.memset(vw3[:, :, :, 0:32], 0.0), writes=[vw3.b])
        vw2 = sb("vw2", [128, TC, 2, 64], BF16, sem=True)
        S.add("pool", lambda e: e.memset(vw2[:, :, :, 32:64], 0.0), writes=[vw2.b])
        xt = sb("xt", [128, D], sem=True)
        xn = sb("xn", [128, D], BF16)
        wk_ring = Ring([sb("wk%d" % i, [128, NT]) for i in range(4)])
        xres_ring = Ring([sb("xres%d" % i, [128, NT], sem=True) for i in range(2)])
        ot_ring = Ring([sb("ot%d" % i, [128, NT], sem=True) for i in range(2)])
        ktile_ring = Ring([sb("ktile%d" % i, [128, NT], BF16, sem=True) for i in range(2)])
        vtile_ring = Ring([sb("vtile%d" % i, [128, NT], BF16, sem=True) for i in range(2)])
        ssq = sb("ssq", [128, 1])
        rstd = sb("rstd", [128, 1])

        evac_rr = [0]

        def evac_eng():
            evac_rr[0] += 1
            return "dve" if evac_rr[0] % 2 else "act"

        def load_wt(src, ct):
            w = wt_ring.next()
            nk = src.ap.shape[2]
            dma(w[:, 0:nk, :], src[ct], w.sem, [src.b], [w.b])
            return w

        def proj_fm(src, ct, hTt, ring, nkt=16):
            w = load_wt(src, ct)
            bk = ring.next()
            for kt in range(nkt):
                mm(bk[:, :], w[:, kt, :], hTt[:, kt, :], kt == 0, kt == nkt - 1, [w.b, hTt.b], [bk.b])
            return bk

        def run_jobs(jobs, ring):
            slots = {}

            def load(i):
                if i < len(jobs) and i not in slots:
                    slots[i] = load_wt(jobs[i][0], jobs[i][1])

            load(0)
            load(1)
            for i, (src, ct, rhsT, nkt, epi) in enumerate(jobs):
                load(i + 2)
                w = slots.pop(i)
                bk = ring.next()
                for kt in range(nkt):
                    mm(bk[:, :], w[:, kt, :], rhsT[:, kt, :], kt == 0, kt == nkt - 1, [w.b, rhsT.b], [bk.b])
                epi(bk)

        def rms_to_hT(role, J, ring):
            xsrc = x_own if role == "own" else x_oth
            h = hT[role]
            for tb in range(4):
                r0_ = J * NT + tb * 128
                dma(xt[:, :], xsrc[r0_:r0_ + 128, :], xt.sem, [], [xt.b])
                act(xn[:, :], xt[:, :], AF.Square, [xt.b], [xn.b, ssq.b], accum_out=ssq[:, :])
                ts("dve", rstd[:, :], ssq[:, :], 1.0 / D, EPS, MUL, ADD, [ssq.b], [rstd.b])
                act(rstd[:, :], rstd[:, :], AF.Sqrt, [rstd.b], [rstd.b])
                S.add("dve", lambda e: e.reciprocal(rstd[:, :], rstd[:, :]), reads=[rstd.b], writes=[rstd.b])
                ts("dve", xn[:, :], xt[:, :], rstd[:, 0:1], None, MUL, None, [xt.b, rstd.b], [xn.b])
                for half in range(2):
                    bk = ring.next()
                    bkb = bk.ap.bitcast(BF16)
                    for k in range(8):
                        kt = half * 8 + k
                        o_, i_ = bkb[:, k * 128:(k + 1) * 128], xn[:, kt * 128:(kt + 1) * 128]
                        S.add("pe", lambda e, o_=o_, i_=i_: e.transpose(o_, i_, ident[:, :]),
                              reads=[xn.b, ident.b], writes=[bk.b])
                    cp(evac_eng(), h[:, half * 8:(half + 1) * 8, tb * 128:(tb + 1) * 128],
                       bkb.rearrange("p (k n) -> p k n", n=128), [bk.b], [h.b])

        def qknorm(bk, gcol_t, dst_ap, dst_b, ring):
            sq = wk_ring.next()
            act(sq[:, :], bk[:, :], AF.Square, [bk.b], [sq.b])
            b2 = ring.next()
            mm(b2[:, :], obd[:, :], sq[:, :], True, True, [obd.b, sq.b], [b2.b])
            rs = wk_ring.next()
            ts("dve", rs[:, :], b2[:, :], EPS, None, ADD, None, [b2.b], [rs.b])
            act(rs[:, :], rs[:, :], AF.Ln, [rs.b], [rs.b])
            act(rs[:, :], rs[:, :], AF.Exp, [rs.b], [rs.b], scale=-0.5)
            stt("dve", dst_ap, bk[:, :], gcol_t[:, 0:1], rs[:, :], MUL, MUL, [bk.b, gcol_t.b, rs.b], [dst_b])

        def stage_kvu(role, J, ring):
            ridx = 0 if role == "own" else 1
            slot0 = J * 2 * NT + ridx * NT
            h = hT[role]
            jobs = []

            def epi_k(hd):
                def f(bk):
                    kt_ = ktile_ring.next()
                    qknorm(bk, kg, kt_[:, :], kt_.b, ring)
                    dma(kt_s[hd, :, slot0:slot0 + NT], kt_[:, :], kt_.sem, [kt_.b], [kt_s.b])
                return f

            def epi_u(tau):
                def f(bk):
                    cp(evac_eng(), uT[role][:, tau, :], bk[:, :], [bk.b], [uT[role].b])
                return f

            for hd in range(8):
                jobs.append((win_s, 8 + hd, h, 16, epi_k(hd)))
            for tau in range(8):
                jobs.append((win_s, 32 + tau, h, 16, epi_u(tau)))
            run_jobs(jobs, ring)
            for grp in range(2):
                dma(wg.ap.rearrange("p k (c n) -> p k c n", n=128),
                    win_s[16 + 4 * grp:20 + 4 * grp].rearrange("c p k n -> p k c n"), wg.sem, [win_s.b], [wg.b])
                for tb in range(4):
                    bk = ring.next()
                    for kt in range(16):
                        mm(bk[:, :], h[:, kt, tb * 128:(tb + 1) * 128], wg[:, kt, :], kt == 0, kt == 15,
                           [h.b, wg.b], [bk.b])
                    vt = vtile_ring.next()
                    cp(evac_eng(), vt[:, :], bk[:, :], [bk.b], [vt.b])
                    blk = slot0 // 128 + tb
                    dma(v_s[4 * grp:4 * grp + 4, :, blk, :].rearrange("h p e -> p h e"),
                        vt[:, :].rearrange("p (h e) -> p h e", e=128), vt.sem, [vt.b], [v_s.b])

        def cmul_add(dst, X, Aa, Ab, addend):
            tt("pool", pt1[:, :, :], X[:, :, :], Aa[:, :, :], MUL, [X.b, Aa.b], [pt1.b])
            tt("pool", pt2[:, 0, :], X[:, 1, :], Ab[:, 0, :], MUL, [X.b, Ab.b], [pt2.b])
            tt("pool", pt2[:, 1, :], X[:, 0, :], Ab[:, 1, :], MUL, [X.b, Ab.b], [pt2.b])
            tt("pool", pt1[:, :, :], pt1[:, :, :], pt2[:, :, :], ADD, [pt1.b, pt2.b], [pt1.b])
            tt("pool", dst[:, :, :], pt1[:, :, :], addend[:, :, :], ADD, [pt1.b, addend.b], [dst.b])

        def ssm_smat(J, ring):
            for tau in range(8):
                wsw = wsw_ring.next()
                dma(wsw[:, :, :, :], ws_s[tau], wsw.sem, [ws_s.b], [wsw.b])
                dma(wsw3[96:128, :, :, :], ws_s[tau, 96:128], wsw3.sem, [ws_s.b], [wsw3.b])
                bks = [ring.next() for _ in range(4)]
                for q in range(4):
                    bk = bks[q]
                    for ri in range(2):
                      for ro in range(SPLIT_ROLES):
                        if SPLIT_ROLES == 1:
                            o = bk[:, ri * 2 * NCH:(ri + 1) * 2 * NCH]
                        else:
                            o = bk[:, (ri * 2 + ro) * NCH:(ri * 2 + ro + 1) * NCH]
                        for j in range(TC):
                            if q < 3:
                                uu = uTc[32 * q:32 * q + 32, tau, :, :]
                                ww = wsw[32 * q:32 * q + 32, j, ri, :]
                            else:
                                uu = uTc[64:128, tau, :, :]
                                ww = wsw3[64:128, j, ri, :]
                            if SPLIT_ROLES == 1:
                                rhs_ = uu.rearrange("p o n -> p (o n)")[:, j::TC]
                            else:
                                rhs_ = uu[:, ro, j::TC]
                            mm(o, ww, rhs_, j == 0, j == TC - 1, [wsw.b, wsw3.b, uTc.b], [bk.b])
                for q in range(4):
                    src = bks[q][:, 0:4 * NCH].rearrange("p (r o c) -> p r o c", r=2, o=2)
                    cp(evac_eng(), Sbuf[:, :, :, 4 * tau + q].rearrange("p c r -> p r c"),
                       src[:, :, 0, :], [bks[q].b], [Sbuf.b])
                    cp(evac_eng(), Sown[:, :, :, 4 * tau + q].rearrange("p c r -> p r c"),
                       src[:, :, 1, :], [bks[q].b], [Sown.b])

        def ssm_scan(role, J):
            Sb = Sbuf if role == "oth" else Sown
            sbufs = [Sbuf.b] if role == "oth" else [Sown.b]
            if role == "oth":
                x0 = Zst
            else:
                cmul_add(pt3, Est, A5A, A5B, Soth)
                ts("pool", Xin[:, :, :], Est[:, :, :], picol[:, 1:2], None, MUL, None, [Est.b, picol.b], [Xin.b])
                ts("pool", pt3[:, :, :], pt3[:, :, :], picol[:, 0:1], None, MUL, None, [pt3.b, picol.b], [pt3.b])
                tt("pool", Xin[:, :, :], Xin[:, :, :], pt3[:, :, :], ADD, [Xin.b, pt3.b], [Xin.b])
                x0 = Xin
            for c in range(NCH):
                rbs = [x0.b] if c == 0 else sbufs
                tt("pool", pt1[:, :, :], (x0[:, :, :] if c == 0 else Sb[:, c - 1, :, :]), CA[:, :, :], MUL,
                   rbs + [CA.b], [pt1.b])
                xim = x0[:, 1, :] if c == 0 else Sb[:, c - 1, 1, :]
                xre = x0[:, 0, :] if c == 0 else Sb[:, c - 1, 0, :]
                tt("pool", pt2[:, 0, :], xim, CB[:, 0, :], MUL, rbs + [CB.b], [pt2.b])
                tt("pool", pt2[:, 1, :], xre, CB[:, 1, :], MUL, rbs + [CB.b], [pt2.b])
                tt("pool", pt1[:, :, :], pt1[:, :, :], pt2[:, :, :], ADD, [pt1.b, pt2.b], [pt1.b])
                tt("pool", Sb[:, c, :, :], Sb[:, c, :, :], pt1[:, :, :], ADD, sbufs + [pt1.b], sbufs)
            if role == "oth":
                cp("pool", Soth[:, :, :], Sb[:, NCH - 1, :, :], sbufs, [Soth.b])
            else:
                cp("pool", Fst[:, :, :], Sb[:, NCH - 1, :, :], sbufs, [Fst.b])
                cp("pool", Xh[:, :, :, 0], Xin[:, :, :], [Xin.b], [Xh.b])
                cp("dve", Xh[:, :, :, 1:NCH], Sb[:, 0:NCH - 1, :, :].rearrange("p c r q -> p r q c"),
                   sbufs, [Xh.b])
                cmul_add(pt3, Fst, A5A, A5B, Soth)
                ts("pool", Est[:, :, :], Fst[:, :, :], picol[:, 0:1], None, MUL, None, [Fst.b, picol.b], [Est.b])
                ts("pool", pt3[:, :, :], pt3[:, :, :], picol[:, 1:2], None, MUL, None, [pt3.b, picol.b], [pt3.b])
                tt("pool", Est[:, :, :], Est[:, :, :], pt3[:, :, :], ADD, [Est.b, pt3.b], [Est.b])

        def ssm_out(J, ring):
            u = uT["own"]
            for tau in range(8):
                bdw = bdw_ring.next()
                vww = vww_ring.next()
                dma(bdw[:, :, :], bd_s[tau], bdw.sem, [bd_s.b], [bdw.b])
                dma(vww[:, :, :, :, :], vw_s[tau], vww.sem, [vw_s.b], [vww.b])
                dma(vw3[:, :, :, 32:64], vw_s[tau, :, 3, :, :, :], vw3.sem, [vw_s.b], [vw3.b])
                dma(vw2[:, :, :, 0:32], vw_s[tau, :, 2, :, :, :], vw2.sem, [vw_s.b], [vw2.b])
                bk = ring.next()
                bkv = bk[:, :].rearrange("p (c i) -> p c i", i=TC)
                uv = u[:, tau, :].rearrange("p (c i) -> p c i", i=TC)
                for i in range(TC):
                    for j in range(i + 1):
                        mm(bkv[:, :, i], bdw[:, i - j, :], uv[:, :, j], (i == 0 and j == 0), False,
                           [bdw.b, u.b], [bk.b])
                for i in range(TC):
                    for q in range(4):
                        for ri in range(2):
                            if q < 2:
                                mm(bkv[32 * q:32 * q + 32, :, i], vww[:, q, i, ri, :], Xh[:, ri, 4 * tau + q, :],
                                   False, (i == TC - 1 and ri == 1), [vww.b, Xh.b], [bk.b])
                            else:
                                vz = vw2 if q == 2 else vw3
                                mm(bkv[64:128, :, i], vz[:, i, ri, :], Xh[:, ri, 4 * tau + q, :],
                                   False, (i == TC - 1 and q == 3 and ri == 1), [vz.b, Xh.b], [bk.b])
                yp = wk_ring.next()
                stt("dve", yp[:, :], u[:, tau, :], dcol[:, tau:tau + 1], bk[:, :], MUL, ADD,
                    [u.b, dcol.b, bk.b], [yp.b])
                gw = wk_ring.next()
                act(gw[:, :], yp[:, :], AF.Square, [yp.b], [gw.b])
                ts("dve", gw[:, :], gw[:, :], 0.044715, 1.0, MUL, ADD, [gw.b], [gw.b])
                tt("dve", gw[:, :], gw[:, :], yp[:, :], MUL, [gw.b, yp.b], [gw.b])
                act(gw[:, :], gw[:, :], AF.Sigmoid, [gw.b], [gw.b], scale=1.5957691216057308)
                tt("dve", gT[:, tau, :], yp[:, :], gw[:, :], MUL, [yp.b, gw.b], [gT.b])

        def glu_stage(J, ring):
            jobs = []
            st = {}

            def epi_g(n):
                def f(b1):
                    sg = wk_ring.next()
                    act(sg[:, :], b1[:, :], AF.Sigmoid, [b1.b, bglu.b], [sg.b], bias=bglu[:, n:n + 1])
                    st[n] = sg
                return f

            def epi_z(n):
                def f(b2):
                    sg = st[n]
                    sz = wk_ring.next()
                    act(sz[:, :], b2[:, :], AF.Silu, [b2.b], [sz.b])
                    tt("dve", sg[:, :], sg[:, :], sz[:, :], MUL, [sg.b, sz.b], [sg.b])
                    tt("dve", yssm[:, n, :], sg[:, :], gT[:, n, :], MUL, [sg.b, gT.b], [yssm.b])
                return f

            for n in range(8):
                jobs.append((wglu_s, n, gT, 8, epi_g(n)))
                jobs.append((win_s, 40 + n, hT["own"], 16, epi_z(n)))
            run_jobs(jobs, ring)

        def attention(J):
            h = hT["own"]
            sp_ring = Ring([0, 2])
            Ob = banks[4:6]
            acc_eng = ("dve", "dve")
            wq_next = load_wt(win_s, 0)
            for hd in range(8):
                wq = wq_next
                wz = load_wt(win_s, 24 + hd)
                if hd < 7:
                    wq_next = load_wt(win_s, hd + 1)
                pieces = {}

                def ensure(jp, hd=hd, pieces=pieces):
                    if jp > J or jp in pieces:
                        return
                    ktp = ktp_ring.next()
                    vp = vp_ring.next()
                    dma(ktp[:, :], kt_s[hd, :, jp * 1024:(jp + 1) * 1024], ktp.sem, [kt_s.b], [ktp.b])
                    dma(vp[:, :, :], v_s[hd, :, jp * 8:(jp + 1) * 8, :], vp.sem, [v_s.b], [vp.b])
                    pieces[jp] = (ktp, vp)

                ensure(0)
                ensure(1)
                bq = misc_ring.next()
                for kt in range(16):
                    mm(bq[:, :], wq[:, kt, :], h[:, kt, :], kt == 0, kt == 15, [wq.b, h.b], [bq.b])
                qknorm(bq, qg8, qT[:, :], qT.b, misc_ring)
                units = []
                for jp in range(J + 1):
                    for blk in range(8):
                        units.append((jp, blk))
                nu = len(units)

                def issue_s(n):
                    jp, blk = units[n]
                    ensure(jp)
                    if blk == 0:
                        ensure(jp + 2)
                    ktp, vp = pieces[jp]
                    bA = sp_ring.next()
                    diag = (jp == J and blk < 4)
                    for c in range(2):
                        sbk = banks[bA + c]
                        mm(sbk[:, :], ktp[64 * c:64 * c + 64, blk * 128:(blk + 1) * 128], qT[64 * c:64 * c + 64, :],
                           True, not diag, [ktp.b, qT.b], [sbk.b])
                        if diag:
                            mm(sbk[:, :], ident[:, :], negm[:, blk, :], False, True, [ident.b, negm.b], [sbk.b])
                    p = p_ring.next()
                    src = psum_all[:, 512 * bA:512 * bA + 1024]
                    rd = [banks[bA].b, banks[bA + 1].b]
                    if jp == J and blk >= 4:
                        act(p[:, :, :], src.rearrange("p (c n) -> p c n", c=2), AF.Exp, rd + [picol.b], [p.b],
                            bias=picol[:, 2:3])
                    else:
                        act(p[:, :, :], src.rearrange("p (c n) -> p c n", c=2), AF.Exp, rd, [p.b])
                    return p

                def issue_pv(n, p, hd=hd):
                    jp, blk = units[n]
                    ktp, vp = pieces[jp]
                    first = n == 0
                    last = n == nu - 1
                    for c in range(2):
                        mm(Ob[c][:, :], vp[:, blk, :], p[:, c, :], first, last, [vp.b, p.b], [Ob[c].b])
                    a_ = n % 2
                    ae = "pool" if (a_ == 1 and hd >= 4) else "dve"
                    if n < 2:
                        cp(ae, pacc[a_][:, :, :], p[:, :, :], [p.b], [pacc[a_].b])
                    else:
                        tt(ae, pacc[a_][:, :, :], pacc[a_][:, :, :], p[:, :, :], ADD,
                           [pacc[a_].b, p.b], [pacc[a_].b])

                LOOK = 1
                pq = [issue_s(n) for n in range(min(LOOK, nu))]
                for n in range(nu):
                    if n + LOOK < nu:
                        pq.append(issue_s(n + LOOK))
                    issue_pv(n, pq[n])
                a = []
                for c in range(2):
                    lb = misc_ring.next()
                    mm(lb[:, :], o128[:, :], pacc[0][:, c, :], True, False, [o128.b, pacc[0].b], [lb.b])
                    mm(lb[:, :], o128[:, :], pacc[1][:, c, :], False, True, [o128.b, pacc[1].b], [lb.b])
                    r = wk_ring.next()
                    S.add("dve", lambda e, r=r, lb=lb: e.reciprocal(r[:, :], lb[:, :]), reads=[lb.b], writes=[r.b])
                    stt("dve", r[:, :], Ob[c][:, :], 1.0 / 128.0, r[:, :], MUL, MUL, [Ob[c].b, r.b], [r.b])
                    a.append(r)
                dm = wk_ring.next()
                stt("dve", dm[:, :], a[1][:, :], neglam[:, 0:1], a[0][:, :], MUL, ADD,
                    [a[1].b, neglam.b, a[0].b], [dm.b])
                sq = wk_ring.next()
                act(sq[:, :], dm[:, :], AF.Square, [dm.b], [sq.b])
                b2 = misc_ring.next()
                mm(b2[:, :], o128[:, :], sq[:, :], True, True, [o128.b, sq.b], [b2.b])
                ts("dve", sq[:, :], b2[:, :], EPS, None, ADD, None, [b2.b], [sq.b])
                act(sq[:, :], sq[:, :], AF.Ln, [sq.b], [sq.b])
                act(sq[:, :], sq[:, :], AF.Exp, [sq.b], [sq.b], scale=-0.5)
                stt("dve", dm[:, :], dm[:, :], sub08[:, 0:1], sq[:, :], MUL, MUL, [dm.b, sub08.b, sq.b], [dm.b])
                bz = misc_ring.next()
                for kt in range(16):
                    mm(bz[:, :], wz[:, kt, :], h[:, kt, :], kt == 0, kt == 15, [wz.b, h.b], [bz.b])
                sz = wk_ring.next()
                act(sz[:, :], bz[:, :], AF.Silu, [bz.b], [sz.b])
                tt("dve", yatt[:, hd, :], dm[:, :], sz[:, :], MUL, [dm.b, sz.b], [yatt.b])

        def merge(J, ring):
            h = hT["own"]
            jobs = []
            st = {}

            def epi_gate(n, key):
                def f(g):
                    sg = wk_ring.next()
                    act(sg[:, :], g[:, :], AF.Sigmoid, [g.b], [sg.b])
                    st[(n, key)] = sg
                return f

            def epi_p(n, key):
                def f(pb):
                    sg = st[(n, key)]
                    tt("dve", sg[:, :], sg[:, :], pb[:, :], MUL, [sg.b, pb.b], [sg.b])
                    if key == "s":
                        sa = st[(n, "a")]
                        tt("dve", mT[:, n, :], sa[:, :], sg[:, :], ADD, [sa.b, sg.b], [mT.b])
                return f

            for n in range(16):
                jobs.append((win_s, 48 + n, h, 16, epi_gate(n, "a")))
                jobs.append((wpa_s, n, yatt, 8, epi_p(n, "a")))
                jobs.append((win_s, 64 + n, h, 16, epi_gate(n, "s")))
                jobs.append((wps_s, n, yssm, 8, epi_p(n, "s")))
            run_jobs(jobs, ring)

        def out_stage(J, ring):
            for grp in range(4):
                dma(wg[:, :, :], wout_s[grp], wg.sem, [wout_s.b], [wg.b])
                for tb in range(4):
                    r0_ = J * NT + tb * 128
                    xr = xres_ring.next()
                    dma(xr[:, :], x_own[r0_:r0_ + 128, grp * 512:(grp + 1) * 512], xr.sem, [], [xr.b])
                    bk = ring.next()
                    for kt in range(16):
                        mm(bk[:, :], mT[:, kt, tb * 128:(tb + 1) * 128], wg[:, kt, :], kt == 0, kt == 15,
                           [mT.b, wg.b], [bk.b])
                    ot = ot_ring.next()
                    tt("dve", ot[:, :], bk[:, :], xr[:, :], ADD, [bk.b, xr.b], [ot.b])
                    dma(out_own[r0_:r0_ + 128, grp * 512:(grp + 1) * 512], ot[:, :], ot.sem, [ot.b], [])

        for J in range(NJ):
            if stop >= 10:
                rms_to_hT("oth", J, gen_ring)
            if stop >= 11:
                stage_kvu("oth", J, gen_ring)
            if stop >= 13:
                rms_to_hT("own", J, gen_ring)
            if stop >= 14:
                stage_kvu("own", J, gen_ring)
            if stop >= 15:
                ssm_smat(J, gen_ring)
                ssm_scan("oth", J)
                ssm_scan("own", J)
            if stop >= 18:
                attention(J)
            if stop >= 16:
                ssm_out(J, gen_ring)
            if stop >= 17:
                glu_stage(J, gen_ring)
            if stop >= 19:
                merge(J, gen_ring)
            if stop >= 20:
                out_stage(J, gen_ring)

        S.emit(nc, es)
    return nc, S


def _bf(a):
    return np.ascontiguousarray(a).astype(ml_dtypes.bfloat16)


def prep_shared(inp):
    f = np.float32
    sh = {}
    sh["w_in"] = np.ascontiguousarray(inp["w_in"][0], dtype=f)
    sh["w_glu"] = np.ascontiguousarray(inp["w_glu"][0], dtype=f)
    sh["w_pa"] = np.ascontiguousarray(inp["w_proj_att"][0], dtype=f)
    sh["w_ps"] = np.ascontiguousarray(inp["w_proj_ssm"][0], dtype=f)
    sh["w_out"] = np.ascontiguousarray(inp["w_out"][0], dtype=f)
    sh["gain_col"] = np.ascontiguousarray(inp["ln_gain"][0].reshape(16, 128).T, dtype=f)
    sh["qg"] = np.ascontiguousarray(np.tile(inp["q_norm_gain"][0], 2).reshape(128, 1), dtype=f)
    sh["kg"] = np.ascontiguousarray(np.tile(inp["k_norm_gain"][0], 2).reshape(128, 1), dtype=f)
    lv = np.stack([inp["lambda_q1"][0], inp["lambda_k1"][0], inp["lambda_q2"][0], inp["lambda_k2"][0]])
    sh["lamv"] = np.ascontiguousarray(np.broadcast_to(lv[None], (128, 4, 64)), dtype=f)
    sh["subln"] = np.ascontiguousarray(inp["subln_gain"][0].reshape(128, 1), dtype=f)
    sh["dcol"] = np.ascontiguousarray(inp["ssm_d"][0].reshape(8, 128).T, dtype=f)
    sh["bglu"] = np.ascontiguousarray(inp["b_glu"][0].reshape(8, 128).T, dtype=f)
    lre = np.asarray(inp["ssm_lambda_re"][0], dtype=f)
    lim = np.asarray(inp["ssm_lambda_im"][0], dtype=f)
    ldt = np.asarray(inp["ssm_log_dt"][0], dtype=f)
    bre = np.asarray(inp["ssm_b_re"][0], dtype=f)
    bim = np.asarray(inp["ssm_b_im"][0], dtype=f)
    cre = np.asarray(inp["ssm_c_re"][0], dtype=f)
    cim = np.asarray(inp["ssm_c_im"][0], dtype=f)

    def nat(a):
        return a.reshape(NPAIR, 2, 64).transpose(1, 2, 0).reshape(128, NPAIR)

    ldt_gp = np.broadcast_to(ldt[:, None], (64, 64))
    sh["pn"] = np.ascontiguousarray(np.stack([nat(lre), nat(lim), nat(ldt_gp)], axis=1), dtype=f)

    def natpad(a_gpc):
        o = np.zeros((2, 64, NPAIR, 2, 16), f)
        a = a_gpc.reshape(NPAIR, 2, 64, 16)
        for m in range(2):
            o[m, :, :, m, :] = a[:, m].transpose(1, 0, 2)
        return o.reshape(128, NPAIR, 32)

    sh["bn"] = np.ascontiguousarray(np.stack([natpad(bre), natpad(bim)], axis=1), dtype=f)
    sh["cn"] = np.ascontiguousarray(np.stack([natpad(cre.transpose(0, 2, 1)), natpad(cim.transpose(0, 2, 1))], axis=1), dtype=f)

    def trn(a_gp):
        a = a_gp.reshape(8, 4, 2, 64).reshape(8, 4, 128)
        return np.broadcast_to(a[:, :, None, :], (8, 4, 32, 128)).reshape(8, 128, 128)

    sh["pt"] = np.ascontiguousarray(np.stack([trn(lre), trn(lim), trn(ldt_gp)], axis=2), dtype=f)

    def trnpad(a_gpc):
        o = np.zeros((8, 4, 2, 16, 2, 64), f)
        a = a_gpc.reshape(8, 4, 2, 64, 16)
        for m in range(2):
            o[:, :, m, :, m, :] = a[:, :, m].transpose(0, 1, 3, 2)
        return o.reshape(8, 128, 128)

    sh["bt"] = np.ascontiguousarray(np.stack([trnpad(bre), trnpad(bim)], axis=2), dtype=f)
    sh["ident"] = _bf(np.eye(128, dtype=f))
    sh["onesb"] = _bf(np.ones((128, 128), f))
    obd = np.zeros((128, 128), f)
    obd[:64, :64] = 1.0 / 64
    obd[64:, 64:] = 1.0 / 64
    sh["onesbd"] = obd
    sh["ones128"] = np.full((128, 128), 1.0 / 128, f)
    sh["kvals"] = np.ascontiguousarray(np.broadcast_to(np.arange(TC + 1, dtype=f)[None], (128, TC + 1)))
    kp = np.arange(128)[:, None, None]
    r = np.arange(4)[None, :, None]
    qq = np.arange(NT)[None, None, :]
    sh["negmask"] = _bf(np.where(128 * r + kp <= qq, 0.0, NEG).astype(f))
    return sh


def prep_core(x, b, pi, NJ):
    xs = np.asarray(x[b]).reshape(16, NT, D)
    own = np.ascontiguousarray(xs[pi::2][:NJ].reshape(NJ * NT, D), dtype=np.float32)
    oth = np.ascontiguousarray(xs[1 - pi::2][:NJ].reshape(NJ * NT, D), dtype=np.float32)
    pc = np.zeros((128, 3), np.float32)
    pc[:, 0] = pi
    pc[:, 1] = 1 - pi
    pc[:, 2] = 0.0 if pi == 1 else NEG
    return {"x_own": own, "x_oth": oth, "picol": pc}


_CACHE = {}


def run(inputs, NJ=NJ_FULL, trace=False, stop=99):
    inp = {k: np.asarray(v) for k, v in inputs.items()}
    if (NJ, stop) not in _CACHE:
        _CACHE[(NJ, stop)] = build(NJ, stop)[0]
    nc = _CACHE[(NJ, stop)]
    sh = prep_shared(inp)
    in_maps = []
    for c in range(8):
        m = dict(sh)
        m.update(prep_core(inp["x"], c // 2, c % 2, NJ))
        in_maps.append(m)
    res = run_bass_kernel_spmd(nc, in_maps, core_ids=list(range(8)), trace=trace)
    out = np.zeros((4, 16, NT, D), np.float32)
    for c in range(8):
        b, pi = c // 2, c % 2
        o = np.asarray(res.results[c]["out_own"]).reshape(NJ, NT, D)
        out[b, pi::2][:NJ] = o
    return out.reshape(4, 16 * NT, D), res


def kernel(**inputs):
    out, _ = run(inputs)
    return out
```

```python
import numpy as np
import ml_dtypes
import contextlib
import concourse.bass as bass
import concourse.mybir as mybir
from concourse.bass_utils import run_bass_kernel_spmd

F32 = mybir.dt.float32
BF16 = mybir.dt.bfloat16
AF = mybir.ActivationFunctionType
ALU = mybir.AluOpType
AX = mybir.AxisListType

D = 2048
NT = 512
NJ_FULL = 8
TC = 8
NCH = NT // TC
NPAIR = 32
MAGIC = 12582912.0
TWO_PI = 6.283185307179586
NEG = -30000.0
EPS = 1e-6
SAME_ENG_SYNC = True
SPLIT_ROLES = 1
ARENA_WORDS = 53000


class Buf:
    __slots__ = ("name", "ws", "rs", "multi", "excl")

    def __init__(self, name, multi=False):
        self.name = name
        self.ws = []
        self.rs = []
        self.multi = multi
        self.excl = False


class DSem:
    def __init__(self, name, bulk=False):
        self.name = name
        self.count = 0
        self.last = None
        self.bulk = bulk
        self.handle = None


class Op:
    __slots__ = ("eng", "fn", "deps", "dsem", "ev", "signal", "idx")


ENGS = ("pe", "act", "dve", "pool", "sp")


class Sched:
    def __init__(self):
        self.by_eng = {e: [] for e in ENGS}
        self.dsems = []
        self.pending = {e: None for e in ENGS}

    def dsem(self, name, bulk=False):
        s = DSem(name, bulk)
        self.dsems.append(s)
        return s

    def barrier(self):
        lasts = [self.by_eng[e][-1] for e in ENGS if e != "sp" and self.by_eng[e]]
        for s in self.dsems:
            if s.last is not None:
                lasts.append(s.last)
        for e in ENGS:
            self.pending[e] = list(lasts)

    def add(self, eng, fn, reads=(), writes=(), dsem=None):
        op = Op()
        op.eng = eng
        op.fn = fn
        op.dsem = dsem
        op.signal = False
        op.ev = None
        deps = set()
        ex = [b for b in reads if b.excl]
        if ex:
            reads = [b for b in reads if not b.excl]
            writes = list(writes) + [b for b in ex if b not in writes]
        if self.pending[eng] is not None:
            deps.update(self.pending[eng])
            self.pending[eng] = None
        for b in reads:
            deps.update(b.ws)
        for b in writes:
            if not b.multi:
                deps.update(b.ws)
            deps.update(b.rs)
        for b in reads:
            b.rs.append(op)
        for b in writes:
            if b.multi:
                b.ws.append(op)
            else:
                b.ws = [op]
            b.rs = []
        if dsem is not None:
            if (not dsem.bulk) and dsem.last is not None:
                deps.add(dsem.last)
            dsem.last = op
            dsem.count += 16
            op.ev = dsem.count
        deps.discard(op)
        fd = []
        for d in deps:
            if d.eng == eng and d.dsem is None:
                if eng == "pe" or not SAME_ENG_SYNC:
                    continue
            fd.append(d)
            d.signal = True
        op.deps = fd
        self.by_eng[eng].append(op)
        return op

    def emit(self, nc, es):
        esem = {}
        for e in ("pe", "act", "dve", "pool"):
            esem[e] = es.enter_context(nc.semaphore("sem_" + e))
        for s in self.dsems:
            s.handle = es.enter_context(nc.semaphore("d_" + s.name))
        for e in ("pe", "act", "dve", "pool"):
            c = 0
            for op in self.by_eng[e]:
                if op.signal:
                    c += 1
                    op.ev = c
        final_waits = [(s.handle, s.count) for s in self.dsems if s.count > 0]
        block = es.enter_context(nc.Block())
        engs = {"pe": block.tensor, "act": block.scalar, "dve": block.vector,
                "pool": block.gpsimd, "sp": block.sync}

        def make(ename):
            ops = self.by_eng[ename]

            def body(eng):
                waited = {}
                for op in ops:
                    need = {}
                    for d in op.deps:
                        if d.dsem is not None:
                            key = ("d", id(d.dsem))
                            sem = d.dsem.handle
                            val = d.dsem.count if d.dsem.bulk else d.ev
                        else:
                            key = ("e", d.eng)
                            sem = esem[d.eng]
                            val = d.ev
                        if key not in need or need[key][1] < val:
                            need[key] = (sem, val)
                    for key, (sem, val) in need.items():
                        if waited.get(key, 0) >= val:
                            continue
                        waited[key] = val
                        eng.wait_ge(sem, val)
                    ins = op.fn(eng)
                    if op.dsem is not None:
                        ins.then_inc(op.dsem.handle, 16)
                    elif op.signal:
                        ins.then_inc(esem[ename], 1)
                if ename == "sp":
                    for sem, val in final_waits:
                        eng.wait_ge(sem, val)
            return body

        for ename, dec in engs.items():
            dec(make(ename))


class T:
    def __init__(self, ap, name, multi=False, sem=None):
        self.ap = ap
        self.b = Buf(name, multi)
        self.sem = sem

    def __getitem__(self, k):
        return self.ap[k]


class Ring:
    def __init__(self, items):
        self.items = items
        self.i = 0

    def next(self):
        it = self.items[self.i % len(self.items)]
        self.i += 1
        return it


def build(NJ=NJ_FULL, stop=99):
    nc = bass.Bass("TRN2", target_bir_lowering=False)
    S = Sched()
    NTOK = NJ * NT
    NSLOT = NJ * 2 * NT

    def din(name, shape, dt=F32):
        return nc.dram_tensor(name, list(shape), dt, kind="ExternalInput").ap()

    x_own = din("x_own", [NTOK, D])
    x_oth = din("x_oth", [NTOK, D])
    w_in = din("w_in", [D, 10240])
    w_glu = din("w_glu", [1024, 1024])
    w_pa = din("w_pa", [1024, 2048])
    w_ps = din("w_ps", [1024, 2048])
    w_out = din("w_out", [D, D])
    gain_col = din("gain_col", [128, 16])
    qg_in = din("qg", [128, 1])
    kg_in = din("kg", [128, 1])
    lamv_in = din("lamv", [128, 4, 64])
    subln_in = din("subln", [128, 1])
    dcol_in = din("dcol", [128, 8])
    bglu_in = din("bglu", [128, 8])
    pn_in = din("pn", [128, 3, NPAIR])
    bn_in = din("bn", [128, 2, NPAIR, 32])
    cn_in = din("cn", [128, 2, NPAIR, 32])
    pt_in = din("pt", [8, 128, 3, 128])
    bt_in = din("bt", [8, 128, 2, 128])
    ident_in = din("ident", [128, 128], BF16)
    ones_in = din("onesb", [128, 128], BF16)
    obd_in = din("onesbd", [128, 128])
    o128_in = din("ones128", [128, 128])
    kv_in = din("kvals", [128, TC + 1])
    negm_in = din("negmask", [128, 4, NT], BF16)
    pic_in = din("picol", [128, 3])
    out_own = nc.dram_tensor("out_own", [NTOK, D], F32, kind="ExternalOutput").ap()

    def dscr(name, shape, dt=BF16):
        return T(nc.dram_tensor(name, list(shape), dt).ap(), name, multi=True)

    win_s = dscr("win_s", [80, 128, 16, 128])
    wglu_s = dscr("wglu_s", [8, 128, 8, 128])
    wpa_s = dscr("wpa_s", [16, 128, 8, 128])
    wps_s = dscr("wps_s", [16, 128, 8, 128])
    wout_s = dscr("wout_s", [4, 128, 16, 512])
    bd_s = dscr("bd_s", [8, 128, TC, 128])
    ws_s = dscr("ws_s", [8, 128, TC, 2, 128])
    vw_s = dscr("vw_s", [8, 128, 4, TC, 2, 32])
    kt_s = dscr("kt_s", [8, 128, NSLOT])
    v_s = dscr("v_s", [8, 128, NSLOT // 128, 128])

    es = contextlib.ExitStack()
    with es:
        arena_h = es.enter_context(nc.sbuf_tensor("arena", [128, ARENA_WORDS], F32))
        aoff = [0]

        def sb(name, shape, dt=F32, sem=False):
            n = 1
            for s_ in shape[1:]:
                n *= s_
            words = n if dt == F32 else (n + 1) // 2
            words += words & 1
            assert aoff[0] + words <= ARENA_WORDS, (name, aoff[0], words)
            ap = arena_h[:, aoff[0]:aoff[0] + words]
            aoff[0] += words
            if dt == BF16:
                ap = ap.bitcast(BF16)
            ap = ap[:, 0:n]
            if len(shape) > 2:
                names = ["a%d" % i for i in range(len(shape) - 1)]
                kw = {nm: s_ for nm, s_ in zip(names, shape[1:])}
                ap = ap.rearrange("p (%s) -> p %s" % (" ".join(names), " ".join(names)), **kw)
            return T(ap, name, sem=S.dsem(name) if sem else None)

        psum_all = es.enter_context(nc.psum_tensor("psall", [128, 4096], F32))
        banks = [T(psum_all[:, 512 * i:512 * (i + 1)], "pb%d" % i) for i in range(8)]
        for bk_ in banks:
            bk_.b.excl = True
        gen_ring = Ring(banks)
        misc_ring = Ring(banks[6:8])

        def mm(out, lhsT, rhs, start, stop, reads, writes):
            return S.add("pe", lambda e: e.matmul(out, lhsT, rhs, start=start, stop=stop),
                         reads=reads, writes=writes)

        def act(out, in_, func, reads, writes, bias=None, scale=None, accum_out=None):
            kw = {}
            if bias is not None:
                kw["bias"] = bias
            if scale is not None:
                kw["scale"] = scale
            if accum_out is not None:
                kw["accum_out"] = accum_out
            return S.add("act", lambda e: e.activation(out, in_, func, **kw), reads=reads, writes=writes)

        def tt(eng, out, in0, in1, op, reads, writes):
            return S.add(eng, lambda e: e.tensor_tensor(out, in0, in1, op), reads=reads, writes=writes)

        def ts(eng, out, in0, s1, s2, op0, op1, reads, writes):
            if s2 is None:
                return S.add(eng, lambda e: e.tensor_scalar(out, in0, s1, None, op0), reads=reads, writes=writes)
            return S.add(eng, lambda e: e.tensor_scalar(out, in0, s1, s2, op0, op1), reads=reads, writes=writes)

        def stt(eng, out, in0, scalar, in1, op0, op1, reads, writes):
            return S.add(eng, lambda e: e.scalar_tensor_tensor(out, in0, scalar, in1, op0, op1),
                         reads=reads, writes=writes)

        def cp(eng, out, in_, reads, writes):
            if eng == "act":
                return S.add("act", lambda e: e.copy(out, in_), reads=reads, writes=writes)
            return S.add(eng, lambda e: e.tensor_copy(out, in_), reads=reads, writes=writes)

        def dma(out, in_, dsem, reads, writes):
            return S.add("sp", lambda e: e.dma_start(out=out, in_=in_), reads=reads, writes=writes, dsem=dsem)

        MUL, ADD, SUB = ALU.mult, ALU.add, ALU.subtract

        bulk = S.dsem("bulk", bulk=True)

        def cload(name, src, shape, dt=F32):
            t = sb("c_" + name, shape, dt)
            dma(t.ap, src, bulk, [], [t.b])
            return t

        gcol = cload("gain", gain_col, [128, 16])
        qg = cload("qg", qg_in, [128, 1])
        kg = cload("kg", kg_in, [128, 1])
        subln = cload("subln", subln_in, [128, 1])
        dcol = cload("dcol", dcol_in, [128, 8])
        bglu = cload("bglu", bglu_in, [128, 8])
        ident = cload("ident", ident_in, [128, 128], BF16)
        onesb = cload("onesb", ones_in, [128, 128], BF16)
        obd = cload("obd", obd_in, [128, 128])
        o128 = cload("o128", o128_in, [128, 128])
        negm = cload("negm", negm_in, [128, 4, NT], BF16)
        picol = cload("picol", pic_in, [128, 3])
        neglam = sb("neglam", [128, 1])
        qg8 = sb("qg8", [128, 1])
        sub08 = sb("sub08", [128, 1])
        SH = [128, 2, NPAIR]
        CA = sb("CA", SH)
        CB = sb("CB", SH)
        A5 = sb("A5", SH)
        A5A = sb("A5A", SH)
        A5B = sb("A5B", SH)
        Est = sb("Est", SH)
        Fst = sb("Fst", SH)
        Xin = sb("Xin", SH)
        Soth = sb("Soth", SH)
        Zst = sb("Zst", SH)
        pt1 = sb("pt1", SH)
        pt2 = sb("pt2", SH)
        pt3 = sb("pt3", SH)
        persist_end = aoff[0]

        lamv = cload("lamv", lamv_in, [128, 4, 64])
        pn = cload("pn", pn_in, [128, 3, NPAIR])
        bn = cload("bn", bn_in, [128, 2, NPAIR, 32])
        cn = cload("cn", cn_in, [128, 2, NPAIR, 32])
        kvals = cload("kvals", kv_in, [128, TC + 1])
        xt0 = [sb("xt%d" % i, [128, D], sem=True) for i in range(2)]
        xn0 = [sb("xn%d" % i, [128, D], BF16, sem=True) for i in range(2)]
        r0 = Ring([0, 1])

        def cast_weight(src, K, N, dst, ncol, use_gain):
            for kt in range(K // 128):
                for c0 in range(0, N, D):
                    w = min(D, N - c0)
                    i = r0.next()
                    a, an = xt0[i], xn0[i]
                    dma(a[:, 0:w], src[kt * 128:(kt + 1) * 128, c0:c0 + w], a.sem, [], [a.b])
                    if use_gain:
                        ts("dve", an[:, 0:w], a[:, 0:w], gcol[:, kt:kt + 1], None, MUL, None, [a.b, gcol.b], [an.b])
                    else:
                        cp("act", an[:, 0:w], a[:, 0:w], [a.b], [an.b])
                    nb = w // ncol
                    dsl = dst[c0 // ncol:c0 // ncol + nb, :, kt, :].rearrange("c p n -> p c n")
                    dma(dsl, an[:, 0:w].rearrange("p (c n) -> p c n", n=ncol), an.sem, [an.b], [dst.b])

        if stop >= 1:
            cast_weight(w_in, D, 10240, win_s, 128, True)
            cast_weight(w_glu, 1024, 1024, wglu_s, 128, False)
            cast_weight(w_pa, 1024, 2048, wpa_s, 128, False)
            cast_weight(w_ps, 1024, 2048, wps_s, 128, False)
            cast_weight(w_out, D, D, wout_s, 512, False)

        real_add = S.add
        if stop < 2:
            S.add = lambda *a_, **k_: None
        lam_init = 0.8 - 0.6 * 1.0
        ltmp = sb("ltmp", [128, 2, 64])
        lsum = sb("lsum", [128, 2])
        tt("dve", ltmp[:, 0, :], lamv[:, 0, :], lamv[:, 1, :], MUL, [lamv.b], [ltmp.b])
        tt("dve", ltmp[:, 1, :], lamv[:, 2, :], lamv[:, 3, :], MUL, [lamv.b], [ltmp.b])
        S.add("dve", lambda e: e.reduce_sum(lsum[:, :], ltmp[:, :, :], AX.X), reads=[ltmp.b], writes=[lsum.b])
        act(lsum[:, :], lsum[:, :], AF.Exp, [lsum.b], [lsum.b])
        stt("dve", neglam[:, :], lsum[:, 1:2], -lam_init, lsum[:, 0:1], ADD, SUB, [lsum.b], [neglam.b])
        ts("dve", qg8[:, :], qg[:, :], 0.125, None, MUL, None, [qg.b], [qg8.b])
        ts("dve", sub08[:, :], subln[:, :], 1.0 - lam_init, None, MUL, None, [subln.b], [sub08.b])

        def trig_alloc(prefix, n):
            sh3 = [128, n, TC + 1]
            sh2 = [128, n]
            d = {"n": n}
            for nm in ("arg", "t1", "t2", "PR", "PI"):
                d[nm] = sb(prefix + nm, sh3)
            for nm in ("dt", "lrdt", "ang", "nr", "den", "u1", "cr", "ci"):
                d[nm] = sb(prefix + nm, sh2)
            return d

        def trig_tables(d, lre, lim, ldt, rd):
            n = d["n"]
            sh3 = [128, n, TC + 1]
            arg, t1, t2, PR, PI = d["arg"], d["t1"], d["t2"], d["PR"], d["PI"]
            dtt, lrdt, ang, nr, den, u1, cr, ci = (d[k] for k in ("dt", "lrdt", "ang", "nr", "den", "u1", "cr", "ci"))
            act(dtt[:, :], ldt, AF.Exp, rd, [dtt.b])
            tt("dve", lrdt[:, :], lre, dtt[:, :], MUL, rd + [dtt.b], [lrdt.b])
            tt("dve", ang[:, :], lim, dtt[:, :], MUL, rd + [dtt.b], [ang.b])
            kb = kvals[:, :].unsqueeze(1).to_broadcast(sh3)
            tt("dve", arg[:, :, :], lrdt[:, :].unsqueeze(2).to_broadcast(sh3), kb, MUL, [lrdt.b, kvals.b], [arg.b])
            act(PR[:, :, :], arg[:, :, :], AF.Exp, [arg.b], [PR.b])
            tt("dve", arg[:, :, :], ang[:, :].unsqueeze(2).to_broadcast(sh3), kb, MUL, [ang.b, kvals.b], [arg.b])

            def reduce_sin(dst, shift):
                ts("dve", t2[:, :, :], arg[:, :, :], shift, None, ADD, None, [arg.b], [t2.b])
                ts("dve", t1[:, :, :], t2[:, :, :], 1.0 / TWO_PI, MAGIC, MUL, ADD, [t2.b], [t1.b])
                ts("dve", t1[:, :, :], t1[:, :, :], MAGIC, None, SUB, None, [t1.b], [t1.b])
                stt("dve", t1[:, :, :], t1[:, :, :], -TWO_PI, t2[:, :, :], MUL, ADD, [t1.b, t2.b], [t1.b])
                ts("dve", t1[:, :, :], t1[:, :, :], 3.1415925, -3.1415925, ALU.min, ALU.max, [t1.b], [t1.b])
                act(dst[:, :, :], t1[:, :, :], AF.Sin, [t1.b], [dst.b])

            reduce_sin(PI, 0.0)
            tt("dve", PI[:, :, :], PI[:, :, :], PR[:, :, :], MUL, [PI.b, PR.b], [PI.b])
            reduce_sin(t1, 1.5707963267948966)
            tt("dve", PR[:, :, :], PR[:, :, :], t1[:, :, :], MUL, [PR.b, t1.b], [PR.b])
            ts("dve", nr[:, :], PR[:, :, 1], -1.0, None, ADD, None, [PR.b], [nr.b])
            tt("dve", den[:, :], lre, lre, MUL, rd, [den.b])
            tt("dve", u1[:, :], lim, lim, MUL, rd, [u1.b])
            tt("dve", den[:, :], den[:, :], u1[:, :], ADD, [den.b, u1.b], [den.b])
            S.add("dve", lambda e: e.reciprocal(den[:, :], den[:, :]), reads=[den.b], writes=[den.b])
            tt("dve", cr[:, :], nr[:, :], lre, MUL, rd + [nr.b], [cr.b])
            tt("dve", u1[:, :], PI[:, :, 1], lim, MUL, rd + [PI.b], [u1.b])
            tt("dve", cr[:, :], cr[:, :], u1[:, :], ADD, [cr.b, u1.b], [cr.b])
            tt("dve", cr[:, :], cr[:, :], den[:, :], MUL, [cr.b, den.b], [cr.b])
            tt("dve", ci[:, :], PI[:, :, 1], lre, MUL, rd + [PI.b], [ci.b])
            tt("dve", u1[:, :], nr[:, :], lim, MUL, rd + [nr.b], [u1.b])
            tt("dve", ci[:, :], ci[:, :], u1[:, :], SUB, [ci.b, u1.b], [ci.b])
            tt("dve", ci[:, :], ci[:, :], den[:, :], MUL, [ci.b, den.b], [ci.b])

        dn = trig_alloc("n_", NPAIR)
        trig_tables(dn, pn[:, 0, :], pn[:, 1, :], pn[:, 2, :], [pn.b])
        PRn, PIn, crn, cin_ = dn["PR"], dn["PI"], dn["cr"], dn["ci"]

        def pack_coef(A, CAt, CBt):
            cp("dve", CAt[:, 0, :], A[:, 0, :], [A.b], [CAt.b])
            cp("dve", CAt[:, 1, :], A[:, 0, :], [A.b], [CAt.b])
            ts("dve", CBt[:, 0, :], A[:, 1, :], -1.0, None, MUL, None, [A.b], [CBt.b])
            cp("dve", CBt[:, 1, :], A[:, 1, :], [A.b], [CBt.b])

        cp("dve", A5[:, 0, :], PRn[:, :, TC], [PRn.b], [A5.b])
        cp("dve", A5[:, 1, :], PIn[:, :, TC], [PIn.b], [A5.b])
        pack_coef(A5, CA, CB)
        for _ in range(6):
            tt("dve", pt1[:, :, :], A5[:, :, :], A5[:, :, :], MUL, [A5.b], [pt1.b])
            tt("dve", pt2[:, 0, :], A5[:, 0, :], A5[:, 1, :], MUL, [A5.b], [pt2.b])
            tt("dve", A5[:, 0, :], pt1[:, 0, :], pt1[:, 1, :], SUB, [pt1.b], [A5.b])
            ts("dve", A5[:, 1, :], pt2[:, 0, :], 2.0, None, MUL, None, [pt2.b], [A5.b])
        pack_coef(A5, A5A, A5B)
        S.add("pool", lambda e: e.memset(Est[:, :, :], 0.0), writes=[Est.b])
        S.add("pool", lambda e: e.memset(Zst[:, :, :], 0.0), writes=[Zst.b])

        sh4 = [128, NPAIR, 32]
        bbn = sb("bbn", [128, 2, NPAIR, 32])
        w1 = sb("w1", sh4)
        g2 = sb("g2", sh4)
        gre = sb("gre", [128, NPAIR, 64])
        gim = sb("gim", [128, NPAIR, 64])
        S.add("pool", lambda e: e.memset(gre[:, :, :], 0.0), writes=[gre.b])
        S.add("pool", lambda e: e.memset(gim[:, :, :], 0.0), writes=[gim.b])
        ncim = sb("ncim", sh4)
        crb = crn[:, :].unsqueeze(2).to_broadcast(sh4)
        cib = cin_[:, :].unsqueeze(2).to_broadcast(sh4)
        tt("dve", bbn[:, 0, :, :], bn[:, 0, :, :], crb, MUL, [bn.b, crn.b], [bbn.b])
        tt("dve", w1[:, :, :], bn[:, 1, :, :], cib, MUL, [bn.b, cin_.b], [w1.b])
        tt("dve", bbn[:, 0, :, :], bbn[:, 0, :, :], w1[:, :, :], SUB, [bbn.b, w1.b], [bbn.b])
        tt("dve", bbn[:, 1, :, :], bn[:, 1, :, :], crb, MUL, [bn.b, crn.b], [bbn.b])
        tt("dve", w1[:, :, :], bn[:, 0, :, :], cib, MUL, [bn.b, cin_.b], [w1.b])
        tt("dve", bbn[:, 1, :, :], bbn[:, 1, :, :], w1[:, :, :], ADD, [bbn.b, w1.b], [bbn.b])
        ts("dve", ncim[:, :, :], cn[:, 1, :, :], -1.0, None, MUL, None, [cn.b], [ncim.b])

        vw = sb("vw", [128, NPAIR, TC, 2, 32], BF16, sem=True)
        for i in range(TC):
            prb = PRn[:, :, i + 1].unsqueeze(2).to_broadcast(sh4)
            pib = PIn[:, :, i + 1].unsqueeze(2).to_broadcast(sh4)
            tt("dve", w1[:, :, :], cn[:, 0, :, :], prb, MUL, [cn.b, PRn.b], [w1.b])
            tt("dve", g2[:, :, :], ncim[:, :, :], pib, MUL, [ncim.b, PIn.b], [g2.b])
            tt("dve", vw[:, :, i, 0, :], w1[:, :, :], g2[:, :, :], ADD, [w1.b, g2.b], [vw.b])
            tt("dve", w1[:, :, :], cn[:, 0, :, :], pib, MUL, [cn.b, PIn.b], [w1.b])
            tt("dve", g2[:, :, :], ncim[:, :, :], prb, MUL, [ncim.b, PRn.b], [g2.b])
            tt("dve", vw[:, :, i, 1, :], g2[:, :, :], w1[:, :, :], SUB, [w1.b, g2.b], [vw.b])
        for tau in range(8):
            dma(vw_s[tau], vw[:, 4 * tau:4 * tau + 4, :, :, :], vw.sem, [vw.b], [vw_s.b])

        bd = sb("bd", [128, 8, TC, 128], BF16, sem=True)
        S.add("pool", lambda e: e.memset(bd[:, :, :, :], 0.0), writes=[bd.b])
        for l in range(TC):
            prb = PRn[:, :, l].unsqueeze(2).to_broadcast(sh4)
            pib = PIn[:, :, l].unsqueeze(2).to_broadcast(sh4)
            tt("dve", gre[:, :, 32:64], bbn[:, 0, :, :], prb, MUL, [bbn.b, PRn.b], [gre.b])
            tt("dve", g2[:, :, :], bbn[:, 1, :, :], pib, MUL, [bbn.b, PIn.b], [g2.b])
            tt("dve", gre[:, :, 32:64], gre[:, :, 32:64], g2[:, :, :], SUB, [gre.b, g2.b], [gre.b])
            tt("dve", gim[:, :, 32:64], bbn[:, 0, :, :], pib, MUL, [bbn.b, PIn.b], [gim.b])
            tt("dve", g2[:, :, :], bbn[:, 1, :, :], prb, MUL, [bbn.b, PRn.b], [g2.b])
            tt("dve", gim[:, :, 32:64], gim[:, :, 32:64], g2[:, :, :], ADD, [gim.b, g2.b], [gim.b])
            for tau in range(8):
                bk = gen_ring.next()
                for q in range(4):
                    P = 4 * tau + q
                    if q < 3:
                        o = bk[32 * q:32 * q + 32, 32 * q:32 * q + 32]
                        c0_ = 32
                    else:
                        o = bk[64:128, 96:128]
                        c0_ = 0
                    mm(o, gre[:, P, c0_:64], cn[:, 0, P, :], True, False, [gre.b, cn.b], [bk.b])
                    mm(o, gim[:, P, c0_:64], ncim[:, P, :], False, True, [gim.b, ncim.b], [bk.b])
                for q in range(3):
                    cp("dve", bd[32 * q:32 * q + 32, tau, l, 32 * q:32 * q + 32],
                       bk[32 * q:32 * q + 32, 32 * q:32 * q + 32], [bk.b], [bd.b])
                cp("dve", bd[64:128, tau, l, 96:128], bk[64:128, 96:128], [bk.b], [bd.b])
        for tau in range(8):
            dma(bd_s[tau], bd[:, tau, :, :], bd.sem, [bd.b], [bd_s.b])

        ptl = sb("ptl", [128, 3, 128], sem=True)
        btl = sb("btl", [128, 2, 128], sem=True)
        wsw0 = [sb("wsw0_%d" % i, [128, TC, 2, 128], BF16, sem=True) for i in range(2)]
        bbt = sb("bbt", [128, 2, 128])
        w2 = sb("w2", [128, 128])
        w3 = sb("w3", [128, 128])
        dt_ = trig_alloc("t_", 128)
        for tau in range(8):
            dma(ptl[:, :, :], pt_in[tau], ptl.sem, [], [ptl.b])
            dma(btl[:, :, :], bt_in[tau], btl.sem, [], [btl.b])
            trig_tables(dt_, ptl[:, 0, :], ptl[:, 1, :], ptl[:, 2, :], [ptl.b])
            PRt, PIt, crt, cit = dt_["PR"], dt_["PI"], dt_["cr"], dt_["ci"]
            tt("dve", bbt[:, 0, :], btl[:, 0, :], crt[:, :], MUL, [btl.b, crt.b], [bbt.b])
            tt("dve", w2[:, :], btl[:, 1, :], cit[:, :], MUL, [btl.b, cit.b], [w2.b])
            tt("dve", bbt[:, 0, :], bbt[:, 0, :], w2[:, :], SUB, [bbt.b, w2.b], [bbt.b])
            tt("dve", bbt[:, 1, :], btl[:, 1, :], crt[:, :], MUL, [btl.b, crt.b], [bbt.b])
            tt("dve", w2[:, :], btl[:, 0, :], cit[:, :], MUL, [btl.b, cit.b], [w2.b])
            tt("dve", bbt[:, 1, :], bbt[:, 1, :], w2[:, :], ADD, [bbt.b, w2.b], [bbt.b])
            ws = wsw0[tau % 2]
            for j in range(TC):
                k = TC - 1 - j
                tt("dve", w2[:, :], bbt[:, 0, :], PRt[:, :, k], MUL, [bbt.b, PRt.b], [w2.b])
                tt("dve", w3[:, :], bbt[:, 1, :], PIt[:, :, k], MUL, [bbt.b, PIt.b], [w3.b])
                tt("dve", ws[:, j, 0, :], w2[:, :], w3[:, :], SUB, [w2.b, w3.b], [ws.b])
                tt("dve", w2[:, :], bbt[:, 0, :], PIt[:, :, k], MUL, [bbt.b, PIt.b], [w2.b])
                tt("dve", w3[:, :], bbt[:, 1, :], PRt[:, :, k], MUL, [bbt.b, PRt.b], [w3.b])
                tt("dve", ws[:, j, 1, :], w2[:, :], w3[:, :], ADD, [w2.b, w3.b], [ws.b])
            dma(ws_s[tau], ws[:, :, :, :], ws.sem, [ws.b], [ws_s.b])

        S.add = real_add
        S.barrier()
        aoff[0] = persist_end
        hT = {"own": sb("hT_own", [128, 16, NT], BF16), "oth": sb("hT_oth", [128, 16, NT], BF16)}
        mT = hT["oth"]
        uTc = sb("uTc", [128, 8, 2, NT], BF16)
        uT = {"oth": T(uTc[:, :, 0, :], "uT_oth"), "own": T(uTc[:, :, 1, :], "uT_own")}
        uT["oth"].b = uTc.b
        uT["own"].b = uTc.b
        gT = uT["oth"]
        ybuf = sb("ybuf", [128, 4096])
        yssm = T(ybuf[:, 0:2048].bitcast(BF16).rearrange("p (a n) -> p a n", n=NT), "yssm")
        yatt = T(ybuf[:, 2048:4096].bitcast(BF16).rearrange("p (a n) -> p a n", n=NT), "yatt")
        Sown = T(hT["oth"].ap.rearrange("p a n -> p (a n)").bitcast(F32)
                 .rearrange("p (c r q) -> p c r q", r=2, q=NPAIR), "Sown")
        Sown.b = hT["oth"].b
        wt_ring = Ring([sb("wt%d" % i, [128, 16, 128], BF16, sem=True) for i in range(3)])
        wg = sb("wg", [128, 16, 512], BF16, sem=True)
        ktp_ring = Ring([sb("ktp%d" % i, [128, 1024], BF16, sem=True) for i in range(4)])
        vp_ring = Ring([sb("vp%d" % i, [128, 8, 128], BF16, sem=True) for i in range(4)])
        p_ring = Ring([sb("P%d" % i, [128, 2, NT], BF16) for i in range(3)])
        pacc = [sb("pacc%d" % i, [128, 2, NT]) for i in range(2)]
        qT = sb("qT", [128, NT], BF16)
        Sbuf = sb("Sbuf", [128, NCH, 2, NPAIR])
        Xh = sb("Xh", [128, 2, NPAIR, NCH], BF16)
        bdw_ring = Ring([sb("bdw%d" % i, [128, TC, 128], BF16, sem=True) for i in range(1)])
        wsw_ring = Ring([sb("wsw%d" % i, [128, TC, 2, 128], BF16, sem=True) for i in range(1)])
        vww_ring = Ring([sb("vww%d" % i, [128, 4, TC, 2, 32], BF16, sem=True) for i in range(1)])
        wsw3 = sb("wsw3", [128, TC, 2, 128], BF16, sem=True)
        vw3 = sb("vw3", [128, TC, 2, 64], BF16, sem=True)
        S.add("pool", lambda e: e.memset(wsw3[64:96, :, :, :], 0.0), writes=[wsw3.b])
        S.add("pool", lambda e: e.memset(vw3[:, :, :, 0:32], 0.0), writes=[vw3.b])
        vw2 = sb("vw2", [128, TC, 2, 64], BF16, sem=True)
        S.add("pool", lambda e: e.memset(vw2[:, :, :, 32:64], 0.0), writes=[vw2.b])
        xt = sb("xt", [128, D], sem=True)
        xn = sb("xn", [128, D], BF16)
        wk_ring = Ring([sb("wk%d" % i, [128, NT]) for i in range(4)])
        xres_ring = Ring([sb("xres%d" % i, [128, NT], sem=True) for i in range(2)])
        ot_ring = Ring([sb("ot%d" % i, [128, NT], sem=True) for i in range(2)])
        ktile_ring = Ring([sb("ktile%d" % i, [128, NT], BF16, sem=True) for i in range(2)])
        vtile_ring = Ring([sb("vtile%d" % i, [128, NT], BF16, sem=True) for i in range(2)])
        ssq = sb("ssq", [128, 1])
        rstd = sb("rstd", [128, 1])

        evac_rr = [0]

        def evac_eng():
            evac_rr[0] += 1
            return "dve" if evac_rr[0] % 2 else "act"

        def load_wt(src, ct):
            w = wt_ring.next()
            nk = src.ap.shape[2]
            dma(w[:, 0:nk, :], src[ct], w.sem, [src.b], [w.b])
            return w

        def proj_fm(src, ct, hTt, ring, nkt=16):
            w = load_wt(src, ct)
            bk = ring.next()
            for kt in range(nkt):
                mm(bk[:, :], w[:, kt, :], hTt[:, kt, :], kt == 0, kt == nkt - 1, [w.b, hTt.b], [bk.b])
            return bk

        def run_jobs(jobs, ring):
            slots = {}

            def load(i):
                if i < len(jobs) and i not in slots:
                    slots[i] = load_wt(jobs[i][0], jobs[i][1])

            load(0)
            load(1)
            for i, (src, ct, rhsT, nkt, epi) in enumerate(jobs):
                load(i + 2)
                w = slots.pop(i)
                bk = ring.next()
                for kt in range(nkt):
                    mm(bk[:, :], w[:, kt, :], rhsT[:, kt, :], kt == 0, kt == nkt - 1, [w.b, rhsT.b], [bk.b])
                epi(bk)

        def rms_to_hT(role, J, ring):
            xsrc = x_own if role == "own" else x_oth
            h = hT[role]
            for tb in range(4):
                r0_ = J * NT + tb * 128
                dma(xt[:, :], xsrc[r0_:r0_ + 128, :], xt.sem, [], [xt.b])
                act(xn[:, :], xt[:, :], AF.Square, [xt.b], [xn.b, ssq.b], accum_out=ssq[:, :])
                ts("dve", rstd[:, :], ssq[:, :], 1.0 / D, EPS, MUL, ADD, [ssq.b], [rstd.b])
                act(rstd[:, :], rstd[:, :], AF.Sqrt, [rstd.b], [rstd.b])
                S.add("dve", lambda e: e.reciprocal(rstd[:, :], rstd[:, :]), reads=[rstd.b], writes=[rstd.b])
                ts("dve", xn[:, :], xt[:, :], rstd[:, 0:1], None, MUL, None, [xt.b, rstd.b], [xn.b])
                for half in range(2):
                    bk = ring.next()
                    bkb = bk.ap.bitcast(BF16)
                    for k in range(8):
                        kt = half * 8 + k
                        o_, i_ = bkb[:, k * 128:(k + 1) * 128], xn[:, kt * 128:(kt + 1) * 128]
                        S.add("pe", lambda e, o_=o_, i_=i_: e.transpose(o_, i_, ident[:, :]),
                              reads=[xn.b, ident.b], writes=[bk.b])
                    cp(evac_eng(), h[:, half * 8:(half + 1) * 8, tb * 128:(tb + 1) * 128],
                       bkb.rearrange("p (k n) -> p k n", n=128), [bk.b], [h.b])

        def qknorm(bk, gcol_t, dst_ap, dst_b, ring):
            sq = wk_ring.next()
            act(sq[:, :], bk[:, :], AF.Square, [bk.b], [sq.b])
            b2 = ring.next()
            mm(b2[:, :], obd[:, :], sq[:, :], True, True, [obd.b, sq.b], [b2.b])
            rs = wk_ring.next()
            ts("dve", rs[:, :], b2[:, :], EPS, None, ADD, None, [b2.b], [rs.b])
            act(rs[:, :], rs[:, :], AF.Ln, [rs.b], [rs.b])
            act(rs[:, :], rs[:, :], AF.Exp, [rs.b], [rs.b], scale=-0.5)
            stt("dve", dst_ap, bk[:, :], gcol_t[:, 0:1], rs[:, :], MUL, MUL, [bk.b, gcol_t.b, rs.b], [dst_b])

        def stage_kvu(role, J, ring):
            ridx = 0 if role == "own" else 1
            slot0 = J * 2 * NT + ridx * NT
            h = hT[role]
            jobs = []

            def epi_k(hd):
                def f(bk):
                    kt_ = ktile_ring.next()
                    qknorm(bk, kg, kt_[:, :], kt_.b, ring)
                    dma(kt_s[hd, :, slot0:slot0 + NT], kt_[:, :], kt_.sem, [kt_.b], [kt_s.b])
                return f

            def epi_u(tau):
                def f(bk):
                    cp(evac_eng(), uT[role][:, tau, :], bk[:, :], [bk.b], [uT[role].b])
                return f

            for hd in range(8):
                jobs.append((win_s, 8 + hd, h, 16, epi_k(hd)))
            for tau in range(8):
                jobs.append((win_s, 32 + tau, h, 16, epi_u(tau)))
            run_jobs(jobs, ring)
            for grp in range(2):
                dma(wg.ap.rearrange("p k (c n) -> p k c n", n=128),
                    win_s[16 + 4 * grp:20 + 4 * grp].rearrange("c p k n -> p k c n"), wg.sem, [win_s.b], [wg.b])
                for tb in range(4):
                    bk = ring.next()
                    for kt in range(16):
                        mm(bk[:, :], h[:, kt, tb * 128:(tb + 1) * 128], wg[:, kt, :], kt == 0, kt == 15,
                           [h.b, wg.b], [bk.b])
                    vt = vtile_ring.next()
                    cp(evac_eng(), vt[:, :], bk[:, :], [bk.b], [vt.b])
                    blk = slot0 // 128 + tb
                    dma(v_s[4 * grp:4 * grp + 4, :, blk, :].rearrange("h p e -> p h e"),
                        vt[:, :].rearrange("p (h e) -> p h e", e=128), vt.sem, [vt.b], [v_s.b])

        def cmul_add(dst, X, Aa, Ab, addend):
            tt("pool", pt1[:, :, :], X[:, :, :], Aa[:, :, :], MUL, [X.b, Aa.b], [pt1.b])
            tt("pool", pt2[:, 0, :], X[:, 1, :], Ab[:, 0, :], MUL, [X.b, Ab.b], [pt2.b])
            tt("pool", pt2[:, 1, :], X[:, 0, :], Ab[:, 1, :], MUL, [X.b, Ab.b], [pt2.b])
            tt("pool", pt1[:, :, :], pt1[:, :, :], pt2[:, :, :], ADD, [pt1.b, pt2.b], [pt1.b])
            tt("pool", dst[:, :, :], pt1[:, :, :], addend[:, :, :], ADD, [pt1.b, addend.b], [dst.b])

        def ssm_smat(J, ring):
            for tau in range(8):
                wsw = wsw_ring.next()
                dma(wsw[:, :, :, :], ws_s[tau], wsw.sem, [ws_s.b], [wsw.b])
                dma(wsw3[96:128, :, :, :], ws_s[tau, 96:128], wsw3.sem, [ws_s.b], [wsw3.b])
                bks = [ring.next() for _ in range(4)]
                for q in range(4):
                    bk = bks[q]
                    for ri in range(2):
                      for ro in range(SPLIT_ROLES):
                        if SPLIT_ROLES == 1:
                            o = bk[:, ri * 2 * NCH:(ri + 1) * 2 * NCH]
                        else:
                            o = bk[:, (ri * 2 + ro) * NCH:(ri * 2 + ro + 1) * NCH]
                        for j in range(TC):
                            if q < 3:
                                uu = uTc[32 * q:32 * q + 32, tau, :, :]
                                ww = wsw[32 * q:32 * q + 32, j, ri, :]
                            else:
                                uu = uTc[64:128, tau, :, :]
                                ww = wsw3[64:128, j, ri, :]
                            if SPLIT_ROLES == 1:
                                rhs_ = uu.rearrange("p o n -> p (o n)")[:, j::TC]
                            else:
                                rhs_ = uu[:, ro, j::TC]
                            mm(o, ww, rhs_, j == 0, j == TC - 1, [wsw.b, wsw3.b, uTc.b], [bk.b])
                for q in range(4):
                    src = bks[q][:, 0:4 * NCH].rearrange("p (r o c) -> p r o c", r=2, o=2)
                    cp(evac_eng(), Sbuf[:, :, :, 4 * tau + q].rearrange("p c r -> p r c"),
                       src[:, :, 0, :], [bks[q].b], [Sbuf.b])
                    cp(evac_eng(), Sown[:, :, :, 4 * tau + q].rearrange("p c r -> p r c"),
                       src[:, :, 1, :], [bks[q].b], [Sown.b])

        def ssm_scan(role, J):
            Sb = Sbuf if role == "oth" else Sown
            sbufs = [Sbuf.b] if role == "oth" else [Sown.b]
            if role == "oth":
                x0 = Zst
            else:
                cmul_add(pt3, Est, A5A, A5B, Soth)
                ts("pool", Xin[:, :, :], Est[:, :, :], picol[:, 1:2], None, MUL, None, [Est.b, picol.b], [Xin.b])
                ts("pool", pt3[:, :, :], pt3[:, :, :], picol[:, 0:1], None, MUL, None, [pt3.b, picol.b], [pt3.b])
                tt("pool", Xin[:, :, :], Xin[:, :, :], pt3[:, :, :], ADD, [Xin.b, pt3.b], [Xin.b])
                x0 = Xin
            for c in range(NCH):
                rbs = [x0.b] if c == 0 else sbufs
                tt("pool", pt1[:, :, :], (x0[:, :, :] if c == 0 else Sb[:, c - 1, :, :]), CA[:, :, :], MUL,
                   rbs + [CA.b], [pt1.b])
                xim = x0[:, 1, :] if c == 0 else Sb[:, c - 1, 1, :]
                xre = x0[:, 0, :] if c == 0 else Sb[:, c - 1, 0, :]
                tt("pool", pt2[:, 0, :], xim, CB[:, 0, :], MUL, rbs + [CB.b], [pt2.b])
                tt("pool", pt2[:, 1, :], xre, CB[:, 1, :], MUL, rbs + [CB.b], [pt2.b])
                tt("pool", pt1[:, :, :], pt1[:, :, :], pt2[:, :, :], ADD, [pt1.b, pt2.b], [pt1.b])
                tt("pool", Sb[:, c, :, :], Sb[:, c, :, :], pt1[:, :, :], ADD, sbufs + [pt1.b], sbufs)
            if role == "oth":
                cp("pool", Soth[:, :, :], Sb[:, NCH - 1, :, :], sbufs, [Soth.b])
            else:
                cp("pool", Fst[:, :, :], Sb[:, NCH - 1, :, :], sbufs, [Fst.b])
                cp("pool", Xh[:, :, :, 0], Xin[:, :, :], [Xin.b], [Xh.b])
                cp("pool", Xh[:, :, :, 1:NCH], Sb[:, 0:NCH - 1, :, :].rearrange("p c r q -> p r q c"),
                   sbufs, [Xh.b])
                cmul_add(pt3, Fst, A5A, A5B, Soth)
                ts("pool", Est[:, :, :], Fst[:, :, :], picol[:, 0:1], None, MUL, None, [Fst.b, picol.b], [Est.b])
                ts("pool", pt3[:, :, :], pt3[:, :, :], picol[:, 1:2], None, MUL, None, [pt3.b, picol.b], [pt3.b])
                tt("pool", Est[:, :, :], Est[:, :, :], pt3[:, :, :], ADD, [Est.b, pt3.b], [Est.b])

        def ssm_out(J, ring):
            u = uT["own"]
            for tau in range(8):
                bdw = bdw_ring.next()
                vww = vww_ring.next()
                dma(bdw[:, :, :], bd_s[tau], bdw.sem, [bd_s.b], [bdw.b])
                dma(vww[:, :, :, :, :], vw_s[tau], vww.sem, [vw_s.b], [vww.b])
                dma(vw3[:, :, :, 32:64], vw_s[tau, :, 3, :, :, :], vw3.sem, [vw_s.b], [vw3.b])
                dma(vw2[:, :, :, 0:32], vw_s[tau, :, 2, :, :, :], vw2.sem, [vw_s.b], [vw2.b])
                bk = ring.next()
                bkv = bk[:, :].rearrange("p (c i) -> p c i", i=TC)
                uv = u[:, tau, :].rearrange("p (c i) -> p c i", i=TC)
                for i in range(TC):
                    for j in range(i + 1):
                        mm(bkv[:, :, i], bdw[:, i - j, :], uv[:, :, j], (i == 0 and j == 0), False,
                           [bdw.b, u.b], [bk.b])
                for i in range(TC):
                    for q in range(4):
                        for ri in range(2):
                            if q < 2:
                                mm(bkv[32 * q:32 * q + 32, :, i], vww[:, q, i, ri, :], Xh[:, ri, 4 * tau + q, :],
                                   False, (i == TC - 1 and ri == 1), [vww.b, Xh.b], [bk.b])
                            else:
                                vz = vw2 if q == 2 else vw3
                                mm(bkv[64:128, :, i], vz[:, i, ri, :], Xh[:, ri, 4 * tau + q, :],
                                   False, (i == TC - 1 and q == 3 and ri == 1), [vz.b, Xh.b], [bk.b])
                yp = wk_ring.next()
                stt("dve", yp[:, :], u[:, tau, :], dcol[:, tau:tau + 1], bk[:, :], MUL, ADD,
                    [u.b, dcol.b, bk.b], [yp.b])
                gw = wk_ring.next()
                act(gw[:, :], yp[:, :], AF.Square, [yp.b], [gw.b])
                ts("dve", gw[:, :], gw[:, :], 0.044715, 1.0, MUL, ADD, [gw.b], [gw.b])
                tt("dve", gw[:, :], gw[:, :], yp[:, :], MUL, [gw.b, yp.b], [gw.b])
                act(gw[:, :], gw[:, :], AF.Sigmoid, [gw.b], [gw.b], scale=1.5957691216057308)
                tt("dve", gT[:, tau, :], yp[:, :], gw[:, :], MUL, [yp.b, gw.b], [gT.b])

        def glu_stage(J, ring):
            jobs = []
            st = {}

            def epi_g(n):
                def f(b1):
                    sg = wk_ring.next()
                    act(sg[:, :], b1[:, :], AF.Sigmoid, [b1.b, bglu.b], [sg.b], bias=bglu[:, n:n + 1])
                    st[n] = sg
                return f

            def epi_z(n):
                def f(b2):
                    sg = st[n]
                    sz = wk_ring.next()
                    act(sz[:, :], b2[:, :], AF.Silu, [b2.b], [sz.b])
                    tt("dve", sg[:, :], sg[:, :], sz[:, :], MUL, [sg.b, sz.b], [sg.b])
                    tt("dve", yssm[:, n, :], sg[:, :], gT[:, n, :], MUL, [sg.b, gT.b], [yssm.b])
                return f

            for n in range(8):
                jobs.append((wglu_s, n, gT, 8, epi_g(n)))
                jobs.append((win_s, 40 + n, hT["own"], 16, epi_z(n)))
            run_jobs(jobs, ring)

        def attention(J):
            h = hT["own"]
            sp_ring = Ring([0, 2])
            Ob = banks[4:6]
            acc_eng = ("dve", "dve")
            wq_next = load_wt(win_s, 0)
            for hd in range(8):
                wq = wq_next
                wz = load_wt(win_s, 24 + hd)
                if hd < 7:
                    wq_next = load_wt(win_s, hd + 1)
                pieces = {}

                def ensure(jp, hd=hd, pieces=pieces):
                    if jp > J or jp in pieces:
                        return
                    ktp = ktp_ring.next()
                    vp = vp_ring.next()
                    dma(ktp[:, :], kt_s[hd, :, jp * 1024:(jp + 1) * 1024], ktp.sem, [kt_s.b], [ktp.b])
                    dma(vp[:, :, :], v_s[hd, :, jp * 8:(jp + 1) * 8, :], vp.sem, [v_s.b], [vp.b])
                    pieces[jp] = (ktp, vp)

                ensure(0)
                ensure(1)
                bq = misc_ring.next()
                for kt in range(16):
                    mm(bq[:, :], wq[:, kt, :], h[:, kt, :], kt == 0, kt == 15, [wq.b, h.b], [bq.b])
                qknorm(bq, qg8, qT[:, :], qT.b, misc_ring)
                units = []
                for jp in range(J + 1):
                    for blk in range(8):
                        units.append((jp, blk))
                nu = len(units)

                def issue_s(n):
                    jp, blk = units[n]
                    ensure(jp)
                    if blk == 0:
                        ensure(jp + 2)
                    ktp, vp = pieces[jp]
                    bA = sp_ring.next()
                    diag = (jp == J and blk < 4)
                    for c in range(2):
                        sbk = banks[bA + c]
                        mm(sbk[:, :], ktp[64 * c:64 * c + 64, blk * 128:(blk + 1) * 128], qT[64 * c:64 * c + 64, :],
                           True, not diag, [ktp.b, qT.b], [sbk.b])
                        if diag:
                            mm(sbk[:, :], ident[:, :], negm[:, blk, :], False, True, [ident.b, negm.b], [sbk.b])
                    p = p_ring.next()
                    src = psum_all[:, 512 * bA:512 * bA + 1024]
                    rd = [banks[bA].b, banks[bA + 1].b]
                    if jp == J and blk >= 4:
                        act(p[:, :, :], src.rearrange("p (c n) -> p c n", c=2), AF.Exp, rd + [picol.b], [p.b],
                            bias=picol[:, 2:3])
                    else:
                        act(p[:, :, :], src.rearrange("p (c n) -> p c n", c=2), AF.Exp, rd, [p.b])
                    return p

                def issue_pv(n, p, hd=hd):
                    jp, blk = units[n]
                    ktp, vp = pieces[jp]
                    first = n == 0
                    last = n == nu - 1
                    for c in range(2):
                        mm(Ob[c][:, :], vp[:, blk, :], p[:, c, :], first, last, [vp.b, p.b], [Ob[c].b])
                    a_ = n % 2
                    ae = "pool" if (a_ == 1 and hd >= 2) else "dve"
                    if n < 2:
                        cp(ae, pacc[a_][:, :, :], p[:, :, :], [p.b], [pacc[a_].b])
                    else:
                        tt(ae, pacc[a_][:, :, :], pacc[a_][:, :, :], p[:, :, :], ADD,
                           [pacc[a_].b, p.b], [pacc[a_].b])

                LOOK = 1
                pq = [issue_s(n) for n in range(min(LOOK, nu))]
                for n in range(nu):
                    if n + LOOK < nu:
                        pq.append(issue_s(n + LOOK))
                    issue_pv(n, pq[n])
                a = []
                for c in range(2):
                    lb = misc_ring.next()
                    mm(lb[:, :], o128[:, :], pacc[0][:, c, :], True, False, [o128.b, pacc[0].b], [lb.b])
                    mm(lb[:, :], o128[:, :], pacc[1][:, c, :], False, True, [o128.b, pacc[1].b], [lb.b])
                    r = wk_ring.next()
                    S.add("dve", lambda e, r=r, lb=lb: e.reciprocal(r[:, :], lb[:, :]), reads=[lb.b], writes=[r.b])
                    stt("dve", r[:, :], Ob[c][:, :], 1.0 / 128.0, r[:, :], MUL, MUL, [Ob[c].b, r.b], [r.b])
                    a.append(r)
                dm = wk_ring.next()
                stt("dve", dm[:, :], a[1][:, :], neglam[:, 0:1], a[0][:, :], MUL, ADD,
                    [a[1].b, neglam.b, a[0].b], [dm.b])
                sq = wk_ring.next()
                act(sq[:, :], dm[:, :], AF.Square, [dm.b], [sq.b])
                b2 = misc_ring.next()
                mm(b2[:, :], o128[:, :], sq[:, :], True, True, [o128.b, sq.b], [b2.b])
                ts("dve", sq[:, :], b2[:, :], EPS, None, ADD, None, [b2.b], [sq.b])
                act(sq[:, :], sq[:, :], AF.Ln, [sq.b], [sq.b])
                act(sq[:, :], sq[:, :], AF.Exp, [sq.b], [sq.b], scale=-0.5)
                stt("dve", dm[:, :], dm[:, :], sub08[:, 0:1], sq[:, :], MUL, MUL, [dm.b, sub08.b, sq.b], [dm.b])
                bz = misc_ring.next()
                for kt in range(16):
                    mm(bz[:, :], wz[:, kt, :], h[:, kt, :], kt == 0, kt == 15, [wz.b, h.b], [bz.b])
                sz = wk_ring.next()
                act(sz[:, :], bz[:, :], AF.Silu, [bz.b], [sz.b])
                tt("dve", yatt[:, hd, :], dm[:, :], sz[:, :], MUL, [dm.b, sz.b], [yatt.b])

        def merge(J, ring):
            h = hT["own"]
            jobs = []
            st = {}

            def epi_gate(n, key):
                def f(g):
                    sg = wk_ring.next()
                    act(sg[:, :], g[:, :], AF.Sigmoid, [g.b], [sg.b])
                    st[(n, key)] = sg
                return f

            def epi_p(n, key):
                def f(pb):
                    sg = st[(n, key)]
                    tt("dve", sg[:, :], sg[:, :], pb[:, :], MUL, [sg.b, pb.b], [sg.b])
                    if key == "s":
                        sa = st[(n, "a")]
                        tt("dve", mT[:, n, :], sa[:, :], sg[:, :], ADD, [sa.b, sg.b], [mT.b])
                return f

            for n in range(16):
                jobs.append((win_s, 48 + n, h, 16, epi_gate(n, "a")))
                jobs.append((wpa_s, n, yatt, 8, epi_p(n, "a")))
                jobs.append((win_s, 64 + n, h, 16, epi_gate(n, "s")))
                jobs.append((wps_s, n, yssm, 8, epi_p(n, "s")))
            run_jobs(jobs, ring)

        def out_stage(J, ring):
            for grp in range(4):
                dma(wg[:, :, :], wout_s[grp], wg.sem, [wout_s.b], [wg.b])
                for tb in range(4):
                    r0_ = J * NT + tb * 128
                    xr = xres_ring.next()
                    dma(xr[:, :], x_own[r0_:r0_ + 128, grp * 512:(grp + 1) * 512], xr.sem, [], [xr.b])
                    bk = ring.next()
                    for kt in range(16):
                        mm(bk[:, :], mT[:, kt, tb * 128:(tb + 1) * 128], wg[:, kt, :], kt == 0, kt == 15,
                           [mT.b, wg.b], [bk.b])
                    ot = ot_ring.next()
                    tt("dve", ot[:, :], bk[:, :], xr[:, :], ADD, [bk.b, xr.b], [ot.b])
                    dma(out_own[r0_:r0_ + 128, grp * 512:(grp + 1) * 512], ot[:, :], ot.sem, [ot.b], [])

        for J in range(NJ):
            if stop >= 10:
                rms_to_hT("oth", J, gen_ring)
            if stop >= 11:
                stage_kvu("oth", J, gen_ring)
            if stop >= 13:
                rms_to_hT("own", J, gen_ring)
            if stop >= 14:
                stage_kvu("own", J, gen_ring)
            if stop >= 15:
                ssm_smat(J, gen_ring)
                ssm_scan("oth", J)
                ssm_scan("own", J)
            if stop >= 18:
                attention(J)
            if stop >= 16:
                ssm_out(J, gen_ring)
            if stop >= 17:
                glu_stage(J, gen_ring)
            if stop >= 19:
                merge(J, gen_ring)
            if stop >= 20:
                out_stage(J, gen_ring)

        S.emit(nc, es)
    return nc, S


def _bf(a):
    return np.ascontiguousarray(a).astype(ml_dtypes.bfloat16)


def prep_shared(inp):
    f = np.float32
    sh = {}
    sh["w_in"] = np.ascontiguousarray(inp["w_in"][0], dtype=f)
    sh["w_glu"] = np.ascontiguousarray(inp["w_glu"][0], dtype=f)
    sh["w_pa"] = np.ascontiguousarray(inp["w_proj_att"][0], dtype=f)
    sh["w_ps"] = np.ascontiguousarray(inp["w_proj_ssm"][0], dtype=f)
    sh["w_out"] = np.ascontiguousarray(inp["w_out"][0], dtype=f)
    sh["gain_col"] = np.ascontiguousarray(inp["ln_gain"][0].reshape(16, 128).T, dtype=f)
    sh["qg"] = np.ascontiguousarray(np.tile(inp["q_norm_gain"][0], 2).reshape(128, 1), dtype=f)
    sh["kg"] = np.ascontiguousarray(np.tile(inp["k_norm_gain"][0], 2).reshape(128, 1), dtype=f)
    lv = np.stack([inp["lambda_q1"][0], inp["lambda_k1"][0], inp["lambda_q2"][0], inp["lambda_k2"][0]])
    sh["lamv"] = np.ascontiguousarray(np.broadcast_to(lv[None], (128, 4, 64)), dtype=f)
    sh["subln"] = np.ascontiguousarray(inp["subln_gain"][0].reshape(128, 1), dtype=f)
    sh["dcol"] = np.ascontiguousarray(inp["ssm_d"][0].reshape(8, 128).T, dtype=f)
    sh["bglu"] = np.ascontiguousarray(inp["b_glu"][0].reshape(8, 128).T, dtype=f)
    lre = np.asarray(inp["ssm_lambda_re"][0], dtype=f)
    lim = np.asarray(inp["ssm_lambda_im"][0], dtype=f)
    ldt = np.asarray(inp["ssm_log_dt"][0], dtype=f)
    bre = np.asarray(inp["ssm_b_re"][0], dtype=f)
    bim = np.asarray(inp["ssm_b_im"][0], dtype=f)
    cre = np.asarray(inp["ssm_c_re"][0], dtype=f)
    cim = np.asarray(inp["ssm_c_im"][0], dtype=f)

    def nat(a):
        return a.reshape(NPAIR, 2, 64).transpose(1, 2, 0).reshape(128, NPAIR)

    ldt_gp = np.broadcast_to(ldt[:, None], (64, 64))
    sh["pn"] = np.ascontiguousarray(np.stack([nat(lre), nat(lim), nat(ldt_gp)], axis=1), dtype=f)

    def natpad(a_gpc):
        o = np.zeros((2, 64, NPAIR, 2, 16), f)
        a = a_gpc.reshape(NPAIR, 2, 64, 16)
        for m in range(2):
            o[m, :, :, m, :] = a[:, m].transpose(1, 0, 2)
        return o.reshape(128, NPAIR, 32)

    sh["bn"] = np.ascontiguousarray(np.stack([natpad(bre), natpad(bim)], axis=1), dtype=f)
    sh["cn"] = np.ascontiguousarray(np.stack([natpad(cre.transpose(0, 2, 1)), natpad(cim.transpose(0, 2, 1))], axis=1), dtype=f)

    def trn(a_gp):
        a = a_gp.reshape(8, 4, 2, 64).reshape(8, 4, 128)
        return np.broadcast_to(a[:, :, None, :], (8, 4, 32, 128)).reshape(8, 128, 128)

    sh["pt"] = np.ascontiguousarray(np.stack([trn(lre), trn(lim), trn(ldt_gp)], axis=2), dtype=f)

    def trnpad(a_gpc):
        o = np.zeros((8, 4, 2, 16, 2, 64), f)
        a = a_gpc.reshape(8, 4, 2, 64, 16)
        for m in range(2):
            o[:, :, m, :, m, :] = a[:, :, m].transpose(0, 1, 3, 2)
        return o.reshape(8, 128, 128)

    sh["bt"] = np.ascontiguousarray(np.stack([trnpad(bre), trnpad(bim)], axis=2), dtype=f)
    sh["ident"] = _bf(np.eye(128, dtype=f))
    sh["onesb"] = _bf(np.ones((128, 128), f))
    obd = np.zeros((128, 128), f)
    obd[:64, :64] = 1.0 / 64
    obd[64:, 64:] = 1.0 / 64
    sh["onesbd"] = obd
    sh["ones128"] = np.full((128, 128), 1.0 / 128, f)
    sh["kvals"] = np.ascontiguousarray(np.broadcast_to(np.arange(TC + 1, dtype=f)[None], (128, TC + 1)))
    kp = np.arange(128)[:, None, None]
    r = np.arange(4)[None, :, None]
    qq = np.arange(NT)[None, None, :]
    sh["negmask"] = _bf(np.where(128 * r + kp <= qq, 0.0, NEG).astype(f))
    return sh


def prep_core(x, b, pi, NJ):
    xs = np.asarray(x[b]).reshape(16, NT, D)
    own = np.ascontiguousarray(xs[pi::2][:NJ].reshape(NJ * NT, D), dtype=np.float32)
    oth = np.ascontiguousarray(xs[1 - pi::2][:NJ].reshape(NJ * NT, D), dtype=np.float32)
    pc = np.zeros((128, 3), np.float32)
    pc[:, 0] = pi
    pc[:, 1] = 1 - pi
    pc[:, 2] = 0.0 if pi == 1 else NEG
    return {"x_own": own, "x_oth": oth, "picol": pc}


_CACHE = {}


def run(inputs, NJ=NJ_FULL, trace=False, stop=99):
    inp = {k: np.asarray(v) for k, v in inputs.items()}
    if (NJ, stop) not in _CACHE:
        _CACHE[(NJ, stop)] = build(NJ, stop)[0]
    nc = _CACHE[(NJ, stop)]
    sh = prep_shared(inp)
    in_maps = []
    for c in range(8):
        m = dict(sh)
        m.update(prep_core(inp["x"], c // 2, c % 2, NJ))
        in_maps.append(m)
    res = run_bass_kernel_spmd(nc, in_maps, core_ids=list(range(8)), trace=trace)
    out = np.zeros((4, 16, NT, D), np.float32)
    for c in range(8):
        b, pi = c // 2, c % 2
        o = np.asarray(res.results[c]["out_own"]).reshape(NJ, NT, D)
        out[b, pi::2][:NJ] = o
    return out.reshape(4, 16 * NT, D), res


def kernel(**inputs):
    out, _ = run(inputs)
    return out
```

```python
import numpy as np
import ml_dtypes
import contextlib
import concourse.bass as bass
import concourse.mybir as mybir
from concourse.bass_utils import run_bass_kernel_spmd

F32 = mybir.dt.float32
BF16 = mybir.dt.bfloat16
AF = mybir.ActivationFunctionType
ALU = mybir.AluOpType
AX = mybir.AxisListType

D = 2048
NT = 512
NJ_FULL = 8
TC = 8
NCH = NT // TC
NPAIR = 32
MAGIC = 12582912.0
TWO_PI = 6.283185307179586
NEG = -30000.0
EPS = 1e-6
SAME_ENG_SYNC = True
SPLIT_ROLES = 1
ARENA_WORDS = 53000


class Buf:
    __slots__ = ("name", "ws", "rs", "multi", "excl")

    def __init__(self, name, multi=False):
        self.name = name
        self.ws = []
        self.rs = []
        self.multi = multi
        self.excl = False


class DSem:
    def __init__(self, name, bulk=False):
        self.name = name
        self.count = 0
        self.last = None
        self.bulk = bulk
        self.handle = None


class Op:
    __slots__ = ("eng", "fn", "deps", "dsem", "ev", "signal", "idx")


ENGS = ("pe", "act", "dve", "pool", "sp")


class Sched:
    def __init__(self):
        self.by_eng = {e: [] for e in ENGS}
        self.dsems = []
        self.pending = {e: None for e in ENGS}

    def dsem(self, name, bulk=False):
        s = DSem(name, bulk)
        self.dsems.append(s)
        return s

    def barrier(self):
        lasts = [self.by_eng[e][-1] for e in ENGS if e != "sp" and self.by_eng[e]]
        for s in self.dsems:
            if s.last is not None:
                lasts.append(s.last)
        for e in ENGS:
            self.pending[e] = list(lasts)

    def add(self, eng, fn, reads=(), writes=(), dsem=None):
        op = Op()
        op.eng = eng
        op.fn = fn
        op.dsem = dsem
        op.signal = False
        op.ev = None
        deps = set()
        ex = [b for b in reads if b.excl]
        if ex:
            reads = [b for b in reads if not b.excl]
            writes = list(writes) + [b for b in ex if b not in writes]
        if self.pending[eng] is not None:
            deps.update(self.pending[eng])
            self.pending[eng] = None
        for b in reads:
            deps.update(b.ws)
        for b in writes:
            if not b.multi:
                deps.update(b.ws)
            deps.update(b.rs)
        for b in reads:
            b.rs.append(op)
        for b in writes:
            if b.multi:
                b.ws.append(op)
            else:
                b.ws = [op]
            b.rs = []
        if dsem is not None:
            if (not dsem.bulk) and dsem.last is not None:
                deps.add(dsem.last)
            dsem.last = op
            dsem.count += 16
            op.ev = dsem.count
        deps.discard(op)
        fd = []
        for d in deps:
            if d.eng == eng and d.dsem is None:
                if eng == "pe" or not SAME_ENG_SYNC:
                    continue
            fd.append(d)
            d.signal = True
        op.deps = fd
        self.by_eng[eng].append(op)
        return op

    def emit(self, nc, es):
        esem = {}
        for e in ("pe", "act", "dve", "pool"):
            esem[e] = es.enter_context(nc.semaphore("sem_" + e))
        for s in self.dsems:
            s.handle = es.enter_context(nc.semaphore("d_" + s.name))
        for e in ("pe", "act", "dve", "pool"):
            c = 0
            for op in self.by_eng[e]:
                if op.signal:
                    c += 1
                    op.ev = c
        final_waits = [(s.handle, s.count) for s in self.dsems if s.count > 0]
        block = es.enter_context(nc.Block())
        engs = {"pe": block.tensor, "act": block.scalar, "dve": block.vector,
                "pool": block.gpsimd, "sp": block.sync}

        def make(ename):
            ops = self.by_eng[ename]

            def body(eng):
                waited = {}
                for op in ops:
                    need = {}
                    for d in op.deps:
                        if d.dsem is not None:
                            key = ("d", id(d.dsem))
                            sem = d.dsem.handle
                            val = d.dsem.count if d.dsem.bulk else d.ev
                        else:
                            key = ("e", d.eng)
                            sem = esem[d.eng]
                            val = d.ev
                        if key not in need or need[key][1] < val:
                            need[key] = (sem, val)
                    for key, (sem, val) in need.items():
                        if waited.get(key, 0) >= val:
                            continue
                        waited[key] = val
                        eng.wait_ge(sem, val)
                    ins = op.fn(eng)
                    if op.dsem is not None:
                        ins.then_inc(op.dsem.handle, 16)
                    elif op.signal:
                        ins.then_inc(esem[ename], 1)
                if ename == "sp":
                    for sem, val in final_waits:
                        eng.wait_ge(sem, val)
            return body

        for ename, dec in engs.items():
            dec(make(ename))


class T:
    def __init__(self, ap, name, multi=False, sem=None):
        self.ap = ap
        self.b = Buf(name, multi)
        self.sem = sem

    def __getitem__(self, k):
        return self.ap[k]


class Ring:
    def __init__(self, items):
        self.items = items
        self.i = 0

    def next(self):
        it = self.items[self.i % len(self.items)]
        self.i += 1
        return it


def build(NJ=NJ_FULL, stop=99):
    nc = bass.Bass("TRN2", target_bir_lowering=False)
    S = Sched()
    NTOK = NJ * NT
    NSLOT = NJ * 2 * NT

    def din(name, shape, dt=F32):
        return nc.dram_tensor(name, list(shape), dt, kind="ExternalInput").ap()

    x_own = din("x_own", [NTOK, D])
    x_oth = din("x_oth", [NTOK, D])
    w_in = din("w_in", [D, 10240])
    w_glu = din("w_glu", [1024, 1024])
    w_pa = din("w_pa", [1024, 2048])
    w_ps = din("w_ps", [1024, 2048])
    w_out = din("w_out", [D, D])
    gain_col = din("gain_col", [128, 16])
    qg_in = din("qg", [128, 1])
    kg_in = din("kg", [128, 1])
    lamv_in = din("lamv", [128, 4, 64])
    subln_in = din("subln", [128, 1])
    dcol_in = din("dcol", [128, 8])
    bglu_in = din("bglu", [128, 8])
    pn_in = din("pn", [128, 3, NPAIR])
    bn_in = din("bn", [128, 2, NPAIR, 32])
    cn_in = din("cn", [128, 2, NPAIR, 32])
    pt_in = din("pt", [8, 128, 3, 128])
    bt_in = din("bt", [8, 128, 2, 128])
    ident_in = din("ident", [128, 128], BF16)
    ones_in = din("onesb", [128, 128], BF16)
    obd_in = din("onesbd", [128, 128])
    o128_in = din("ones128", [128, 128])
    kv_in = din("kvals", [128, TC + 1])
    negm_in = din("negmask", [128, 4, NT], BF16)
    pic_in = din("picol", [128, 3])
    out_own = nc.dram_tensor("out_own", [NTOK, D], F32, kind="ExternalOutput").ap()

    def dscr(name, shape, dt=BF16):
        return T(nc.dram_tensor(name, list(shape), dt).ap(), name, multi=True)

    win_s = dscr("win_s", [80, 128, 16, 128])
    wglu_s = dscr("wglu_s", [8, 128, 8, 128])
    wpa_s = dscr("wpa_s", [16, 128, 8, 128])
    wps_s = dscr("wps_s", [16, 128, 8, 128])
    wout_s = dscr("wout_s", [4, 128, 16, 512])
    bd_s = dscr("bd_s", [8, 128, TC, 128])
    ws_s = dscr("ws_s", [8, 128, TC, 2, 128])
    vw_s = dscr("vw_s", [8, 128, 4, TC, 2, 32])
    kt_s = dscr("kt_s", [8, 128, NSLOT])
    v_s = dscr("v_s", [8, 128, NSLOT // 128, 128])

    es = contextlib.ExitStack()
    with es:
        arena_h = es.enter_context(nc.sbuf_tensor("arena", [128, ARENA_WORDS], F32))
        aoff = [0]

        def sb(name, shape, dt=F32, sem=False):
            n = 1
            for s_ in shape[1:]:
                n *= s_
            words = n if dt == F32 else (n + 1) // 2
            words += words & 1
            assert aoff[0] + words <= ARENA_WORDS, (name, aoff[0], words)
            ap = arena_h[:, aoff[0]:aoff[0] + words]
            aoff[0] += words
            if dt == BF16:
                ap = ap.bitcast(BF16)
            ap = ap[:, 0:n]
            if len(shape) > 2:
                names = ["a%d" % i for i in range(len(shape) - 1)]
                kw = {nm: s_ for nm, s_ in zip(names, shape[1:])}
                ap = ap.rearrange("p (%s) -> p %s" % (" ".join(names), " ".join(names)), **kw)
            return T(ap, name, sem=S.dsem(name) if sem else None)

        psum_all = es.enter_context(nc.psum_tensor("psall", [128, 4096], F32))
        banks = [T(psum_all[:, 512 * i:512 * (i + 1)], "pb%d" % i) for i in range(8)]
        for bk_ in banks:
            bk_.b.excl = True
        gen_ring = Ring(banks)
        misc_ring = Ring(banks[6:8])

        def mm(out, lhsT, rhs, start, stop, reads, writes):
            return S.add("pe", lambda e: e.matmul(out, lhsT, rhs, start=start, stop=stop),
                         reads=reads, writes=writes)

        def act(out, in_, func, reads, writes, bias=None, scale=None, accum_out=None):
            kw = {}
            if bias is not None:
                kw["bias"] = bias
            if scale is not None:
                kw["scale"] = scale
            if accum_out is not None:
                kw["accum_out"] = accum_out
            return S.add("act", lambda e: e.activation(out, in_, func, **kw), reads=reads, writes=writes)

        def tt(eng, out, in0, in1, op, reads, writes):
            return S.add(eng, lambda e: e.tensor_tensor(out, in0, in1, op), reads=reads, writes=writes)

        def ts(eng, out, in0, s1, s2, op0, op1, reads, writes):
            if s2 is None:
                return S.add(eng, lambda e: e.tensor_scalar(out, in0, s1, None, op0), reads=reads, writes=writes)
            return S.add(eng, lambda e: e.tensor_scalar(out, in0, s1, s2, op0, op1), reads=reads, writes=writes)

        def stt(eng, out, in0, scalar, in1, op0, op1, reads, writes):
            return S.add(eng, lambda e: e.scalar_tensor_tensor(out, in0, scalar, in1, op0, op1),
                         reads=reads, writes=writes)

        def cp(eng, out, in_, reads, writes):
            if eng == "act":
                return S.add("act", lambda e: e.copy(out, in_), reads=reads, writes=writes)
            return S.add(eng, lambda e: e.tensor_copy(out, in_), reads=reads, writes=writes)

        def dma(out, in_, dsem, reads, writes, eng="sp"):
            return S.add(eng, lambda e: e.dma_start(out=out, in_=in_), reads=reads, writes=writes, dsem=dsem)

        MUL, ADD, SUB = ALU.mult, ALU.add, ALU.subtract

        bulk = S.dsem("bulk", bulk=True)

        def cload(name, src, shape, dt=F32):
            t = sb("c_" + name, shape, dt)
            dma(t.ap, src, bulk, [], [t.b])
            return t

        gcol = cload("gain", gain_col, [128, 16])
        qg = cload("qg", qg_in, [128, 1])
        kg = cload("kg", kg_in, [128, 1])
        subln = cload("subln", subln_in, [128, 1])
        dcol = cload("dcol", dcol_in, [128, 8])
        bglu = cload("bglu", bglu_in, [128, 8])
        ident = cload("ident", ident_in, [128, 128], BF16)
        onesb = cload("onesb", ones_in, [128, 128], BF16)
        obd = cload("obd", obd_in, [128, 128])
        o128 = cload("o128", o128_in, [128, 128])
        negm = cload("negm", negm_in, [128, 4, NT], BF16)
        picol = cload("picol", pic_in, [128, 3])
        neglam = sb("neglam", [128, 1])
        qg8 = sb("qg8", [128, 1])
        sub08 = sb("sub08", [128, 1])
        SH = [128, 2, NPAIR]
        CA = sb("CA", SH)
        CB = sb("CB", SH)
        A5 = sb("A5", SH)
        A5A = sb("A5A", SH)
        A5B = sb("A5B", SH)
        Est = sb("Est", SH)
        Fst = sb("Fst", SH)
        Xin = sb("Xin", SH)
        Soth = sb("Soth", SH)
        Zst = sb("Zst", SH)
        pt1 = sb("pt1", SH)
        pt2 = sb("pt2", SH)
        pt3 = sb("pt3", SH)
        persist_end = aoff[0]

        lamv = cload("lamv", lamv_in, [128, 4, 64])
        pn = cload("pn", pn_in, [128, 3, NPAIR])
        bn = cload("bn", bn_in, [128, 2, NPAIR, 32])
        cn = cload("cn", cn_in, [128, 2, NPAIR, 32])
        kvals = cload("kvals", kv_in, [128, TC + 1])
        xt0 = [sb("xt%d" % i, [128, D], sem=True) for i in range(4)]
        xn0 = [sb("xn%d" % i, [128, D], BF16, sem=True) for i in range(4)]
        r0 = Ring([0, 1, 2, 3])

        def cast_weight(src, K, N, dst, ncol, use_gain):
            for kt in range(K // 128):
                for c0 in range(0, N, D):
                    w = min(D, N - c0)
                    i = r0.next()
                    a, an = xt0[i], xn0[i]
                    dma(a[:, 0:w], src[kt * 128:(kt + 1) * 128, c0:c0 + w], a.sem, [], [a.b])
                    if use_gain:
                        ts("dve", an[:, 0:w], a[:, 0:w], gcol[:, kt:kt + 1], None, MUL, None, [a.b, gcol.b], [an.b])
                    else:
                        cp("act", an[:, 0:w], a[:, 0:w], [a.b], [an.b])
                    nb = w // ncol
                    dsl = dst[c0 // ncol:c0 // ncol + nb, :, kt, :].rearrange("c p n -> p c n")
                    dma(dsl, an[:, 0:w].rearrange("p (c n) -> p c n", n=ncol), an.sem, [an.b], [dst.b])

        if stop >= 1:
            cast_weight(w_in, D, 10240, win_s, 128, True)
            cast_weight(w_glu, 1024, 1024, wglu_s, 128, False)
            cast_weight(w_pa, 1024, 2048, wpa_s, 128, False)
            cast_weight(w_ps, 1024, 2048, wps_s, 128, False)
            cast_weight(w_out, D, D, wout_s, 512, False)

        real_add = S.add
        if stop < 2:
            S.add = lambda *a_, **k_: None
        lam_init = 0.8 - 0.6 * 1.0
        ltmp = sb("ltmp", [128, 2, 64])
        lsum = sb("lsum", [128, 2])
        tt("dve", ltmp[:, 0, :], lamv[:, 0, :], lamv[:, 1, :], MUL, [lamv.b], [ltmp.b])
        tt("dve", ltmp[:, 1, :], lamv[:, 2, :], lamv[:, 3, :], MUL, [lamv.b], [ltmp.b])
        S.add("dve", lambda e: e.reduce_sum(lsum[:, :], ltmp[:, :, :], AX.X), reads=[ltmp.b], writes=[lsum.b])
        act(lsum[:, :], lsum[:, :], AF.Exp, [lsum.b], [lsum.b])
        stt("dve", neglam[:, :], lsum[:, 1:2], -lam_init, lsum[:, 0:1], ADD, SUB, [lsum.b], [neglam.b])
        ts("dve", qg8[:, :], qg[:, :], 0.125, None, MUL, None, [qg.b], [qg8.b])
        ts("dve", sub08[:, :], subln[:, :], 1.0 - lam_init, None, MUL, None, [subln.b], [sub08.b])

        def trig_alloc(prefix, n):
            sh3 = [128, n, TC + 1]
            sh2 = [128, n]
            d = {"n": n}
            for nm in ("arg", "t1", "t2", "PR", "PI"):
                d[nm] = sb(prefix + nm, sh3)
            for nm in ("dt", "lrdt", "ang", "nr", "den", "u1", "cr", "ci"):
                d[nm] = sb(prefix + nm, sh2)
            return d

        def trig_tables(d, lre, lim, ldt, rd):
            n = d["n"]
            sh3 = [128, n, TC + 1]
            arg, t1, t2, PR, PI = d["arg"], d["t1"], d["t2"], d["PR"], d["PI"]
            dtt, lrdt, ang, nr, den, u1, cr, ci = (d[k] for k in ("dt", "lrdt", "ang", "nr", "den", "u1", "cr", "ci"))
            act(dtt[:, :], ldt, AF.Exp, rd, [dtt.b])
            tt("dve", lrdt[:, :], lre, dtt[:, :], MUL, rd + [dtt.b], [lrdt.b])
            tt("dve", ang[:, :], lim, dtt[:, :], MUL, rd + [dtt.b], [ang.b])
            kb = kvals[:, :].unsqueeze(1).to_broadcast(sh3)
            tt("dve", arg[:, :, :], lrdt[:, :].unsqueeze(2).to_broadcast(sh3), kb, MUL, [lrdt.b, kvals.b], [arg.b])
            act(PR[:, :, :], arg[:, :, :], AF.Exp, [arg.b], [PR.b])
            tt("dve", arg[:, :, :], ang[:, :].unsqueeze(2).to_broadcast(sh3), kb, MUL, [ang.b, kvals.b], [arg.b])

            def reduce_sin(dst, shift):
                ts("dve", t2[:, :, :], arg[:, :, :], shift, None, ADD, None, [arg.b], [t2.b])
                ts("dve", t1[:, :, :], t2[:, :, :], 1.0 / TWO_PI, MAGIC, MUL, ADD, [t2.b], [t1.b])
                ts("dve", t1[:, :, :], t1[:, :, :], MAGIC, None, SUB, None, [t1.b], [t1.b])
                stt("dve", t1[:, :, :], t1[:, :, :], -TWO_PI, t2[:, :, :], MUL, ADD, [t1.b, t2.b], [t1.b])
                ts("dve", t1[:, :, :], t1[:, :, :], 3.1415925, -3.1415925, ALU.min, ALU.max, [t1.b], [t1.b])
                act(dst[:, :, :], t1[:, :, :], AF.Sin, [t1.b], [dst.b])

            reduce_sin(PI, 0.0)
            tt("dve", PI[:, :, :], PI[:, :, :], PR[:, :, :], MUL, [PI.b, PR.b], [PI.b])
            reduce_sin(t1, 1.5707963267948966)
            tt("dve", PR[:, :, :], PR[:, :, :], t1[:, :, :], MUL, [PR.b, t1.b], [PR.b])
            ts("dve", nr[:, :], PR[:, :, 1], -1.0, None, ADD, None, [PR.b], [nr.b])
            tt("dve", den[:, :], lre, lre, MUL, rd, [den.b])
            tt("dve", u1[:, :], lim, lim, MUL, rd, [u1.b])
            tt("dve", den[:, :], den[:, :], u1[:, :], ADD, [den.b, u1.b], [den.b])
            S.add("dve", lambda e: e.reciprocal(den[:, :], den[:, :]), reads=[den.b], writes=[den.b])
            tt("dve", cr[:, :], nr[:, :], lre, MUL, rd + [nr.b], [cr.b])
            tt("dve", u1[:, :], PI[:, :, 1], lim, MUL, rd + [PI.b], [u1.b])
            tt("dve", cr[:, :], cr[:, :], u1[:, :], ADD, [cr.b, u1.b], [cr.b])
            tt("dve", cr[:, :], cr[:, :], den[:, :], MUL, [cr.b, den.b], [cr.b])
            tt("dve", ci[:, :], PI[:, :, 1], lre, MUL, rd + [PI.b], [ci.b])
            tt("dve", u1[:, :], nr[:, :], lim, MUL, rd + [nr.b], [u1.b])
            tt("dve", ci[:, :], ci[:, :], u1[:, :], SUB, [ci.b, u1.b], [ci.b])
            tt("dve", ci[:, :], ci[:, :], den[:, :], MUL, [ci.b, den.b], [ci.b])

        dn = trig_alloc("n_", NPAIR)
        trig_tables(dn, pn[:, 0, :], pn[:, 1, :], pn[:, 2, :], [pn.b])
        PRn, PIn, crn, cin_ = dn["PR"], dn["PI"], dn["cr"], dn["ci"]

        def pack_coef(A, CAt, CBt):
            cp("dve", CAt[:, 0, :], A[:, 0, :], [A.b], [CAt.b])
            cp("dve", CAt[:, 1, :], A[:, 0, :], [A.b], [CAt.b])
            ts("dve", CBt[:, 0, :], A[:, 1, :], -1.0, None, MUL, None, [A.b], [CBt.b])
            cp("dve", CBt[:, 1, :], A[:, 1, :], [A.b], [CBt.b])

        cp("dve", A5[:, 0, :], PRn[:, :, TC], [PRn.b], [A5.b])
        cp("dve", A5[:, 1, :], PIn[:, :, TC], [PIn.b], [A5.b])
        pack_coef(A5, CA, CB)
        for _ in range(6):
            tt("dve", pt1[:, :, :], A5[:, :, :], A5[:, :, :], MUL, [A5.b], [pt1.b])
            tt("dve", pt2[:, 0, :], A5[:, 0, :], A5[:, 1, :], MUL, [A5.b], [pt2.b])
            tt("dve", A5[:, 0, :], pt1[:, 0, :], pt1[:, 1, :], SUB, [pt1.b], [A5.b])
            ts("dve", A5[:, 1, :], pt2[:, 0, :], 2.0, None, MUL, None, [pt2.b], [A5.b])
        pack_coef(A5, A5A, A5B)
        S.add("pool", lambda e: e.memset(Est[:, :, :], 0.0), writes=[Est.b])
        S.add("pool", lambda e: e.memset(Zst[:, :, :], 0.0), writes=[Zst.b])

        sh4 = [128, NPAIR, 32]
        bbn = sb("bbn", [128, 2, NPAIR, 32])
        w1 = sb("w1", sh4)
        g2 = sb("g2", sh4)
        gre = sb("gre", [128, NPAIR, 64])
        gim = sb("gim", [128, NPAIR, 64])
        S.add("pool", lambda e: e.memset(gre[:, :, :], 0.0), writes=[gre.b])
        S.add("pool", lambda e: e.memset(gim[:, :, :], 0.0), writes=[gim.b])
        ncim = sb("ncim", sh4)
        crb = crn[:, :].unsqueeze(2).to_broadcast(sh4)
        cib = cin_[:, :].unsqueeze(2).to_broadcast(sh4)
        tt("dve", bbn[:, 0, :, :], bn[:, 0, :, :], crb, MUL, [bn.b, crn.b], [bbn.b])
        tt("dve", w1[:, :, :], bn[:, 1, :, :], cib, MUL, [bn.b, cin_.b], [w1.b])
        tt("dve", bbn[:, 0, :, :], bbn[:, 0, :, :], w1[:, :, :], SUB, [bbn.b, w1.b], [bbn.b])
        tt("dve", bbn[:, 1, :, :], bn[:, 1, :, :], crb, MUL, [bn.b, crn.b], [bbn.b])
        tt("dve", w1[:, :, :], bn[:, 0, :, :], cib, MUL, [bn.b, cin_.b], [w1.b])
        tt("dve", bbn[:, 1, :, :], bbn[:, 1, :, :], w1[:, :, :], ADD, [bbn.b, w1.b], [bbn.b])
        ts("dve", ncim[:, :, :], cn[:, 1, :, :], -1.0, None, MUL, None, [cn.b], [ncim.b])

        vw = sb("vw", [128, NPAIR, TC, 2, 32], BF16, sem=True)
        for i in range(TC):
            prb = PRn[:, :, i + 1].unsqueeze(2).to_broadcast(sh4)
            pib = PIn[:, :, i + 1].unsqueeze(2).to_broadcast(sh4)
            tt("dve", w1[:, :, :], cn[:, 0, :, :], prb, MUL, [cn.b, PRn.b], [w1.b])
            tt("dve", g2[:, :, :], ncim[:, :, :], pib, MUL, [ncim.b, PIn.b], [g2.b])
            tt("dve", vw[:, :, i, 0, :], w1[:, :, :], g2[:, :, :], ADD, [w1.b, g2.b], [vw.b])
            tt("dve", w1[:, :, :], cn[:, 0, :, :], pib, MUL, [cn.b, PIn.b], [w1.b])
            tt("dve", g2[:, :, :], ncim[:, :, :], prb, MUL, [ncim.b, PRn.b], [g2.b])
            tt("dve", vw[:, :, i, 1, :], g2[:, :, :], w1[:, :, :], SUB, [w1.b, g2.b], [vw.b])
        for tau in range(8):
            dma(vw_s[tau], vw[:, 4 * tau:4 * tau + 4, :, :, :], vw.sem, [vw.b], [vw_s.b])

        bd = sb("bd", [128, 8, TC, 128], BF16, sem=True)
        S.add("pool", lambda e: e.memset(bd[:, :, :, :], 0.0), writes=[bd.b])
        for l in range(TC):
            prb = PRn[:, :, l].unsqueeze(2).to_broadcast(sh4)
            pib = PIn[:, :, l].unsqueeze(2).to_broadcast(sh4)
            tt("dve", gre[:, :, 32:64], bbn[:, 0, :, :], prb, MUL, [bbn.b, PRn.b], [gre.b])
            tt("dve", g2[:, :, :], bbn[:, 1, :, :], pib, MUL, [bbn.b, PIn.b], [g2.b])
            tt("dve", gre[:, :, 32:64], gre[:, :, 32:64], g2[:, :, :], SUB, [gre.b, g2.b], [gre.b])
            tt("dve", gim[:, :, 32:64], bbn[:, 0, :, :], pib, MUL, [bbn.b, PIn.b], [gim.b])
            tt("dve", g2[:, :, :], bbn[:, 1, :, :], prb, MUL, [bbn.b, PRn.b], [g2.b])
            tt("dve", gim[:, :, 32:64], gim[:, :, 32:64], g2[:, :, :], ADD, [gim.b, g2.b], [gim.b])
            for tau in range(8):
                bk = gen_ring.next()
                for q in range(4):
                    P = 4 * tau + q
                    if q < 3:
                        o = bk[32 * q:32 * q + 32, 32 * q:32 * q + 32]
                        c0_ = 32
                    else:
                        o = bk[64:128, 96:128]
                        c0_ = 0
                    mm(o, gre[:, P, c0_:64], cn[:, 0, P, :], True, False, [gre.b, cn.b], [bk.b])
                    mm(o, gim[:, P, c0_:64], ncim[:, P, :], False, True, [gim.b, ncim.b], [bk.b])
                for q in range(3):
                    cp("dve", bd[32 * q:32 * q + 32, tau, l, 32 * q:32 * q + 32],
                       bk[32 * q:32 * q + 32, 32 * q:32 * q + 32], [bk.b], [bd.b])
                cp("dve", bd[64:128, tau, l, 96:128], bk[64:128, 96:128], [bk.b], [bd.b])
        for tau in range(8):
            dma(bd_s[tau], bd[:, tau, :, :], bd.sem, [bd.b], [bd_s.b])

        ptl = sb("ptl", [128, 3, 128], sem=True)
        btl = sb("btl", [128, 2, 128], sem=True)
        wsw0 = [sb("wsw0_%d" % i, [128, TC, 2, 128], BF16, sem=True) for i in range(2)]
        bbt = sb("bbt", [128, 2, 128])
        w2 = sb("w2", [128, 128])
        w3 = sb("w3", [128, 128])
        dt_ = trig_alloc("t_", 128)
        for tau in range(8):
            dma(ptl[:, :, :], pt_in[tau], ptl.sem, [], [ptl.b])
            dma(btl[:, :, :], bt_in[tau], btl.sem, [], [btl.b])
            trig_tables(dt_, ptl[:, 0, :], ptl[:, 1, :], ptl[:, 2, :], [ptl.b])
            PRt, PIt, crt, cit = dt_["PR"], dt_["PI"], dt_["cr"], dt_["ci"]
            tt("dve", bbt[:, 0, :], btl[:, 0, :], crt[:, :], MUL, [btl.b, crt.b], [bbt.b])
            tt("dve", w2[:, :], btl[:, 1, :], cit[:, :], MUL, [btl.b, cit.b], [w2.b])
            tt("dve", bbt[:, 0, :], bbt[:, 0, :], w2[:, :], SUB, [bbt.b, w2.b], [bbt.b])
            tt("dve", bbt[:, 1, :], btl[:, 1, :], crt[:, :], MUL, [btl.b, crt.b], [bbt.b])
            tt("dve", w2[:, :], btl[:, 0, :], cit[:, :], MUL, [btl.b, cit.b], [w2.b])
            tt("dve", bbt[:, 1, :], bbt[:, 1, :], w2[:, :], ADD, [bbt.b, w2.b], [bbt.b])
            ws = wsw0[tau % 2]
            for j in range(TC):
                k = TC - 1 - j
                tt("dve", w2[:, :], bbt[:, 0, :], PRt[:, :, k], MUL, [bbt.b, PRt.b], [w2.b])
                tt("dve", w3[:, :], bbt[:, 1, :], PIt[:, :, k], MUL, [bbt.b, PIt.b], [w3.b])
                tt("dve", ws[:, j, 0, :], w2[:, :], w3[:, :], SUB, [w2.b, w3.b], [ws.b])
                tt("dve", w2[:, :], bbt[:, 0, :], PIt[:, :, k], MUL, [bbt.b, PIt.b], [w2.b])
                tt("dve", w3[:, :], bbt[:, 1, :], PRt[:, :, k], MUL, [bbt.b, PRt.b], [w3.b])
                tt("dve", ws[:, j, 1, :], w2[:, :], w3[:, :], ADD, [w2.b, w3.b], [ws.b])
            dma(ws_s[tau], ws[:, :, :, :], ws.sem, [ws.b], [ws_s.b])

        S.add = real_add
        S.barrier()
        aoff[0] = persist_end
        hT = {"own": sb("hT_own", [128, 16, NT], BF16), "oth": sb("hT_oth", [128, 16, NT], BF16)}
        mT = hT["oth"]
        uTc = sb("uTc", [128, 8, 2, NT], BF16)
        uT = {"oth": T(uTc[:, :, 0, :], "uT_oth"), "own": T(uTc[:, :, 1, :], "uT_own")}
        uT["oth"].b = uTc.b
        uT["own"].b = uTc.b
        gT = uT["oth"]
        ybuf = sb("ybuf", [128, 4096])
        yssm = T(ybuf[:, 0:2048].bitcast(BF16).rearrange("p (a n) -> p a n", n=NT), "yssm")
        yatt = T(ybuf[:, 2048:4096].bitcast(BF16).rearrange("p (a n) -> p a n", n=NT), "yatt")
        Sown = T(hT["oth"].ap.rearrange("p a n -> p (a n)").bitcast(F32)
                 .rearrange("p (c r q) -> p c r q", r=2, q=NPAIR), "Sown")
        Sown.b = hT["oth"].b
        wt_ring = Ring([sb("wt%d" % i, [128, 16, 128], BF16, sem=True) for i in range(3)])
        wg = sb("wg", [128, 16, 512], BF16, sem=True)
        ktp_ring = Ring([sb("ktp%d" % i, [128, 1024], BF16, sem=True) for i in range(4)])
        vp_ring = Ring([sb("vp%d" % i, [128, 8, 128], BF16, sem=True) for i in range(4)])
        p_ring = Ring([sb("P%d" % i, [128, 2, NT], BF16) for i in range(3)])
        pacc = [sb("pacc%d" % i, [128, 2, NT]) for i in range(2)]
        qT = sb("qT", [128, NT], BF16)
        Sbuf = sb("Sbuf", [128, NCH, 2, NPAIR])
        Xh = sb("Xh", [128, 2, NPAIR, NCH], BF16)
        bdw_ring = Ring([sb("bdw%d" % i, [128, TC, 128], BF16, sem=True) for i in range(1)])
        wsw_ring = Ring([sb("wsw%d" % i, [128, TC, 2, 128], BF16, sem=True) for i in range(1)])
        vww_ring = Ring([sb("vww%d" % i, [128, 4, TC, 2, 32], BF16, sem=True) for i in range(1)])
        wsw3 = sb("wsw3", [128, TC, 2, 128], BF16, sem=True)
        vw3 = sb("vw3", [128, TC, 2, 64], BF16, sem=True)
        S.add("pool", lambda e: e.memset(wsw3[64:96, :, :, :], 0.0), writes=[wsw3.b])
        S.add("pool", lambda e: e.memset(vw3[:, :, :, 0:32], 0.0), writes=[vw3.b])
        vw2 = sb("vw2", [128, TC, 2, 64], BF16, sem=True)
        S.add("pool", lambda e: e.memset(vw2[:, :, :, 32:64], 0.0), writes=[vw2.b])
        xt = sb("xt", [128, D], sem=True)
        xn = sb("xn", [128, D], BF16)
        wk_ring = Ring([sb("wk%d" % i, [128, NT]) for i in range(4)])
        xres_ring = Ring([sb("xres%d" % i, [128, NT], sem=True) for i in range(2)])
        ot_ring = Ring([sb("ot%d" % i, [128, NT], sem=True) for i in range(2)])
        ktile_ring = Ring([sb("ktile%d" % i, [128, NT], BF16, sem=True) for i in range(2)])
        vtile_ring = Ring([sb("vtile%d" % i, [128, NT], BF16, sem=True) for i in range(2)])
        ssq = sb("ssq", [128, 1])
        rstd = sb("rstd", [128, 1])

        evac_rr = [0]

        def evac_eng():
            evac_rr[0] += 1
            return "dve" if evac_rr[0] % 2 else "act"

        def load_wt(src, ct):
            w = wt_ring.next()
            nk = src.ap.shape[2]
            dma(w[:, 0:nk, :], src[ct], w.sem, [src.b], [w.b])
            return w

        def proj_fm(src, ct, hTt, ring, nkt=16):
            w = load_wt(src, ct)
            bk = ring.next()
            for kt in range(nkt):
                mm(bk[:, :], w[:, kt, :], hTt[:, kt, :], kt == 0, kt == nkt - 1, [w.b, hTt.b], [bk.b])
            return bk

        def run_jobs(jobs, ring):
            slots = {}

            def load(i):
                if i < len(jobs) and i not in slots:
                    slots[i] = load_wt(jobs[i][0], jobs[i][1])

            load(0)
            load(1)
            for i, (src, ct, rhsT, nkt, epi) in enumerate(jobs):
                load(i + 2)
                w = slots.pop(i)
                bk = ring.next()
                for kt in range(nkt):
                    mm(bk[:, :], w[:, kt, :], rhsT[:, kt, :], kt == 0, kt == nkt - 1, [w.b, rhsT.b], [bk.b])
                epi(bk)

        def rms_to_hT(role, J, ring):
            xsrc = x_own if role == "own" else x_oth
            h = hT[role]
            for tb in range(4):
                r0_ = J * NT + tb * 128
                dma(xt[:, :], xsrc[r0_:r0_ + 128, :], xt.sem, [], [xt.b])
                act(xn[:, :], xt[:, :], AF.Square, [xt.b], [xn.b, ssq.b], accum_out=ssq[:, :])
                ts("dve", rstd[:, :], ssq[:, :], 1.0 / D, EPS, MUL, ADD, [ssq.b], [rstd.b])
                act(rstd[:, :], rstd[:, :], AF.Sqrt, [rstd.b], [rstd.b])
                S.add("dve", lambda e: e.reciprocal(rstd[:, :], rstd[:, :]), reads=[rstd.b], writes=[rstd.b])
                ts("dve", xn[:, :], xt[:, :], rstd[:, 0:1], None, MUL, None, [xt.b, rstd.b], [xn.b])
                for half in range(2):
                    bk = ring.next()
                    bkb = bk.ap.bitcast(BF16)
                    for k in range(8):
                        kt = half * 8 + k
                        o_, i_ = bkb[:, k * 128:(k + 1) * 128], xn[:, kt * 128:(kt + 1) * 128]
                        S.add("pe", lambda e, o_=o_, i_=i_: e.transpose(o_, i_, ident[:, :]),
                              reads=[xn.b, ident.b], writes=[bk.b])
                    cp(evac_eng(), h[:, half * 8:(half + 1) * 8, tb * 128:(tb + 1) * 128],
                       bkb.rearrange("p (k n) -> p k n", n=128), [bk.b], [h.b])

        def qknorm(bk, gcol_t, dst_ap, dst_b, ring):
            sq = wk_ring.next()
            act(sq[:, :], bk[:, :], AF.Square, [bk.b], [sq.b])
            b2 = ring.next()
            mm(b2[:, :], obd[:, :], sq[:, :], True, True, [obd.b, sq.b], [b2.b])
            rs = wk_ring.next()
            ts("dve", rs[:, :], b2[:, :], EPS, None, ADD, None, [b2.b], [rs.b])
            act(rs[:, :], rs[:, :], AF.Ln, [rs.b], [rs.b])
            act(rs[:, :], rs[:, :], AF.Exp, [rs.b], [rs.b], scale=-0.5)
            stt("dve", dst_ap, bk[:, :], gcol_t[:, 0:1], rs[:, :], MUL, MUL, [bk.b, gcol_t.b, rs.b], [dst_b])

        def stage_kvu(role, J, ring):
            ridx = 0 if role == "own" else 1
            slot0 = J * 2 * NT + ridx * NT
            h = hT[role]
            def load_wv(grp):
                dma(wg.ap.rearrange("p k (c n) -> p k c n", n=128),
                    win_s[16 + 4 * grp:20 + 4 * grp].rearrange("c p k n -> p k c n"), wg.sem, [win_s.b], [wg.b])

            load_wv(0)
            jobs = []

            def epi_k(hd):
                def f(bk):
                    kt_ = ktile_ring.next()
                    qknorm(bk, kg, kt_[:, :], kt_.b, ring)
                    dma(kt_s[hd, :, slot0:slot0 + NT], kt_[:, :], kt_.sem, [kt_.b], [kt_s.b])
                return f

            def epi_u(tau):
                def f(bk):
                    cp(evac_eng(), uT[role][:, tau, :], bk[:, :], [bk.b], [uT[role].b])
                return f

            for hd in range(8):
                jobs.append((win_s, 8 + hd, h, 16, epi_k(hd)))
            for tau in range(8):
                jobs.append((win_s, 32 + tau, h, 16, epi_u(tau)))
            run_jobs(jobs, ring)
            for grp in range(2):
                if grp > 0:
                    load_wv(grp)
                for tb in range(4):
                    bk = ring.next()
                    for kt in range(16):
                        mm(bk[:, :], h[:, kt, tb * 128:(tb + 1) * 128], wg[:, kt, :], kt == 0, kt == 15,
                           [h.b, wg.b], [bk.b])
                    vt = vtile_ring.next()
                    cp(evac_eng(), vt[:, :], bk[:, :], [bk.b], [vt.b])
                    blk = slot0 // 128 + tb
                    dma(v_s[4 * grp:4 * grp + 4, :, blk, :].rearrange("h p e -> p h e"),
                        vt[:, :].rearrange("p (h e) -> p h e", e=128), vt.sem, [vt.b], [v_s.b])

        def cmul_add(dst, X, Aa, Ab, addend):
            tt("pool", pt1[:, :, :], X[:, :, :], Aa[:, :, :], MUL, [X.b, Aa.b], [pt1.b])
            tt("pool", pt2[:, 0, :], X[:, 1, :], Ab[:, 0, :], MUL, [X.b, Ab.b], [pt2.b])
            tt("pool", pt2[:, 1, :], X[:, 0, :], Ab[:, 1, :], MUL, [X.b, Ab.b], [pt2.b])
            tt("pool", pt1[:, :, :], pt1[:, :, :], pt2[:, :, :], ADD, [pt1.b, pt2.b], [pt1.b])
            tt("pool", dst[:, :, :], pt1[:, :, :], addend[:, :, :], ADD, [pt1.b, addend.b], [dst.b])

        def ssm_smat(J, ring):
            for tau in range(8):
                wsw = wsw_ring.next()
                dma(wsw[:, :, :, :], ws_s[tau], wsw.sem, [ws_s.b], [wsw.b])
                dma(wsw3[96:128, :, :, :], ws_s[tau, 96:128], wsw3.sem, [ws_s.b], [wsw3.b])
                bks = [ring.next() for _ in range(4)]
                for q in range(4):
                    bk = bks[q]
                    for ri in range(2):
                      for ro in range(SPLIT_ROLES):
                        if SPLIT_ROLES == 1:
                            o = bk[:, ri * 2 * NCH:(ri + 1) * 2 * NCH]
                        else:
                            o = bk[:, (ri * 2 + ro) * NCH:(ri * 2 + ro + 1) * NCH]
                        for j in range(TC):
                            if q < 3:
                                uu = uTc[32 * q:32 * q + 32, tau, :, :]
                                ww = wsw[32 * q:32 * q + 32, j, ri, :]
                            else:
                                uu = uTc[64:128, tau, :, :]
                                ww = wsw3[64:128, j, ri, :]
                            if SPLIT_ROLES == 1:
                                rhs_ = uu.rearrange("p o n -> p (o n)")[:, j::TC]
                            else:
                                rhs_ = uu[:, ro, j::TC]
                            mm(o, ww, rhs_, j == 0, j == TC - 1, [wsw.b, wsw3.b, uTc.b], [bk.b])
                for q in range(4):
                    src = bks[q][:, 0:4 * NCH].rearrange("p (r o c) -> p r o c", r=2, o=2)
                    cp(evac_eng(), Sbuf[:, :, :, 4 * tau + q].rearrange("p c r -> p r c"),
                       src[:, :, 0, :], [bks[q].b], [Sbuf.b])
                    cp(evac_eng(), Sown[:, :, :, 4 * tau + q].rearrange("p c r -> p r c"),
                       src[:, :, 1, :], [bks[q].b], [Sown.b])

        def ssm_scan(role, J):
            Sb = Sbuf if role == "oth" else Sown
            sbufs = [Sbuf.b] if role == "oth" else [Sown.b]
            if role == "oth":
                x0 = Zst
            else:
                cmul_add(pt3, Est, A5A, A5B, Soth)
                ts("pool", Xin[:, :, :], Est[:, :, :], picol[:, 1:2], None, MUL, None, [Est.b, picol.b], [Xin.b])
                ts("pool", pt3[:, :, :], pt3[:, :, :], picol[:, 0:1], None, MUL, None, [pt3.b, picol.b], [pt3.b])
                tt("pool", Xin[:, :, :], Xin[:, :, :], pt3[:, :, :], ADD, [Xin.b, pt3.b], [Xin.b])
                x0 = Xin
            for c in range(NCH):
                rbs = [x0.b] if c == 0 else sbufs
                tt("pool", pt1[:, :, :], (x0[:, :, :] if c == 0 else Sb[:, c - 1, :, :]), CA[:, :, :], MUL,
                   rbs + [CA.b], [pt1.b])
                xim = x0[:, 1, :] if c == 0 else Sb[:, c - 1, 1, :]
                xre = x0[:, 0, :] if c == 0 else Sb[:, c - 1, 0, :]
                tt("pool", pt2[:, 0, :], xim, CB[:, 0, :], MUL, rbs + [CB.b], [pt2.b])
                tt("pool", pt2[:, 1, :], xre, CB[:, 1, :], MUL, rbs + [CB.b], [pt2.b])
                tt("pool", pt1[:, :, :], pt1[:, :, :], pt2[:, :, :], ADD, [pt1.b, pt2.b], [pt1.b])
                tt("pool", Sb[:, c, :, :], Sb[:, c, :, :], pt1[:, :, :], ADD, sbufs + [pt1.b], sbufs)
            if role == "oth":
                cp("pool", Soth[:, :, :], Sb[:, NCH - 1, :, :], sbufs, [Soth.b])
            else:
                cp("pool", Fst[:, :, :], Sb[:, NCH - 1, :, :], sbufs, [Fst.b])
                cp("pool", Xh[:, :, :, 0], Xin[:, :, :], [Xin.b], [Xh.b])
                cp("pool", Xh[:, :, :, 1:NCH], Sb[:, 0:NCH - 1, :, :].rearrange("p c r q -> p r q c"),
                   sbufs, [Xh.b])
                cmul_add(pt3, Fst, A5A, A5B, Soth)
                ts("pool", Est[:, :, :], Fst[:, :, :], picol[:, 0:1], None, MUL, None, [Fst.b, picol.b], [Est.b])
                ts("pool", pt3[:, :, :], pt3[:, :, :], picol[:, 1:2], None, MUL, None, [pt3.b, picol.b], [pt3.b])
                tt("pool", Est[:, :, :], Est[:, :, :], pt3[:, :, :], ADD, [Est.b, pt3.b], [Est.b])

        def ssm_out(J, ring):
            u = uT["own"]
            for tau in range(8):
                bdw = bdw_ring.next()
                vww = vww_ring.next()
                dma(bdw[:, :, :], bd_s[tau], bdw.sem, [bd_s.b], [bdw.b])
                dma(vww[:, :, :, :, :], vw_s[tau], vww.sem, [vw_s.b], [vww.b])
                dma(vw3[:, :, :, 32:64], vw_s[tau, :, 3, :, :, :], vw3.sem, [vw_s.b], [vw3.b])
                dma(vw2[:, :, :, 0:32], vw_s[tau, :, 2, :, :, :], vw2.sem, [vw_s.b], [vw2.b])
                bk = ring.next()
                bkv = bk[:, :].rearrange("p (c i) -> p c i", i=TC)
                uv = u[:, tau, :].rearrange("p (c i) -> p c i", i=TC)
                for i in range(TC):
                    for j in range(i + 1):
                        mm(bkv[:, :, i], bdw[:, i - j, :], uv[:, :, j], (i == 0 and j == 0), False,
                           [bdw.b, u.b], [bk.b])
                for i in range(TC):
                    for q in range(4):
                        for ri in range(2):
                            if q < 2:
                                mm(bkv[32 * q:32 * q + 32, :, i], vww[:, q, i, ri, :], Xh[:, ri, 4 * tau + q, :],
                                   False, (i == TC - 1 and ri == 1), [vww.b, Xh.b], [bk.b])
                            else:
                                vz = vw2 if q == 2 else vw3
                                mm(bkv[64:128, :, i], vz[:, i, ri, :], Xh[:, ri, 4 * tau + q, :],
                                   False, (i == TC - 1 and q == 3 and ri == 1), [vz.b, Xh.b], [bk.b])
                yp = wk_ring.next()
                stt("dve", yp[:, :], u[:, tau, :], dcol[:, tau:tau + 1], bk[:, :], MUL, ADD,
                    [u.b, dcol.b, bk.b], [yp.b])
                gw = wk_ring.next()
                act(gw[:, :], yp[:, :], AF.Square, [yp.b], [gw.b])
                ts("dve", gw[:, :], gw[:, :], 0.044715, 1.0, MUL, ADD, [gw.b], [gw.b])
                tt("dve", gw[:, :], gw[:, :], yp[:, :], MUL, [gw.b, yp.b], [gw.b])
                act(gw[:, :], gw[:, :], AF.Sigmoid, [gw.b], [gw.b], scale=1.5957691216057308)
                tt("dve", gT[:, tau, :], yp[:, :], gw[:, :], MUL, [yp.b, gw.b], [gT.b])

        def glu_stage(J, ring):
            jobs = []
            st = {}

            def epi_g(n):
                def f(b1):
                    sg = wk_ring.next()
                    act(sg[:, :], b1[:, :], AF.Sigmoid, [b1.b, bglu.b], [sg.b], bias=bglu[:, n:n + 1])
                    st[n] = sg
                return f

            def epi_z(n):
                def f(b2):
                    sg = st[n]
                    sz = wk_ring.next()
                    act(sz[:, :], b2[:, :], AF.Silu, [b2.b], [sz.b])
                    tt("dve", sg[:, :], sg[:, :], sz[:, :], MUL, [sg.b, sz.b], [sg.b])
                    tt("dve", yssm[:, n, :], sg[:, :], gT[:, n, :], MUL, [sg.b, gT.b], [yssm.b])
                return f

            for n in range(8):
                jobs.append((wglu_s, n, gT, 8, epi_g(n)))
                jobs.append((win_s, 40 + n, hT["own"], 16, epi_z(n)))
            run_jobs(jobs, ring)

        def attention(J):
            h = hT["own"]
            sp_ring = Ring([0, 2])
            Ob = banks[4:6]
            acc_eng = ("dve", "dve")
            wq_next = load_wt(win_s, 0)
            for hd in range(8):
                wq = wq_next
                wz = load_wt(win_s, 24 + hd)
                if hd < 7:
                    wq_next = load_wt(win_s, hd + 1)
                pieces = {}

                def ensure(jp, hd=hd, pieces=pieces):
                    if jp > J or jp in pieces:
                        return
                    ktp = ktp_ring.next()
                    vp = vp_ring.next()
                    dma(ktp[:, :], kt_s[hd, :, jp * 1024:(jp + 1) * 1024], ktp.sem, [kt_s.b], [ktp.b])
                    dma(vp[:, :, :], v_s[hd, :, jp * 8:(jp + 1) * 8, :], vp.sem, [v_s.b], [vp.b])
                    pieces[jp] = (ktp, vp)

                ensure(0)
                ensure(1)
                bq = misc_ring.next()
                for kt in range(16):
                    mm(bq[:, :], wq[:, kt, :], h[:, kt, :], kt == 0, kt == 15, [wq.b, h.b], [bq.b])
                qknorm(bq, qg8, qT[:, :], qT.b, misc_ring)
                units = []
                for jp in range(J + 1):
                    for blk in range(8):
                        units.append((jp, blk))
                nu = len(units)

                def issue_s(n):
                    jp, blk = units[n]
                    ensure(jp)
                    if blk == 0:
                        ensure(jp + 2)
                    ktp, vp = pieces[jp]
                    bA = sp_ring.next()
                    diag = (jp == J and blk < 4)
                    for c in range(2):
                        sbk = banks[bA + c]
                        mm(sbk[:, :], ktp[64 * c:64 * c + 64, blk * 128:(blk + 1) * 128], qT[64 * c:64 * c + 64, :],
                           True, not diag, [ktp.b, qT.b], [sbk.b])
                        if diag:
                            mm(sbk[:, :], ident[:, :], negm[:, blk, :], False, True, [ident.b, negm.b], [sbk.b])
                    p = p_ring.next()
                    src = psum_all[:, 512 * bA:512 * bA + 1024]
                    rd = [banks[bA].b, banks[bA + 1].b]
                    if jp == J and blk >= 4:
                        act(p[:, :, :], src.rearrange("p (c n) -> p c n", c=2), AF.Exp, rd + [picol.b], [p.b],
                            bias=picol[:, 2:3])
                    else:
                        act(p[:, :, :], src.rearrange("p (c n) -> p c n", c=2), AF.Exp, rd, [p.b])
                    return p

                def issue_pv(n, p, hd=hd):
                    jp, blk = units[n]
                    ktp, vp = pieces[jp]
                    first = n == 0
                    last = n == nu - 1
                    for c in range(2):
                        mm(Ob[c][:, :], vp[:, blk, :], p[:, c, :], first, last, [vp.b, p.b], [Ob[c].b])
                    a_ = n % 2
                    ae = "pool" if (a_ == 1 and hd >= 2) else "dve"
                    if n < 2:
                        cp(ae, pacc[a_][:, :, :], p[:, :, :], [p.b], [pacc[a_].b])
                    else:
                        tt(ae, pacc[a_][:, :, :], pacc[a_][:, :, :], p[:, :, :], ADD,
                           [pacc[a_].b, p.b], [pacc[a_].b])

                LOOK = 1
                pq = [issue_s(n) for n in range(min(LOOK, nu))]
                for n in range(nu):
                    if n + LOOK < nu:
                        pq.append(issue_s(n + LOOK))
                    issue_pv(n, pq[n])
                a = []
                for c in range(2):
                    lb = misc_ring.next()
                    mm(lb[:, :], o128[:, :], pacc[0][:, c, :], True, False, [o128.b, pacc[0].b], [lb.b])
                    mm(lb[:, :], o128[:, :], pacc[1][:, c, :], False, True, [o128.b, pacc[1].b], [lb.b])
                    r = wk_ring.next()
                    S.add("dve", lambda e, r=r, lb=lb: e.reciprocal(r[:, :], lb[:, :]), reads=[lb.b], writes=[r.b])
                    stt("dve", r[:, :], Ob[c][:, :], 1.0 / 128.0, r[:, :], MUL, MUL, [Ob[c].b, r.b], [r.b])
                    a.append(r)
                dm = wk_ring.next()
                stt("dve", dm[:, :], a[1][:, :], neglam[:, 0:1], a[0][:, :], MUL, ADD,
                    [a[1].b, neglam.b, a[0].b], [dm.b])
                sq = wk_ring.next()
                act(sq[:, :], dm[:, :], AF.Square, [dm.b], [sq.b])
                b2 = misc_ring.next()
                mm(b2[:, :], o128[:, :], sq[:, :], True, True, [o128.b, sq.b], [b2.b])
                ts("dve", sq[:, :], b2[:, :], EPS, None, ADD, None, [b2.b], [sq.b])
                act(sq[:, :], sq[:, :], AF.Ln, [sq.b], [sq.b])
                act(sq[:, :], sq[:, :], AF.Exp, [sq.b], [sq.b], scale=-0.5)
                stt("dve", dm[:, :], dm[:, :], sub08[:, 0:1], sq[:, :], MUL, MUL, [dm.b, sub08.b, sq.b], [dm.b])
                bz = misc_ring.next()
                for kt in range(16):
                    mm(bz[:, :], wz[:, kt, :], h[:, kt, :], kt == 0, kt == 15, [wz.b, h.b], [bz.b])
                sz = wk_ring.next()
                act(sz[:, :], bz[:, :], AF.Silu, [bz.b], [sz.b])
                tt("dve", yatt[:, hd, :], dm[:, :], sz[:, :], MUL, [dm.b, sz.b], [yatt.b])

        def merge(J, ring):
            h = hT["own"]
            jobs = []
            st = {}

            def epi_gate(n, key):
                def f(g):
                    sg = wk_ring.next()
                    act(sg[:, :], g[:, :], AF.Sigmoid, [g.b], [sg.b])
                    st[(n, key)] = sg
                return f

            def epi_p(n, key):
                def f(pb):
                    sg = st[(n, key)]
                    tt("dve", sg[:, :], sg[:, :], pb[:, :], MUL, [sg.b, pb.b], [sg.b])
                    if key == "s":
                        sa = st[(n, "a")]
                        tt("dve", mT[:, n, :], sa[:, :], sg[:, :], ADD, [sa.b, sg.b], [mT.b])
                return f

            for n in range(16):
                jobs.append((win_s, 48 + n, h, 16, epi_gate(n, "a")))
                jobs.append((wpa_s, n, yatt, 8, epi_p(n, "a")))
                jobs.append((win_s, 64 + n, h, 16, epi_gate(n, "s")))
                jobs.append((wps_s, n, yssm, 8, epi_p(n, "s")))
            run_jobs(jobs, ring)

        def out_stage(J, ring):
            for grp in range(4):
                if grp > 0:
                    dma(wg[:, :, :], wout_s[grp], wg.sem, [wout_s.b], [wg.b])
                for tb in range(4):
                    r0_ = J * NT + tb * 128
                    xr = xres_ring.next()
                    dma(xr[:, :], x_own[r0_:r0_ + 128, grp * 512:(grp + 1) * 512], xr.sem, [], [xr.b])
                    bk = ring.next()
                    for kt in range(16):
                        mm(bk[:, :], mT[:, kt, tb * 128:(tb + 1) * 128], wg[:, kt, :], kt == 0, kt == 15,
                           [mT.b, wg.b], [bk.b])
                    ot = ot_ring.next()
                    tt("dve", ot[:, :], bk[:, :], xr[:, :], ADD, [bk.b, xr.b], [ot.b])
                    dma(out_own[r0_:r0_ + 128, grp * 512:(grp + 1) * 512], ot[:, :], ot.sem, [ot.b], [])

        for J in range(NJ):
            if stop >= 10:
                rms_to_hT("oth", J, gen_ring)
            if stop >= 11:
                stage_kvu("oth", J, gen_ring)
            if stop >= 13:
                rms_to_hT("own", J, gen_ring)
            if stop >= 14:
                stage_kvu("own", J, gen_ring)
            if stop >= 15:
                ssm_smat(J, gen_ring)
                ssm_scan("oth", J)
                ssm_scan("own", J)
            if stop >= 18:
                attention(J)
            if stop >= 16:
                ssm_out(J, gen_ring)
            if stop >= 17:
                glu_stage(J, gen_ring)
            if stop >= 19:
                dma(wg[:, :, :], wout_s[0], wg.sem, [wout_s.b], [wg.b])
                merge(J, gen_ring)
            if stop >= 20:
                out_stage(J, gen_ring)

        S.emit(nc, es)
    return nc, S


def _bf(a):
    return np.ascontiguousarray(a).astype(ml_dtypes.bfloat16)


def prep_shared(inp):
    f = np.float32
    sh = {}
    sh["w_in"] = np.ascontiguousarray(inp["w_in"][0], dtype=f)
    sh["w_glu"] = np.ascontiguousarray(inp["w_glu"][0], dtype=f)
    sh["w_pa"] = np.ascontiguousarray(inp["w_proj_att"][0], dtype=f)
    sh["w_ps"] = np.ascontiguousarray(inp["w_proj_ssm"][0], dtype=f)
    sh["w_out"] = np.ascontiguousarray(inp["w_out"][0], dtype=f)
    sh["gain_col"] = np.ascontiguousarray(inp["ln_gain"][0].reshape(16, 128).T, dtype=f)
    sh["qg"] = np.ascontiguousarray(np.tile(inp["q_norm_gain"][0], 2).reshape(128, 1), dtype=f)
    sh["kg"] = np.ascontiguousarray(np.tile(inp["k_norm_gain"][0], 2).reshape(128, 1), dtype=f)
    lv = np.stack([inp["lambda_q1"][0], inp["lambda_k1"][0], inp["lambda_q2"][0], inp["lambda_k2"][0]])
    sh["lamv"] = np.ascontiguousarray(np.broadcast_to(lv[None], (128, 4, 64)), dtype=f)
    sh["subln"] = np.ascontiguousarray(inp["subln_gain"][0].reshape(128, 1), dtype=f)
    sh["dcol"] = np.ascontiguousarray(inp["ssm_d"][0].reshape(8, 128).T, dtype=f)
    sh["bglu"] = np.ascontiguousarray(inp["b_glu"][0].reshape(8, 128).T, dtype=f)
    lre = np.asarray(inp["ssm_lambda_re"][0], dtype=f)
    lim = np.asarray(inp["ssm_lambda_im"][0], dtype=f)
    ldt = np.asarray(inp["ssm_log_dt"][0], dtype=f)
    bre = np.asarray(inp["ssm_b_re"][0], dtype=f)
    bim = np.asarray(inp["ssm_b_im"][0], dtype=f)
    cre = np.asarray(inp["ssm_c_re"][0], dtype=f)
    cim = np.asarray(inp["ssm_c_im"][0], dtype=f)

    def nat(a):
        return a.reshape(NPAIR, 2, 64).transpose(1, 2, 0).reshape(128, NPAIR)

    ldt_gp = np.broadcast_to(ldt[:, None], (64, 64))
    sh["pn"] = np.ascontiguousarray(np.stack([nat(lre), nat(lim), nat(ldt_gp)], axis=1), dtype=f)

    def natpad(a_gpc):
        o = np.zeros((2, 64, NPAIR, 2, 16), f)
        a = a_gpc.reshape(NPAIR, 2, 64, 16)
        for m in range(2):
            o[m, :, :, m, :] = a[:, m].transpose(1, 0, 2)
        return o.reshape(128, NPAIR, 32)

    sh["bn"] = np.ascontiguousarray(np.stack([natpad(bre), natpad(bim)], axis=1), dtype=f)
    sh["cn"] = np.ascontiguousarray(np.stack([natpad(cre.transpose(0, 2, 1)), natpad(cim.transpose(0, 2, 1))], axis=1), dtype=f)

    def trn(a_gp):
        a = a_gp.reshape(8, 4, 2, 64).reshape(8, 4, 128)
        return np.broadcast_to(a[:, :, None, :], (8, 4, 32, 128)).reshape(8, 128, 128)

    sh["pt"] = np.ascontiguousarray(np.stack([trn(lre), trn(lim), trn(ldt_gp)], axis=2), dtype=f)

    def trnpad(a_gpc):
        o = np.zeros((8, 4, 2, 16, 2, 64), f)
        a = a_gpc.reshape(8, 4, 2, 64, 16)
        for m in range(2):
            o[:, :, m, :, m, :] = a[:, :, m].transpose(0, 1, 3, 2)
        return o.reshape(8, 128, 128)

    sh["bt"] = np.ascontiguousarray(np.stack([trnpad(bre), trnpad(bim)], axis=2), dtype=f)
    sh["ident"] = _bf(np.eye(128, dtype=f))
    sh["onesb"] = _bf(np.ones((128, 128), f))
    obd = np.zeros((128, 128), f)
    obd[:64, :64] = 1.0 / 64
    obd[64:, 64:] = 1.0 / 64
    sh["onesbd"] = obd
    sh["ones128"] = np.full((128, 128), 1.0 / 128, f)
    sh["kvals"] = np.ascontiguousarray(np.broadcast_to(np.arange(TC + 1, dtype=f)[None], (128, TC + 1)))
    kp = np.arange(128)[:, None, None]
    r = np.arange(4)[None, :, None]
    qq = np.arange(NT)[None, None, :]
    sh["negmask"] = _bf(np.where(128 * r + kp <= qq, 0.0, NEG).astype(f))
    return sh


def prep_core(x, b, pi, NJ):
    xs = np.asarray(x[b]).reshape(16, NT, D)
    own = np.ascontiguousarray(xs[pi::2][:NJ].reshape(NJ * NT, D), dtype=np.float32)
    oth = np.ascontiguousarray(xs[1 - pi::2][:NJ].reshape(NJ * NT, D), dtype=np.float32)
    pc = np.zeros((128, 3), np.float32)
    pc[:, 0] = pi
    pc[:, 1] = 1 - pi
    pc[:, 2] = 0.0 if pi == 1 else NEG
    return {"x_own": own, "x_oth": oth, "picol": pc}


_CACHE = {}


def run(inputs, NJ=NJ_FULL, trace=False, stop=99):
    inp = {k: np.asarray(v) for k, v in inputs.items()}
    if (NJ, stop) not in _CACHE:
        _CACHE[(NJ, stop)] = build(NJ, stop)[0]
    nc = _CACHE[(NJ, stop)]
    sh = prep_shared(inp)
    in_maps = []
    for c in range(8):
        m = dict(sh)
        m.update(prep_core(inp["x"], c // 2, c % 2, NJ))
        in_maps.append(m)
    res = run_bass_kernel_spmd(nc, in_maps, core_ids=list(range(8)), trace=trace)
    out = np.zeros((4, 16, NT, D), np.float32)
    for c in range(8):
        b, pi = c // 2, c % 2
        o = np.asarray(res.results[c]["out_own"]).reshape(NJ, NT, D)
        out[b, pi::2][:NJ] = o
    return out.reshape(4, 16 * NT, D), res


def kernel(**inputs):
    out, _ = run(inputs)
    return out
```

```python
import numpy as np
import ml_dtypes
import contextlib
import concourse.bass as bass
import concourse.mybir as mybir
from concourse.bass_utils import run_bass_kernel_spmd

F32 = mybir.dt.float32
BF16 = mybir.dt.bfloat16
AF = mybir.ActivationFunctionType
ALU = mybir.AluOpType
AX = mybir.AxisListType

D = 2048
NT = 512
NJ_FULL = 8
TC = 8
NCH = NT // TC
NPAIR = 32
MAGIC = 12582912.0
TWO_PI = 6.283185307179586
NEG = -30000.0
EPS = 1e-6
SAME_ENG_SYNC = True
SPLIT_ROLES = 1
ARENA_WORDS = 53000


class Buf:
    __slots__ = ("name", "ws", "rs", "multi", "excl")

    def __init__(self, name, multi=False):
        self.name = name
        self.ws = []
        self.rs = []
        self.multi = multi
        self.excl = False


class DSem:
    def __init__(self, name, bulk=False):
        self.name = name
        self.count = 0
        self.last = None
        self.bulk = bulk
        self.handle = None


class Op:
    __slots__ = ("eng", "fn", "deps", "dsem", "ev", "signal", "idx")


ENGS = ("pe", "act", "dve", "pool", "sp")


class Sched:
    def __init__(self):
        self.by_eng = {e: [] for e in ENGS}
        self.dsems = []
        self.pending = {e: None for e in ENGS}

    def dsem(self, name, bulk=False):
        s = DSem(name, bulk)
        self.dsems.append(s)
        return s

    def barrier(self):
        lasts = [self.by_eng[e][-1] for e in ENGS if e != "sp" and self.by_eng[e]]
        for s in self.dsems:
            if s.last is not None:
                lasts.append(s.last)
        for e in ENGS:
            self.pending[e] = list(lasts)

    def add(self, eng, fn, reads=(), writes=(), dsem=None):
        op = Op()
        op.eng = eng
        op.fn = fn
        op.dsem = dsem
        op.signal = False
        op.ev = None
        deps = set()
        ex = [b for b in reads if b.excl]
        if ex:
            reads = [b for b in reads if not b.excl]
            writes = list(writes) + [b for b in ex if b not in writes]
        if self.pending[eng] is not None:
            deps.update(self.pending[eng])
            self.pending[eng] = None
        for b in reads:
            deps.update(b.ws)
        for b in writes:
            if not b.multi:
                deps.update(b.ws)
            deps.update(b.rs)
        for b in reads:
            b.rs.append(op)
        for b in writes:
            if b.multi:
                b.ws.append(op)
            else:
                b.ws = [op]
            b.rs = []
        if dsem is not None:
            if (not dsem.bulk) and dsem.last is not None:
                deps.add(dsem.last)
            dsem.last = op
            dsem.count += 16
            op.ev = dsem.count
        deps.discard(op)
        fd = []
        for d in deps:
            if d.eng == eng and d.dsem is None:
                if eng == "pe" or not SAME_ENG_SYNC:
                    continue
            fd.append(d)
            d.signal = True
        op.deps = fd
        self.by_eng[eng].append(op)
        return op

    def emit(self, nc, es):
        esem = {}
        for e in ("pe", "act", "dve", "pool"):
            esem[e] = es.enter_context(nc.semaphore("sem_" + e))
        for s in self.dsems:
            s.handle = es.enter_context(nc.semaphore("d_" + s.name))
        for e in ("pe", "act", "dve", "pool"):
            c = 0
            for op in self.by_eng[e]:
                if op.signal:
                    c += 1
                    op.ev = c
        final_waits = [(s.handle, s.count) for s in self.dsems if s.count > 0]
        block = es.enter_context(nc.Block())
        engs = {"pe": block.tensor, "act": block.scalar, "dve": block.vector,
                "pool": block.gpsimd, "sp": block.sync}

        def make(ename):
            ops = self.by_eng[ename]

            def body(eng):
                waited = {}
                for op in ops:
                    need = {}
                    for d in op.deps:
                        if d.dsem is not None:
                            key = ("d", id(d.dsem))
                            sem = d.dsem.handle
                            val = d.dsem.count if d.dsem.bulk else d.ev
                        else:
                            key = ("e", d.eng)
                            sem = esem[d.eng]
                            val = d.ev
                        if key not in need or need[key][1] < val:
                            need[key] = (sem, val)
                    for key, (sem, val) in need.items():
                        if waited.get(key, 0) >= val:
                            continue
                        waited[key] = val
                        eng.wait_ge(sem, val)
                    ins = op.fn(eng)
                    if op.dsem is not None:
                        ins.then_inc(op.dsem.handle, 16)
                    elif op.signal:
                        ins.then_inc(esem[ename], 1)
                if ename == "sp":
                    for sem, val in final_waits:
                        eng.wait_ge(sem, val)
            return body

        for ename, dec in engs.items():
            dec(make(ename))


class T:
    def __init__(self, ap, name, multi=False, sem=None):
        self.ap = ap
        self.b = Buf(name, multi)
        self.sem = sem

    def __getitem__(self, k):
        return self.ap[k]


class Ring:
    def __init__(self, items):
        self.items = items
        self.i = 0

    def next(self):
        it = self.items[self.i % len(self.items)]
        self.i += 1
        return it


def build(NJ=NJ_FULL, stop=99):
    nc = bass.Bass("TRN2", target_bir_lowering=False)
    S = Sched()
    NTOK = NJ * NT
    NSLOT = NJ * 2 * NT

    def din(name, shape, dt=F32):
        return nc.dram_tensor(name, list(shape), dt, kind="ExternalInput").ap()

    x_own = din("x_own", [NTOK, D])
    x_oth = din("x_oth", [NTOK, D])
    w_in = din("w_in", [D, 10240])
    w_glu = din("w_glu", [1024, 1024])
    w_pa = din("w_pa", [1024, 2048])
    w_ps = din("w_ps", [1024, 2048])
    w_out = din("w_out", [D, D])
    gain_col = din("gain_col", [128, 16])
    qg_in = din("qg", [128, 1])
    kg_in = din("kg", [128, 1])
    lamv_in = din("lamv", [128, 4, 64])
    subln_in = din("subln", [128, 1])
    dcol_in = din("dcol", [128, 8])
    bglu_in = din("bglu", [128, 8])
    pn_in = din("pn", [128, 3, NPAIR])
    bn_in = din("bn", [128, 2, NPAIR, 32])
    cn_in = din("cn", [128, 2, NPAIR, 32])
    pt_in = din("pt", [8, 128, 3, 128])
    bt_in = din("bt", [8, 128, 2, 128])
    ident_in = din("ident", [128, 128], BF16)
    ones_in = din("onesb", [128, 128], BF16)
    obd_in = din("onesbd", [128, 128])
    o128_in = din("ones128", [128, 128])
    kv_in = din("kvals", [128, TC + 1])
    negm_in = din("negmask", [128, 4, NT], BF16)
    pic_in = din("picol", [128, 3])
    out_own = nc.dram_tensor("out_own", [NTOK, D], F32, kind="ExternalOutput").ap()

    def dscr(name, shape, dt=BF16):
        return T(nc.dram_tensor(name, list(shape), dt).ap(), name, multi=True)

    win_s = dscr("win_s", [80, 128, 16, 128])
    wglu_s = dscr("wglu_s", [8, 128, 8, 128])
    wpa_s = dscr("wpa_s", [16, 128, 8, 128])
    wps_s = dscr("wps_s", [16, 128, 8, 128])
    wout_s = dscr("wout_s", [4, 128, 16, 512])
    bd_s = dscr("bd_s", [8, 128, TC, 128])
    ws_s = dscr("ws_s", [8, 128, TC, 2, 128])
    vw_s = dscr("vw_s", [8, 128, 4, TC, 2, 32])
    kt_s = dscr("kt_s", [8, 128, NSLOT])
    v_s = dscr("v_s", [8, 128, NSLOT // 128, 128])

    es = contextlib.ExitStack()
    with es:
        arena_h = es.enter_context(nc.sbuf_tensor("arena", [128, ARENA_WORDS], F32))
        aoff = [0]

        def sb(name, shape, dt=F32, sem=False):
            n = 1
            for s_ in shape[1:]:
                n *= s_
            words = n if dt == F32 else (n + 1) // 2
            words += words & 1
            assert aoff[0] + words <= ARENA_WORDS, (name, aoff[0], words)
            ap = arena_h[:, aoff[0]:aoff[0] + words]
            aoff[0] += words
            if dt == BF16:
                ap = ap.bitcast(BF16)
            ap = ap[:, 0:n]
            if len(shape) > 2:
                names = ["a%d" % i for i in range(len(shape) - 1)]
                kw = {nm: s_ for nm, s_ in zip(names, shape[1:])}
                ap = ap.rearrange("p (%s) -> p %s" % (" ".join(names), " ".join(names)), **kw)
            return T(ap, name, sem=S.dsem(name) if sem else None)

        psum_all = es.enter_context(nc.psum_tensor("psall", [128, 4096], F32))
        banks = [T(psum_all[:, 512 * i:512 * (i + 1)], "pb%d" % i) for i in range(8)]
        for bk_ in banks:
            bk_.b.excl = True
        gen_ring = Ring(banks)
        misc_ring = Ring(banks[6:8])

        def mm(out, lhsT, rhs, start, stop, reads, writes):
            return S.add("pe", lambda e: e.matmul(out, lhsT, rhs, start=start, stop=stop),
                         reads=reads, writes=writes)

        def act(out, in_, func, reads, writes, bias=None, scale=None, accum_out=None):
            kw = {}
            if bias is not None:
                kw["bias"] = bias
            if scale is not None:
                kw["scale"] = scale
            if accum_out is not None:
                kw["accum_out"] = accum_out
            return S.add("act", lambda e: e.activation(out, in_, func, **kw), reads=reads, writes=writes)

        def tt(eng, out, in0, in1, op, reads, writes):
            return S.add(eng, lambda e: e.tensor_tensor(out, in0, in1, op), reads=reads, writes=writes)

        def ts(eng, out, in0, s1, s2, op0, op1, reads, writes):
            if s2 is None:
                return S.add(eng, lambda e: e.tensor_scalar(out, in0, s1, None, op0), reads=reads, writes=writes)
            return S.add(eng, lambda e: e.tensor_scalar(out, in0, s1, s2, op0, op1), reads=reads, writes=writes)

        def stt(eng, out, in0, scalar, in1, op0, op1, reads, writes):
            return S.add(eng, lambda e: e.scalar_tensor_tensor(out, in0, scalar, in1, op0, op1),
                         reads=reads, writes=writes)

        def cp(eng, out, in_, reads, writes):
            if eng == "act":
                return S.add("act", lambda e: e.copy(out, in_), reads=reads, writes=writes)
            return S.add(eng, lambda e: e.tensor_copy(out, in_), reads=reads, writes=writes)

        def dma(out, in_, dsem, reads, writes):
            return S.add("sp", lambda e: e.dma_start(out=out, in_=in_), reads=reads, writes=writes, dsem=dsem)

        MUL, ADD, SUB = ALU.mult, ALU.add, ALU.subtract

        bulk = S.dsem("bulk", bulk=True)

        def cload(name, src, shape, dt=F32):
            t = sb("c_" + name, shape, dt)
            dma(t.ap, src, bulk, [], [t.b])
            return t

        gcol = cload("gain", gain_col, [128, 16])
        qg = cload("qg", qg_in, [128, 1])
        kg = cload("kg", kg_in, [128, 1])
        subln = cload("subln", subln_in, [128, 1])
        dcol = cload("dcol", dcol_in, [128, 8])
        bglu = cload("bglu", bglu_in, [128, 8])
        ident = cload("ident", ident_in, [128, 128], BF16)
        onesb = cload("onesb", ones_in, [128, 128], BF16)
        obd = cload("obd", obd_in, [128, 128])
        o128 = cload("o128", o128_in, [128, 128])
        negm = cload("negm", negm_in, [128, 4, NT], BF16)
        picol = cload("picol", pic_in, [128, 3])
        neglam = sb("neglam", [128, 1])
        qg8 = sb("qg8", [128, 1])
        sub08 = sb("sub08", [128, 1])
        SH = [128, 2, NPAIR]
        CA = sb("CA", SH)
        CB = sb("CB", SH)
        A5 = sb("A5", SH)
        A5A = sb("A5A", SH)
        A5B = sb("A5B", SH)
        Est = sb("Est", SH)
        Fst = sb("Fst", SH)
        Xin = sb("Xin", SH)
        Soth = sb("Soth", SH)
        Zst = sb("Zst", SH)
        pt1 = sb("pt1", SH)
        pt2 = sb("pt2", SH)
        pt3 = sb("pt3", SH)
        persist_end = aoff[0]

        lamv = cload("lamv", lamv_in, [128, 4, 64])
        pn = cload("pn", pn_in, [128, 3, NPAIR])
        bn = cload("bn", bn_in, [128, 2, NPAIR, 32])
        cn = cload("cn", cn_in, [128, 2, NPAIR, 32])
        kvals = cload("kvals", kv_in, [128, TC + 1])
        xt0 = [sb("xt%d" % i, [128, D], sem=True) for i in range(2)]
        xn0 = [sb("xn%d" % i, [128, D], BF16, sem=True) for i in range(2)]
        r0 = Ring([0, 1])

        def cast_weight(src, K, N, dst, ncol, use_gain):
            for kt in range(K // 128):
                for c0 in range(0, N, D):
                    w = min(D, N - c0)
                    i = r0.next()
                    a, an = xt0[i], xn0[i]
                    dma(a[:, 0:w], src[kt * 128:(kt + 1) * 128, c0:c0 + w], a.sem, [], [a.b])
                    if use_gain:
                        ts("dve", an[:, 0:w], a[:, 0:w], gcol[:, kt:kt + 1], None, MUL, None, [a.b, gcol.b], [an.b])
                    else:
                        cp("act", an[:, 0:w], a[:, 0:w], [a.b], [an.b])
                    nb = w // ncol
                    dsl = dst[c0 // ncol:c0 // ncol + nb, :, kt, :].rearrange("c p n -> p c n")
                    dma(dsl, an[:, 0:w].rearrange("p (c n) -> p c n", n=ncol), an.sem, [an.b], [dst.b])

        if stop >= 1:
            cast_weight(w_in, D, 10240, win_s, 128, True)
            cast_weight(w_glu, 1024, 1024, wglu_s, 128, False)
            cast_weight(w_pa, 1024, 2048, wpa_s, 128, False)
            cast_weight(w_ps, 1024, 2048, wps_s, 128, False)
            cast_weight(w_out, D, D, wout_s, 512, False)

        real_add = S.add
        if stop < 2:
            S.add = lambda *a_, **k_: None
        lam_init = 0.8 - 0.6 * 1.0
        ltmp = sb("ltmp", [128, 2, 64])
        lsum = sb("lsum", [128, 2])
        tt("dve", ltmp[:, 0, :], lamv[:, 0, :], lamv[:, 1, :], MUL, [lamv.b], [ltmp.b])
        tt("dve", ltmp[:, 1, :], lamv[:, 2, :], lamv[:, 3, :], MUL, [lamv.b], [ltmp.b])
        S.add("dve", lambda e: e.reduce_sum(lsum[:, :], ltmp[:, :, :], AX.X), reads=[ltmp.b], writes=[lsum.b])
        act(lsum[:, :], lsum[:, :], AF.Exp, [lsum.b], [lsum.b])
        stt("dve", neglam[:, :], lsum[:, 1:2], -lam_init, lsum[:, 0:1], ADD, SUB, [lsum.b], [neglam.b])
        ts("dve", qg8[:, :], qg[:, :], 0.125, None, MUL, None, [qg.b], [qg8.b])
        ts("dve", sub08[:, :], subln[:, :], 1.0 - lam_init, None, MUL, None, [subln.b], [sub08.b])

        def trig_alloc(prefix, n):
            sh3 = [128, n, TC + 1]
            sh2 = [128, n]
            d = {"n": n}
            for nm in ("arg", "t1", "t2", "PR", "PI"):
                d[nm] = sb(prefix + nm, sh3)
            for nm in ("dt", "lrdt", "ang", "nr", "den", "u1", "cr", "ci"):
                d[nm] = sb(prefix + nm, sh2)
            return d

        def trig_tables(d, lre, lim, ldt, rd):
            n = d["n"]
            sh3 = [128, n, TC + 1]
            arg, t1, t2, PR, PI = d["arg"], d["t1"], d["t2"], d["PR"], d["PI"]
            dtt, lrdt, ang, nr, den, u1, cr, ci = (d[k] for k in ("dt", "lrdt", "ang", "nr", "den", "u1", "cr", "ci"))
            act(dtt[:, :], ldt, AF.Exp, rd, [dtt.b])
            tt("dve", lrdt[:, :], lre, dtt[:, :], MUL, rd + [dtt.b], [lrdt.b])
            tt("dve", ang[:, :], lim, dtt[:, :], MUL, rd + [dtt.b], [ang.b])
            kb = kvals[:, :].unsqueeze(1).to_broadcast(sh3)
            tt("dve", arg[:, :, :], lrdt[:, :].unsqueeze(2).to_broadcast(sh3), kb, MUL, [lrdt.b, kvals.b], [arg.b])
            act(PR[:, :, :], arg[:, :, :], AF.Exp, [arg.b], [PR.b])
            tt("dve", arg[:, :, :], ang[:, :].unsqueeze(2).to_broadcast(sh3), kb, MUL, [ang.b, kvals.b], [arg.b])

            def reduce_sin(dst, shift):
                ts("dve", t2[:, :, :], arg[:, :, :], shift, None, ADD, None, [arg.b], [t2.b])
                ts("dve", t1[:, :, :], t2[:, :, :], 1.0 / TWO_PI, MAGIC, MUL, ADD, [t2.b], [t1.b])
                ts("dve", t1[:, :, :], t1[:, :, :], MAGIC, None, SUB, None, [t1.b], [t1.b])
                stt("dve", t1[:, :, :], t1[:, :, :], -TWO_PI, t2[:, :, :], MUL, ADD, [t1.b, t2.b], [t1.b])
                ts("dve", t1[:, :, :], t1[:, :, :], 3.1415925, -3.1415925, ALU.min, ALU.max, [t1.b], [t1.b])
                act(dst[:, :, :], t1[:, :, :], AF.Sin, [t1.b], [dst.b])

            reduce_sin(PI, 0.0)
            tt("dve", PI[:, :, :], PI[:, :, :], PR[:, :, :], MUL, [PI.b, PR.b], [PI.b])
            reduce_sin(t1, 1.5707963267948966)
            tt("dve", PR[:, :, :], PR[:, :, :], t1[:, :, :], MUL, [PR.b, t1.b], [PR.b])
            ts("dve", nr[:, :], PR[:, :, 1], -1.0, None, ADD, None, [PR.b], [nr.b])
            tt("dve", den[:, :], lre, lre, MUL, rd, [den.b])
            tt("dve", u1[:, :], lim, lim, MUL, rd, [u1.b])
            tt("dve", den[:, :], den[:, :], u1[:, :], ADD, [den.b, u1.b], [den.b])
            S.add("dve", lambda e: e.reciprocal(den[:, :], den[:, :]), reads=[den.b], writes=[den.b])
            tt("dve", cr[:, :], nr[:, :], lre, MUL, rd + [nr.b], [cr.b])
            tt("dve", u1[:, :], PI[:, :, 1], lim, MUL, rd + [PI.b], [u1.b])
            tt("dve", cr[:, :], cr[:, :], u1[:, :], ADD, [cr.b, u1.b], [cr.b])
            tt("dve", cr[:, :], cr[:, :], den[:, :], MUL, [cr.b, den.b], [cr.b])
            tt("dve", ci[:, :], PI[:, :, 1], lre, MUL, rd + [PI.b], [ci.b])
            tt("dve", u1[:, :], nr[:, :], lim, MUL, rd + [nr.b], [u1.b])
            tt("dve", ci[:, :], ci[:, :], u1[:, :], SUB, [ci.b, u1.b], [ci.b])
            tt("dve", ci[:, :], ci[:, :], den[:, :], MUL, [ci.b, den.b], [ci.b])

        dn = trig_alloc("n_", NPAIR)
        trig_tables(dn, pn[:, 0, :], pn[:, 1, :], pn[:, 2, :], [pn.b])
        PRn, PIn, crn, cin_ = dn["PR"], dn["PI"], dn["cr"], dn["ci"]

        def pack_coef(A, CAt, CBt):
            cp("dve", CAt[:, 0, :], A[:, 0, :], [A.b], [CAt.b])
            cp("dve", CAt[:, 1, :], A[:, 0, :], [A.b], [CAt.b])
            ts("dve", CBt[:, 0, :], A[:, 1, :], -1.0, None, MUL, None, [A.b], [CBt.b])
            cp("dve", CBt[:, 1, :], A[:, 1, :], [A.b], [CBt.b])

        cp("dve", A5[:, 0, :], PRn[:, :, TC], [PRn.b], [A5.b])
        cp("dve", A5[:, 1, :], PIn[:, :, TC], [PIn.b], [A5.b])
        pack_coef(A5, CA, CB)
        for _ in range(6):
            tt("dve", pt1[:, :, :], A5[:, :, :], A5[:, :, :], MUL, [A5.b], [pt1.b])
            tt("dve", pt2[:, 0, :], A5[:, 0, :], A5[:, 1, :], MUL, [A5.b], [pt2.b])
            tt("dve", A5[:, 0, :], pt1[:, 0, :], pt1[:, 1, :], SUB, [pt1.b], [A5.b])
            ts("dve", A5[:, 1, :], pt2[:, 0, :], 2.0, None, MUL, None, [pt2.b], [A5.b])
        pack_coef(A5, A5A, A5B)
        S.add("pool", lambda e: e.memset(Est[:, :, :], 0.0), writes=[Est.b])
        S.add("pool", lambda e: e.memset(Zst[:, :, :], 0.0), writes=[Zst.b])

        sh4 = [128, NPAIR, 32]
        bbn = sb("bbn", [128, 2, NPAIR, 32])
        w1 = sb("w1", sh4)
        g2 = sb("g2", sh4)
        gre = sb("gre", [128, NPAIR, 64])
        gim = sb("gim", [128, NPAIR, 64])
        S.add("pool", lambda e: e.memset(gre[:, :, :], 0.0), writes=[gre.b])
        S.add("pool", lambda e: e.memset(gim[:, :, :], 0.0), writes=[gim.b])
        ncim = sb("ncim", sh4)
        crb = crn[:, :].unsqueeze(2).to_broadcast(sh4)
        cib = cin_[:, :].unsqueeze(2).to_broadcast(sh4)
        tt("dve", bbn[:, 0, :, :], bn[:, 0, :, :], crb, MUL, [bn.b, crn.b], [bbn.b])
        tt("dve", w1[:, :, :], bn[:, 1, :, :], cib, MUL, [bn.b, cin_.b], [w1.b])
        tt("dve", bbn[:, 0, :, :], bbn[:, 0, :, :], w1[:, :, :], SUB, [bbn.b, w1.b], [bbn.b])
        tt("dve", bbn[:, 1, :, :], bn[:, 1, :, :], crb, MUL, [bn.b, crn.b], [bbn.b])
        tt("dve", w1[:, :, :], bn[:, 0, :, :], cib, MUL, [bn.b, cin_.b], [w1.b])
        tt("dve", bbn[:, 1, :, :], bbn[:, 1, :, :], w1[:, :, :], ADD, [bbn.b, w1.b], [bbn.b])
        ts("dve", ncim[:, :, :], cn[:, 1, :, :], -1.0, None, MUL, None, [cn.b], [ncim.b])

        vw = sb("vw", [128, NPAIR, TC, 2, 32], BF16, sem=True)
        for i in range(TC):
            prb = PRn[:, :, i + 1].unsqueeze(2).to_broadcast(sh4)
            pib = PIn[:, :, i + 1].unsqueeze(2).to_broadcast(sh4)
            tt("dve", w1[:, :, :], cn[:, 0, :, :], prb, MUL, [cn.b, PRn.b], [w1.b])
            tt("dve", g2[:, :, :], ncim[:, :, :], pib, MUL, [ncim.b, PIn.b], [g2.b])
            tt("dve", vw[:, :, i, 0, :], w1[:, :, :], g2[:, :, :], ADD, [w1.b, g2.b], [vw.b])
            tt("dve", w1[:, :, :], cn[:, 0, :, :], pib, MUL, [cn.b, PIn.b], [w1.b])
            tt("dve", g2[:, :, :], ncim[:, :, :], prb, MUL, [ncim.b, PRn.b], [g2.b])
            tt("dve", vw[:, :, i, 1, :], g2[:, :, :], w1[:, :, :], SUB, [w1.b, g2.b], [vw.b])
        for tau in range(8):
            dma(vw_s[tau], vw[:, 4 * tau:4 * tau + 4, :, :, :], vw.sem, [vw.b], [vw_s.b])

        bd = sb("bd", [128, 8, TC, 128], BF16, sem=True)
        S.add("pool", lambda e: e.memset(bd[:, :, :, :], 0.0), writes=[bd.b])
        for l in range(TC):
            prb = PRn[:, :, l].unsqueeze(2).to_broadcast(sh4)
            pib = PIn[:, :, l].unsqueeze(2).to_broadcast(sh4)
            tt("dve", gre[:, :, 32:64], bbn[:, 0, :, :], prb, MUL, [bbn.b, PRn.b], [gre.b])
            tt("dve", g2[:, :, :], bbn[:, 1, :, :], pib, MUL, [bbn.b, PIn.b], [g2.b])
            tt("dve", gre[:, :, 32:64], gre[:, :, 32:64], g2[:, :, :], SUB, [gre.b, g2.b], [gre.b])
            tt("dve", gim[:, :, 32:64], bbn[:, 0, :, :], pib, MUL, [bbn.b, PIn.b], [gim.b])
            tt("dve", g2[:, :, :], bbn[:, 1, :, :], prb, MUL, [bbn.b, PRn.b], [g2.b])
            tt("dve", gim[:, :, 32:64], gim[:, :, 32:64], g2[:, :, :], ADD, [gim.b, g2.b], [gim.b])
            for tau in range(8):
                bk = gen_ring.next()
                for q in range(4):
                    P = 4 * tau + q
                    if q < 3:
                        o = bk[32 * q:32 * q + 32, 32 * q:32 * q + 32]
                        c0_ = 32
                    else:
                        o = bk[64:128, 96:128]
                        c0_ = 0
                    mm(o, gre[:, P, c0_:64], cn[:, 0, P, :], True, False, [gre.b, cn.b], [bk.b])
                    mm(o, gim[:, P, c0_:64], ncim[:, P, :], False, True, [gim.b, ncim.b], [bk.b])
                for q in range(3):
                    cp("dve", bd[32 * q:32 * q + 32, tau, l, 32 * q:32 * q + 32],
                       bk[32 * q:32 * q + 32, 32 * q:32 * q + 32], [bk.b], [bd.b])
                cp("dve", bd[64:128, tau, l, 96:128], bk[64:128, 96:128], [bk.b], [bd.b])
        for tau in range(8):
            dma(bd_s[tau], bd[:, tau, :, :], bd.sem, [bd.b], [bd_s.b])

        ptl = sb("ptl", [128, 3, 128], sem=True)
        btl = sb("btl", [128, 2, 128], sem=True)
        wsw0 = [sb("wsw0_%d" % i, [128, TC, 2, 128], BF16, sem=True) for i in range(2)]
        bbt = sb("bbt", [128, 2, 128])
        w2 = sb("w2", [128, 128])
        w3 = sb("w3", [128, 128])
        dt_ = trig_alloc("t_", 128)
        for tau in range(8):
            dma(ptl[:, :, :], pt_in[tau], ptl.sem, [], [ptl.b])
            dma(btl[:, :, :], bt_in[tau], btl.sem, [], [btl.b])
            trig_tables(dt_, ptl[:, 0, :], ptl[:, 1, :], ptl[:, 2, :], [ptl.b])
            PRt, PIt, crt, cit = dt_["PR"], dt_["PI"], dt_["cr"], dt_["ci"]
            tt("dve", bbt[:, 0, :], btl[:, 0, :], crt[:, :], MUL, [btl.b, crt.b], [bbt.b])
            tt("dve", w2[:, :], btl[:, 1, :], cit[:, :], MUL, [btl.b, cit.b], [w2.b])
            tt("dve", bbt[:, 0, :], bbt[:, 0, :], w2[:, :], SUB, [bbt.b, w2.b], [bbt.b])
            tt("dve", bbt[:, 1, :], btl[:, 1, :], crt[:, :], MUL, [btl.b, crt.b], [bbt.b])
            tt("dve", w2[:, :], btl[:, 0, :], cit[:, :], MUL, [btl.b, cit.b], [w2.b])
            tt("dve", bbt[:, 1, :], bbt[:, 1, :], w2[:, :], ADD, [bbt.b, w2.b], [bbt.b])
            ws = wsw0[tau % 2]
            for j in range(TC):
                k = TC - 1 - j
                tt("dve", w2[:, :], bbt[:, 0, :], PRt[:, :, k], MUL, [bbt.b, PRt.b], [w2.b])
                tt("dve", w3[:, :], bbt[:, 1, :], PIt[:, :, k], MUL, [bbt.b, PIt.b], [w3.b])
                tt("dve", ws[:, j, 0, :], w2[:, :], w3[:, :], SUB, [w2.b, w3.b], [ws.b])
                tt("dve", w2[:, :], bbt[:, 0, :], PIt[:, :, k], MUL, [bbt.b, PIt.b], [w2.b])
                tt("dve", w3[:, :], bbt[:, 1, :], PRt[:, :, k], MUL, [bbt.b, PRt.b], [w3.b])
                tt("dve", ws[:, j, 1, :], w2[:, :], w3[:, :], ADD, [w2.b, w3.b], [ws.b])
            dma(ws_s[tau], ws[:, :, :, :], ws.sem, [ws.b], [ws_s.b])

        S.add = real_add
        S.barrier()
        aoff[0] = persist_end
        hT = {"own": sb("hT_own", [128, 16, NT], BF16), "oth": sb("hT_oth", [128, 16, NT], BF16)}
        mT = hT["oth"]
        uTc = sb("uTc", [128, 8, 2, NT], BF16)
        uT = {"oth": T(uTc[:, :, 0, :], "uT_oth"), "own": T(uTc[:, :, 1, :], "uT_own")}
        uT["oth"].b = uTc.b
        uT["own"].b = uTc.b
        gT = uT["oth"]
        ybuf = sb("ybuf", [128, 4096])
        yssm = T(ybuf[:, 0:2048].bitcast(BF16).rearrange("p (a n) -> p a n", n=NT), "yssm")
        yatt = T(ybuf[:, 2048:4096].bitcast(BF16).rearrange("p (a n) -> p a n", n=NT), "yatt")
        Sown = T(hT["oth"].ap.rearrange("p a n -> p (a n)").bitcast(F32)
                 .rearrange("p (c r q) -> p c r q", r=2, q=NPAIR), "Sown")
        Sown.b = hT["oth"].b
        wt_ring = Ring([sb("wt%d" % i, [128, 16, 128], BF16, sem=True) for i in range(3)])
        wg = sb("wg", [128, 16, 512], BF16, sem=True)
        ktp_ring = Ring([sb("ktp%d" % i, [128, 1024], BF16, sem=True) for i in range(4)])
        vp_ring = Ring([sb("vp%d" % i, [128, 8, 128], BF16, sem=True) for i in range(4)])
        p_ring = Ring([sb("P%d" % i, [128, 2, NT], BF16) for i in range(3)])
        pacc = [sb("pacc%d" % i, [128, 2, NT]) for i in range(2)]
        qT = sb("qT", [128, NT], BF16)
        Sbuf = sb("Sbuf", [128, NCH, 2, NPAIR])
        Xh = sb("Xh", [128, 2, NPAIR, NCH], BF16)
        bdw_ring = Ring([sb("bdw%d" % i, [128, TC, 128], BF16, sem=True) for i in range(1)])
        wsw_ring = Ring([sb("wsw%d" % i, [128, TC, 2, 128], BF16, sem=True) for i in range(1)])
        vww_ring = Ring([sb("vww%d" % i, [128, 4, TC, 2, 32], BF16, sem=True) for i in range(1)])
        wsw3 = sb("wsw3", [128, TC, 2, 128], BF16, sem=True)
        vw3 = sb("vw3", [128, TC, 2, 64], BF16, sem=True)
        S.add("pool", lambda e: e.memset(wsw3[64:96, :, :, :], 0.0), writes=[wsw3.b])
        S.add("pool", lambda e: e.memset(vw3[:, :, :, 0:32], 0.0), writes=[vw3.b])
        vw2 = sb("vw2", [128, TC, 2, 64], BF16, sem=True)
        S.add("pool", lambda e: e.memset(vw2[:, :, :, 32:64], 0.0), writes=[vw2.b])
        xt = sb("xt", [128, D], sem=True)
        xn = sb("xn", [128, D], BF16)
        wk_ring = Ring([sb("wk%d" % i, [128, NT]) for i in range(4)])
        xres_ring = Ring([sb("xres%d" % i, [128, NT], sem=True) for i in range(2)])
        ot_ring = Ring([sb("ot%d" % i, [128, NT], sem=True) for i in range(2)])
        ktile_ring = Ring([sb("ktile%d" % i, [128, NT], BF16, sem=True) for i in range(2)])
        vtile_ring = Ring([sb("vtile%d" % i, [128, NT], BF16, sem=True) for i in range(2)])
        ssq = sb("ssq", [128, 1])
        rstd = sb("rstd", [128, 1])

        evac_rr = [0]

        def evac_eng():
            evac_rr[0] += 1
            return "dve" if evac_rr[0] % 2 else "act"

        def load_wt(src, ct):
            w = wt_ring.next()
            nk = src.ap.shape[2]
            dma(w[:, 0:nk, :], src[ct], w.sem, [src.b], [w.b])
            return w

        def proj_fm(src, ct, hTt, ring, nkt=16):
            w = load_wt(src, ct)
            bk = ring.next()
            for kt in range(nkt):
                mm(bk[:, :], w[:, kt, :], hTt[:, kt, :], kt == 0, kt == nkt - 1, [w.b, hTt.b], [bk.b])
            return bk

        def run_jobs(jobs, ring, hook=None):
            slots = {}

            def load(i):
                if i < len(jobs) and i not in slots:
                    slots[i] = load_wt(jobs[i][0], jobs[i][1])

            load(0)
            load(1)
            for i, (src, ct, rhsT, nkt, epi) in enumerate(jobs):
                load(i + 2)
                w = slots.pop(i)
                bk = ring.next()
                for kt in range(nkt):
                    mm(bk[:, :], w[:, kt, :], rhsT[:, kt, :], kt == 0, kt == nkt - 1, [w.b, rhsT.b], [bk.b])
                epi(bk)
                if hook is not None and i % 2 == 1:
                    next(hook, None)

        def rms_to_hT(role, J, ring):
            for _ in rms_gen(role, J, ring):
                pass

        def rms_gen(role, J, ring):
            xsrc = x_own if role == "own" else x_oth
            h = hT[role]
            for tb in range(4):
                r0_ = J * NT + tb * 128
                dma(xt[:, :], xsrc[r0_:r0_ + 128, :], xt.sem, [], [xt.b])
                act(xn[:, :], xt[:, :], AF.Square, [xt.b], [xn.b, ssq.b], accum_out=ssq[:, :])
                ts("dve", rstd[:, :], ssq[:, :], 1.0 / D, EPS, MUL, ADD, [ssq.b], [rstd.b])
                act(rstd[:, :], rstd[:, :], AF.Sqrt, [rstd.b], [rstd.b])
                S.add("dve", lambda e: e.reciprocal(rstd[:, :], rstd[:, :]), reads=[rstd.b], writes=[rstd.b])
                ts("dve", xn[:, :], xt[:, :], rstd[:, 0:1], None, MUL, None, [xt.b, rstd.b], [xn.b])
                yield
                for half in range(2):
                    bk = ring.next()
                    bkb = bk.ap.bitcast(BF16)
                    for k in range(8):
                        kt = half * 8 + k
                        o_, i_ = bkb[:, k * 128:(k + 1) * 128], xn[:, kt * 128:(kt + 1) * 128]
                        S.add("pe", lambda e, o_=o_, i_=i_: e.transpose(o_, i_, ident[:, :]),
                              reads=[xn.b, ident.b], writes=[bk.b])
                    cp(evac_eng(), h[:, half * 8:(half + 1) * 8, tb * 128:(tb + 1) * 128],
                       bkb.rearrange("p (k n) -> p k n", n=128), [bk.b], [h.b])
                yield

        def qknorm(bk, gcol_t, dst_ap, dst_b, ring):
            sq = wk_ring.next()
            act(sq[:, :], bk[:, :], AF.Square, [bk.b], [sq.b])
            b2 = ring.next()
            mm(b2[:, :], obd[:, :], sq[:, :], True, True, [obd.b, sq.b], [b2.b])
            rs = wk_ring.next()
            ts("dve", rs[:, :], b2[:, :], EPS, None, ADD, None, [b2.b], [rs.b])
            act(rs[:, :], rs[:, :], AF.Ln, [rs.b], [rs.b])
            act(rs[:, :], rs[:, :], AF.Exp, [rs.b], [rs.b], scale=-0.5)
            stt("dve", dst_ap, bk[:, :], gcol_t[:, 0:1], rs[:, :], MUL, MUL, [bk.b, gcol_t.b, rs.b], [dst_b])

        def stage_kvu(role, J, ring, hook=None):
            ridx = 0 if role == "own" else 1
            slot0 = J * 2 * NT + ridx * NT
            h = hT[role]
            jobs = []

            def epi_k(hd):
                def f(bk):
                    kt_ = ktile_ring.next()
                    qknorm(bk, kg, kt_[:, :], kt_.b, ring)
                    dma(kt_s[hd, :, slot0:slot0 + NT], kt_[:, :], kt_.sem, [kt_.b], [kt_s.b])
                return f

            def epi_u(tau):
                def f(bk):
                    cp(evac_eng(), uT[role][:, tau, :], bk[:, :], [bk.b], [uT[role].b])
                return f

            for hd in range(8):
                jobs.append((win_s, 8 + hd, h, 16, epi_k(hd)))
            for tau in range(8):
                jobs.append((win_s, 32 + tau, h, 16, epi_u(tau)))
            if hook is not None:
                next(hook, None)
            run_jobs(jobs, ring, hook)
            if hook is not None:
                for _ in hook:
                    pass
            for grp in range(2):
                dma(wg.ap.rearrange("p k (c n) -> p k c n", n=128),
                    win_s[16 + 4 * grp:20 + 4 * grp].rearrange("c p k n -> p k c n"), wg.sem, [win_s.b], [wg.b])
                for tb in range(4):
                    bk = ring.next()
                    for kt in range(16):
                        mm(bk[:, :], h[:, kt, tb * 128:(tb + 1) * 128], wg[:, kt, :], kt == 0, kt == 15,
                           [h.b, wg.b], [bk.b])
                    vt = vtile_ring.next()
                    cp(evac_eng(), vt[:, :], bk[:, :], [bk.b], [vt.b])
                    blk = slot0 // 128 + tb
                    dma(v_s[4 * grp:4 * grp + 4, :, blk, :].rearrange("h p e -> p h e"),
                        vt[:, :].rearrange("p (h e) -> p h e", e=128), vt.sem, [vt.b], [v_s.b])

        def cmul_add(dst, X, Aa, Ab, addend):
            tt("pool", pt1[:, :, :], X[:, :, :], Aa[:, :, :], MUL, [X.b, Aa.b], [pt1.b])
            tt("pool", pt2[:, 0, :], X[:, 1, :], Ab[:, 0, :], MUL, [X.b, Ab.b], [pt2.b])
            tt("pool", pt2[:, 1, :], X[:, 0, :], Ab[:, 1, :], MUL, [X.b, Ab.b], [pt2.b])
            tt("pool", pt1[:, :, :], pt1[:, :, :], pt2[:, :, :], ADD, [pt1.b, pt2.b], [pt1.b])
            tt("pool", dst[:, :, :], pt1[:, :, :], addend[:, :, :], ADD, [pt1.b, addend.b], [dst.b])

        def ssm_smat(J, ring):
            for tau in range(8):
                wsw = wsw_ring.next()
                dma(wsw[:, :, :, :], ws_s[tau], wsw.sem, [ws_s.b], [wsw.b])
                dma(wsw3[96:128, :, :, :], ws_s[tau, 96:128], wsw3.sem, [ws_s.b], [wsw3.b])
                bks = [ring.next() for _ in range(4)]
                for q in range(4):
                    bk = bks[q]
                    for ri in range(2):
                      for ro in range(SPLIT_ROLES):
                        if SPLIT_ROLES == 1:
                            o = bk[:, ri * 2 * NCH:(ri + 1) * 2 * NCH]
                        else:
                            o = bk[:, (ri * 2 + ro) * NCH:(ri * 2 + ro + 1) * NCH]
                        for j in range(TC):
                            if q < 3:
                                uu = uTc[32 * q:32 * q + 32, tau, :, :]
                                ww = wsw[32 * q:32 * q + 32, j, ri, :]
                            else:
                                uu = uTc[64:128, tau, :, :]
                                ww = wsw3[64:128, j, ri, :]
                            if SPLIT_ROLES == 1:
                                rhs_ = uu.rearrange("p o n -> p (o n)")[:, j::TC]
                            else:
                                rhs_ = uu[:, ro, j::TC]
                            mm(o, ww, rhs_, j == 0, j == TC - 1, [wsw.b, wsw3.b, uTc.b], [bk.b])
                for q in range(4):
                    src = bks[q][:, 0:4 * NCH].rearrange("p (r o c) -> p r o c", r=2, o=2)
                    cp(evac_eng(), Sbuf[:, :, :, 4 * tau + q].rearrange("p c r -> p r c"),
                       src[:, :, 0, :], [bks[q].b], [Sbuf.b])
                    cp(evac_eng(), Sown[:, :, :, 4 * tau + q].rearrange("p c r -> p r c"),
                       src[:, :, 1, :], [bks[q].b], [Sown.b])

        def ssm_scan(role, J):
            Sb = Sbuf if role == "oth" else Sown
            sbufs = [Sbuf.b] if role == "oth" else [Sown.b]
            if role == "oth":
                x0 = Zst
            else:
                cmul_add(pt3, Est, A5A, A5B, Soth)
                ts("pool", Xin[:, :, :], Est[:, :, :], picol[:, 1:2], None, MUL, None, [Est.b, picol.b], [Xin.b])
                ts("pool", pt3[:, :, :], pt3[:, :, :], picol[:, 0:1], None, MUL, None, [pt3.b, picol.b], [pt3.b])
                tt("pool", Xin[:, :, :], Xin[:, :, :], pt3[:, :, :], ADD, [Xin.b, pt3.b], [Xin.b])
                x0 = Xin
            for c in range(NCH):
                rbs = [x0.b] if c == 0 else sbufs
                tt("pool", pt1[:, :, :], (x0[:, :, :] if c == 0 else Sb[:, c - 1, :, :]), CA[:, :, :], MUL,
                   rbs + [CA.b], [pt1.b])
                xim = x0[:, 1, :] if c == 0 else Sb[:, c - 1, 1, :]
                xre = x0[:, 0, :] if c == 0 else Sb[:, c - 1, 0, :]
                tt("pool", pt2[:, 0, :], xim, CB[:, 0, :], MUL, rbs + [CB.b], [pt2.b])
                tt("pool", pt2[:, 1, :], xre, CB[:, 1, :], MUL, rbs + [CB.b], [pt2.b])
                tt("pool", pt1[:, :, :], pt1[:, :, :], pt2[:, :, :], ADD, [pt1.b, pt2.b], [pt1.b])
                tt("pool", Sb[:, c, :, :], Sb[:, c, :, :], pt1[:, :, :], ADD, sbufs + [pt1.b], sbufs)
            if role == "oth":
                cp("pool", Soth[:, :, :], Sb[:, NCH - 1, :, :], sbufs, [Soth.b])
            else:
                cp("pool", Fst[:, :, :], Sb[:, NCH - 1, :, :], sbufs, [Fst.b])
                cp("pool", Xh[:, :, :, 0], Xin[:, :, :], [Xin.b], [Xh.b])
                cp("pool", Xh[:, :, :, 1:NCH], Sb[:, 0:NCH - 1, :, :].rearrange("p c r q -> p r q c"),
                   sbufs, [Xh.b])
                cmul_add(pt3, Fst, A5A, A5B, Soth)
                ts("pool", Est[:, :, :], Fst[:, :, :], picol[:, 0:1], None, MUL, None, [Fst.b, picol.b], [Est.b])
                ts("pool", pt3[:, :, :], pt3[:, :, :], picol[:, 1:2], None, MUL, None, [pt3.b, picol.b], [pt3.b])
                tt("pool", Est[:, :, :], Est[:, :, :], pt3[:, :, :], ADD, [Est.b, pt3.b], [Est.b])

        def ssm_out(J, ring):
            u = uT["own"]
            for tau in range(8):
                bdw = bdw_ring.next()
                vww = vww_ring.next()
                dma(bdw[:, :, :], bd_s[tau], bdw.sem, [bd_s.b], [bdw.b])
                dma(vww[:, :, :, :, :], vw_s[tau], vww.sem, [vw_s.b], [vww.b])
                dma(vw3[:, :, :, 32:64], vw_s[tau, :, 3, :, :, :], vw3.sem, [vw_s.b], [vw3.b])
                dma(vw2[:, :, :, 0:32], vw_s[tau, :, 2, :, :, :], vw2.sem, [vw_s.b], [vw2.b])
                bk = ring.next()
                bkv = bk[:, :].rearrange("p (c i) -> p c i", i=TC)
                uv = u[:, tau, :].rearrange("p (c i) -> p c i", i=TC)
                for i in range(TC):
                    for j in range(i + 1):
                        mm(bkv[:, :, i], bdw[:, i - j, :], uv[:, :, j], (i == 0 and j == 0), False,
                           [bdw.b, u.b], [bk.b])
                for i in range(TC):
                    for q in range(4):
                        for ri in range(2):
                            if q < 2:
                                mm(bkv[32 * q:32 * q + 32, :, i], vww[:, q, i, ri, :], Xh[:, ri, 4 * tau + q, :],
                                   False, (i == TC - 1 and ri == 1), [vww.b, Xh.b], [bk.b])
                            else:
                                vz = vw2 if q == 2 else vw3
                                mm(bkv[64:128, :, i], vz[:, i, ri, :], Xh[:, ri, 4 * tau + q, :],
                                   False, (i == TC - 1 and q == 3 and ri == 1), [vz.b, Xh.b], [bk.b])
                yp = wk_ring.next()
                stt("dve", yp[:, :], u[:, tau, :], dcol[:, tau:tau + 1], bk[:, :], MUL, ADD,
                    [u.b, dcol.b, bk.b], [yp.b])
                gw = wk_ring.next()
                act(gw[:, :], yp[:, :], AF.Square, [yp.b], [gw.b])
                ts("dve", gw[:, :], gw[:, :], 0.044715, 1.0, MUL, ADD, [gw.b], [gw.b])
                tt("dve", gw[:, :], gw[:, :], yp[:, :], MUL, [gw.b, yp.b], [gw.b])
                act(gw[:, :], gw[:, :], AF.Sigmoid, [gw.b], [gw.b], scale=1.5957691216057308)
                tt("dve", gT[:, tau, :], yp[:, :], gw[:, :], MUL, [yp.b, gw.b], [gT.b])

        def glu_stage(J, ring):
            jobs = []
            st = {}

            def epi_g(n):
                def f(b1):
                    sg = wk_ring.next()
                    act(sg[:, :], b1[:, :], AF.Sigmoid, [b1.b, bglu.b], [sg.b], bias=bglu[:, n:n + 1])
                    st[n] = sg
                return f

            def epi_z(n):
                def f(b2):
                    sg = st[n]
                    sz = wk_ring.next()
                    act(sz[:, :], b2[:, :], AF.Silu, [b2.b], [sz.b])
                    tt("dve", sg[:, :], sg[:, :], sz[:, :], MUL, [sg.b, sz.b], [sg.b])
                    tt("dve", yssm[:, n, :], sg[:, :], gT[:, n, :], MUL, [sg.b, gT.b], [yssm.b])
                return f

            for n in range(8):
                jobs.append((wglu_s, n, gT, 8, epi_g(n)))
                jobs.append((win_s, 40 + n, hT["own"], 16, epi_z(n)))
            run_jobs(jobs, ring)

        def attention(J):
            h = hT["own"]
            sp_ring = Ring([0, 2])
            Ob = banks[4:6]
            acc_eng = ("dve", "dve")
            wq_next = load_wt(win_s, 0)
            for hd in range(8):
                wq = wq_next
                wz = load_wt(win_s, 24 + hd)
                if hd < 7:
                    wq_next = load_wt(win_s, hd + 1)
                pieces = {}

                def ensure(jp, hd=hd, pieces=pieces):
                    if jp > J or jp in pieces:
                        return
                    ktp = ktp_ring.next()
                    vp = vp_ring.next()
                    dma(ktp[:, :], kt_s[hd, :, jp * 1024:(jp + 1) * 1024], ktp.sem, [kt_s.b], [ktp.b])
                    dma(vp[:, :, :], v_s[hd, :, jp * 8:(jp + 1) * 8, :], vp.sem, [v_s.b], [vp.b])
                    pieces[jp] = (ktp, vp)

                ensure(0)
                ensure(1)
                bq = misc_ring.next()
                for kt in range(16):
                    mm(bq[:, :], wq[:, kt, :], h[:, kt, :], kt == 0, kt == 15, [wq.b, h.b], [bq.b])
                qknorm(bq, qg8, qT[:, :], qT.b, misc_ring)
                units = []
                for jp in range(J + 1):
                    for blk in range(8):
                        units.append((jp, blk))
                nu = len(units)

                def issue_s(n):
                    jp, blk = units[n]
                    ensure(jp)
                    if blk == 0:
                        ensure(jp + 2)
                    ktp, vp = pieces[jp]
                    bA = sp_ring.next()
                    diag = (jp == J and blk < 4)
                    for c in range(2):
                        sbk = banks[bA + c]
                        mm(sbk[:, :], ktp[64 * c:64 * c + 64, blk * 128:(blk + 1) * 128], qT[64 * c:64 * c + 64, :],
                           True, not diag, [ktp.b, qT.b], [sbk.b])
                        if diag:
                            mm(sbk[:, :], ident[:, :], negm[:, blk, :], False, True, [ident.b, negm.b], [sbk.b])
                    p = p_ring.next()
                    src = psum_all[:, 512 * bA:512 * bA + 1024]
                    rd = [banks[bA].b, banks[bA + 1].b]
                    if jp == J and blk >= 4:
                        act(p[:, :, :], src.rearrange("p (c n) -> p c n", c=2), AF.Exp, rd + [picol.b], [p.b],
                            bias=picol[:, 2:3])
                    else:
                        act(p[:, :, :], src.rearrange("p (c n) -> p c n", c=2), AF.Exp, rd, [p.b])
                    return p

                def issue_pv(n, p, hd=hd):
                    jp, blk = units[n]
                    ktp, vp = pieces[jp]
                    first = n == 0
                    last = n == nu - 1
                    for c in range(2):
                        mm(Ob[c][:, :], vp[:, blk, :], p[:, c, :], first, last, [vp.b, p.b], [Ob[c].b])
                    a_ = n % 2
                    ae = "pool" if (a_ == 1 and hd >= 2) else "dve"
                    if n < 2:
                        cp(ae, pacc[a_][:, :, :], p[:, :, :], [p.b], [pacc[a_].b])
                    else:
                        tt(ae, pacc[a_][:, :, :], pacc[a_][:, :, :], p[:, :, :], ADD,
                           [pacc[a_].b, p.b], [pacc[a_].b])

                LOOK = 1
                pq = [issue_s(n) for n in range(min(LOOK, nu))]
                for n in range(nu):
                    if n + LOOK < nu:
                        pq.append(issue_s(n + LOOK))
                    issue_pv(n, pq[n])
                a = []
                for c in range(2):
                    lb = misc_ring.next()
                    mm(lb[:, :], o128[:, :], pacc[0][:, c, :], True, False, [o128.b, pacc[0].b], [lb.b])
                    mm(lb[:, :], o128[:, :], pacc[1][:, c, :], False, True, [o128.b, pacc[1].b], [lb.b])
                    r = wk_ring.next()
                    S.add("dve", lambda e, r=r, lb=lb: e.reciprocal(r[:, :], lb[:, :]), reads=[lb.b], writes=[r.b])
                    stt("dve", r[:, :], Ob[c][:, :], 1.0 / 128.0, r[:, :], MUL, MUL, [Ob[c].b, r.b], [r.b])
                    a.append(r)
                dm = wk_ring.next()
                stt("dve", dm[:, :], a[1][:, :], neglam[:, 0:1], a[0][:, :], MUL, ADD,
                    [a[1].b, neglam.b, a[0].b], [dm.b])
                sq = wk_ring.next()
                act(sq[:, :], dm[:, :], AF.Square, [dm.b], [sq.b])
                b2 = misc_ring.next()
                mm(b2[:, :], o128[:, :], sq[:, :], True, True, [o128.b, sq.b], [b2.b])
                ts("dve", sq[:, :], b2[:, :], EPS, None, ADD, None, [b2.b], [sq.b])
                act(sq[:, :], sq[:, :], AF.Ln, [sq.b], [sq.b])
                act(sq[:, :], sq[:, :], AF.Exp, [sq.b], [sq.b], scale=-0.5)
                stt("dve", dm[:, :], dm[:, :], sub08[:, 0:1], sq[:, :], MUL, MUL, [dm.b, sub08.b, sq.b], [dm.b])
                bz = misc_ring.next()
                for kt in range(16):
                    mm(bz[:, :], wz[:, kt, :], h[:, kt, :], kt == 0, kt == 15, [wz.b, h.b], [bz.b])
                sz = wk_ring.next()
                act(sz[:, :], bz[:, :], AF.Silu, [bz.b], [sz.b])
                tt("dve", yatt[:, hd, :], dm[:, :], sz[:, :], MUL, [dm.b, sz.b], [yatt.b])

        def merge(J, ring):
            h = hT["own"]
            jobs = []
            st = {}

            def epi_gate(n, key):
                def f(g):
                    sg = wk_ring.next()
                    act(sg[:, :], g[:, :], AF.Sigmoid, [g.b], [sg.b])
                    st[(n, key)] = sg
                return f

            def epi_p(n, key):
                def f(pb):
                    sg = st[(n, key)]
                    tt("dve", sg[:, :], sg[:, :], pb[:, :], MUL, [sg.b, pb.b], [sg.b])
                    if key == "s":
                        sa = st[(n, "a")]
                        tt("dve", mT[:, n, :], sa[:, :], sg[:, :], ADD, [sa.b, sg.b], [mT.b])
                return f

            for n in range(16):
                jobs.append((win_s, 48 + n, h, 16, epi_gate(n, "a")))
                jobs.append((wpa_s, n, yatt, 8, epi_p(n, "a")))
                jobs.append((win_s, 64 + n, h, 16, epi_gate(n, "s")))
                jobs.append((wps_s, n, yssm, 8, epi_p(n, "s")))
            run_jobs(jobs, ring)

        def out_stage(J, ring):
            for grp in range(4):
                dma(wg[:, :, :], wout_s[grp], wg.sem, [wout_s.b], [wg.b])
                for tb in range(4):
                    r0_ = J * NT + tb * 128
                    xr = xres_ring.next()
                    dma(xr[:, :], x_own[r0_:r0_ + 128, grp * 512:(grp + 1) * 512], xr.sem, [], [xr.b])
                    bk = ring.next()
                    for kt in range(16):
                        mm(bk[:, :], mT[:, kt, tb * 128:(tb + 1) * 128], wg[:, kt, :], kt == 0, kt == 15,
                           [mT.b, wg.b], [bk.b])
                    ot = ot_ring.next()
                    tt("dve", ot[:, :], bk[:, :], xr[:, :], ADD, [bk.b, xr.b], [ot.b])
                    dma(out_own[r0_:r0_ + 128, grp * 512:(grp + 1) * 512], ot[:, :], ot.sem, [ot.b], [])

        for J in range(NJ):
            if stop >= 10:
                rms_to_hT("oth", J, gen_ring)
            if stop >= 11:
                stage_kvu("oth", J, gen_ring, hook=rms_gen("own", J, gen_ring))
            if stop >= 14:
                stage_kvu("own", J, gen_ring)
            if stop >= 15:
                ssm_smat(J, gen_ring)
                ssm_scan("oth", J)
                ssm_scan("own", J)
            if stop >= 18:
                attention(J)
            if stop >= 16:
                ssm_out(J, gen_ring)
            if stop >= 17:
                glu_stage(J, gen_ring)
            if stop >= 19:
                merge(J, gen_ring)
            if stop >= 20:
                out_stage(J, gen_ring)

        S.emit(nc, es)
    return nc, S


def _bf(a):
    return np.ascontiguousarray(a).astype(ml_dtypes.bfloat16)


def prep_shared(inp):
    f = np.float32
    sh = {}
    sh["w_in"] = np.ascontiguousarray(inp["w_in"][0], dtype=f)
    sh["w_glu"] = np.ascontiguousarray(inp["w_glu"][0], dtype=f)
    sh["w_pa"] = np.ascontiguousarray(inp["w_proj_att"][0], dtype=f)
    sh["w_ps"] = np.ascontiguousarray(inp["w_proj_ssm"][0], dtype=f)
    sh["w_out"] = np.ascontiguousarray(inp["w_out"][0], dtype=f)
    sh["gain_col"] = np.ascontiguousarray(inp["ln_gain"][0].reshape(16, 128).T, dtype=f)
    sh["qg"] = np.ascontiguousarray(np.tile(inp["q_norm_gain"][0], 2).reshape(128, 1), dtype=f)
    sh["kg"] = np.ascontiguousarray(np.tile(inp["k_norm_gain"][0], 2).reshape(128, 1), dtype=f)
    lv = np.stack([inp["lambda_q1"][0], inp["lambda_k1"][0], inp["lambda_q2"][0], inp["lambda_k2"][0]])
    sh["lamv"] = np.ascontiguousarray(np.broadcast_to(lv[None], (128, 4, 64)), dtype=f)
    sh["subln"] = np.ascontiguousarray(inp["subln_gain"][0].reshape(128, 1), dtype=f)
    sh["dcol"] = np.ascontiguousarray(inp["ssm_d"][0].reshape(8, 128).T, dtype=f)
    sh["bglu"] = np.ascontiguousarray(inp["b_glu"][0].reshape(8, 128).T, dtype=f)
    lre = np.asarray(inp["ssm_lambda_re"][0], dtype=f)
    lim = np.asarray(inp["ssm_lambda_im"][0], dtype=f)
    ldt = np.asarray(inp["ssm_log_dt"][0], dtype=f)
    bre = np.asarray(inp["ssm_b_re"][0], dtype=f)
    bim = np.asarray(inp["ssm_b_im"][0], dtype=f)
    cre = np.asarray(inp["ssm_c_re"][0], dtype=f)
    cim = np.asarray(inp["ssm_c_im"][0], dtype=f)

    def nat(a):
        return a.reshape(NPAIR, 2, 64).transpose(1, 2, 0).reshape(128, NPAIR)

    ldt_gp = np.broadcast_to(ldt[:, None], (64, 64))
    sh["pn"] = np.ascontiguousarray(np.stack([nat(lre), nat(lim), nat(ldt_gp)], axis=1), dtype=f)

    def natpad(a_gpc):
        o = np.zeros((2, 64, NPAIR, 2, 16), f)
        a = a_gpc.reshape(NPAIR, 2, 64, 16)
        for m in range(2):
            o[m, :, :, m, :] = a[:, m].transpose(1, 0, 2)
        return o.reshape(128, NPAIR, 32)

    sh["bn"] = np.ascontiguousarray(np.stack([natpad(bre), natpad(bim)], axis=1), dtype=f)
    sh["cn"] = np.ascontiguousarray(np.stack([natpad(cre.transpose(0, 2, 1)), natpad(cim.transpose(0, 2, 1))], axis=1), dtype=f)

    def trn(a_gp):
        a = a_gp.reshape(8, 4, 2, 64).reshape(8, 4, 128)
        return np.broadcast_to(a[:, :, None, :], (8, 4, 32, 128)).reshape(8, 128, 128)

    sh["pt"] = np.ascontiguousarray(np.stack([trn(lre), trn(lim), trn(ldt_gp)], axis=2), dtype=f)

    def trnpad(a_gpc):
        o = np.zeros((8, 4, 2, 16, 2, 64), f)
        a = a_gpc.reshape(8, 4, 2, 64, 16)
        for m in range(2):
            o[:, :, m, :, m, :] = a[:, :, m].transpose(0, 1, 3, 2)
        return o.reshape(8, 128, 128)

    sh["bt"] = np.ascontiguousarray(np.stack([trnpad(bre), trnpad(bim)], axis=2), dtype=f)
    sh["ident"] = _bf(np.eye(128, dtype=f))
    sh["onesb"] = _bf(np.ones((128, 128), f))
    obd = np.zeros((128, 128), f)
    obd[:64, :64] = 1.0 / 64
    obd[64:, 64:] = 1.0 / 64
    sh["onesbd"] = obd
    sh["ones128"] = np.full((128, 128), 1.0 / 128, f)
    sh["kvals"] = np.ascontiguousarray(np.broadcast_to(np.arange(TC + 1, dtype=f)[None], (128, TC + 1)))
    kp = np.arange(128)[:, None, None]
    r = np.arange(4)[None, :, None]
    qq = np.arange(NT)[None, None, :]
    sh["negmask"] = _bf(np.where(128 * r + kp <= qq, 0.0, NEG).astype(f))
    return sh


def prep_core(x, b, pi, NJ):
    xs = np.asarray(x[b]).reshape(16, NT, D)
    own = np.ascontiguousarray(xs[pi::2][:NJ].reshape(NJ * NT, D), dtype=np.float32)
    oth = np.ascontiguousarray(xs[1 - pi::2][:NJ].reshape(NJ * NT, D), dtype=np.float32)
    pc = np.zeros((128, 3), np.float32)
    pc[:, 0] = pi
    pc[:, 1] = 1 - pi
    pc[:, 2] = 0.0 if pi == 1 else NEG
    return {"x_own": own, "x_oth": oth, "picol": pc}


_CACHE = {}


def run(inputs, NJ=NJ_FULL, trace=False, stop=99):
    inp = {k: np.asarray(v) for k, v in inputs.items()}
    if (NJ, stop) not in _CACHE:
        _CACHE[(NJ, stop)] = build(NJ, stop)[0]
    nc = _CACHE[(NJ, stop)]
    sh = prep_shared(inp)
    in_maps = []
    for c in range(8):
        m = dict(sh)
        m.update(prep_core(inp["x"], c // 2, c % 2, NJ))
        in_maps.append(m)
    res = run_bass_kernel_spmd(nc, in_maps, core_ids=list(range(8)), trace=trace)
    out = np.zeros((4, 16, NT, D), np.float32)
    for c in range(8):
        b, pi = c // 2, c % 2
        o = np.asarray(res.results[c]["out_own"]).reshape(NJ, NT, D)
        out[b, pi::2][:NJ] = o
    return out.reshape(4, 16 * NT, D), res


def kernel(**inputs):
    out, _ = run(inputs)
    return out
```
